# Optimizing a Trainium2 kernel written in Bass

```python
import math
import jax, jax.numpy as jnp
from jax import lax
import numpy as np

D_MODEL = 1024
BATCH = 4
SEQ = 8192
DEPTH = 2

N_EVEN = (DEPTH + 1) // 2
N_ODD = DEPTH // 2
LRU_WIDTH = D_MODEL
LRU_BLOCKS = 8
LRU_BLOCK = LRU_WIDTH // LRU_BLOCKS
LRU_C = 8.0
CONV_WIDTH = 4
SB_HEADS = 8
SB_HEAD_DIM = 128
SB_WIDTH = SB_HEADS * SB_HEAD_DIM
C_HEADS = 16
C_KV_HEADS = 2
C_GROUP = C_HEADS // C_KV_HEADS
C_HEAD_DIM = 64
C_WIDTH = C_HEADS * C_HEAD_DIM
C_KV_WIDTH = C_KV_HEADS * C_HEAD_DIM
WINDOW = 128
Q_BLOCK = 128
EVEN_IN = 2 * LRU_WIDTH + 4 * SB_WIDTH
EVEN_MIX = LRU_WIDTH + SB_WIDTH
ODD_IN = 2 * C_WIDTH + 2 * C_KV_WIDTH
DEEPNORM_ALPHA = float((2 * DEPTH) ** 0.25)
DEEPNORM_BETA = float((8 * DEPTH) ** -0.25)
LN_EPS = 1e-5

kernel_name = "hybrid_rglru_stickbreak_swa_deepnorm"


def _layer_norm(x, g, b):
    xf = x.astype(jnp.float32)
    mu = jnp.mean(xf, axis=-1, keepdims=True)
    var = jnp.mean(jnp.square(xf - mu), axis=-1, keepdims=True)
    y = (xf - mu) * lax.rsqrt(var + LN_EPS) * g.astype(jnp.float32) + b.astype(jnp.float32)
    return y.astype(x.dtype)


def _causal_dwconv(x, w, b):
    s = x.shape[1]
    xp = jnp.pad(x, ((0, 0), (CONV_WIDTH - 1, 0), (0, 0)))
    y = b
    for k in range(CONV_WIDTH):
        y = y + w[k] * xp[:, k:k + s]
    return y


def _rg_lru(x, w_a, b_a, w_x, b_x, lam):
    bsz, s, _ = x.shape
    xb = x.reshape(bsz, s, LRU_BLOCKS, LRU_BLOCK)
    r = jax.nn.sigmoid(jnp.einsum('bsnd,nde->bsne', xb, w_a).reshape(bsz, s, LRU_WIDTH) + b_a)
    i = jax.nn.sigmoid(jnp.einsum('bsnd,nde->bsne', xb, w_x).reshape(bsz, s, LRU_WIDTH) + b_x)
    log_a = LRU_C * r.astype(jnp.float32) * jax.nn.log_sigmoid(lam.astype(jnp.float32))
    a = jnp.exp(log_a)
    u = jnp.sqrt(-jnp.expm1(2.0 * log_a)) * (i * x).astype(jnp.float32)

    def combine(lhs, rhs):
        a1, b1 = lhs
        a2, b2 = rhs
        return a1 * a2, a2 * b1 + b2

    _, h = lax.associative_scan(combine, (a, u), axis=1)
    return h.astype(x.dtype)


def _stick_breaking(q, k, v):
    bsz, s, h, dh = q.shape
    nb = s // Q_BLOCK
    scale = 1.0 / math.sqrt(dh)
    qb = q.reshape(bsz, nb, Q_BLOCK, h, dh).transpose(1, 0, 2, 3, 4)
    kpos = jnp.arange(s)

    def one_block(args):
        n, qblk = args
        z = jnp.einsum('bqhd,bkhd->bhqk', qblk, k).astype(jnp.float32) * scale
        qpos = n * Q_BLOCK + jnp.arange(Q_BLOCK)
        causal = kpos[None, :] < qpos[:, None]
        log_1mb = jnp.where(causal, jax.nn.log_sigmoid(-z), 0.0)
        suffix = lax.cumsum(log_1mb, axis=3, reverse=True) - log_1mb
        w = jnp.where(causal, jnp.exp(jax.nn.log_sigmoid(z) + suffix), 0.0)
        return jnp.einsum('bhqk,bkhd->bqhd', w.astype(v.dtype), v)

    o = lax.map(one_block, (jnp.arange(nb), qb))
    return o.transpose(1, 0, 2, 3, 4).reshape(bsz, s, h, dh)


def _alibi_slopes(n_heads):
    return np.array([2.0 ** (-8.0 * (i + 1) / n_heads) for i in range(n_heads)], dtype=np.float32)


def _swa_sinks_alibi(q, k, v, sinks):
    bsz, s, _, dh = q.shape
    nb = s // Q_BLOCK
    scale = 1.0 / math.sqrt(dh)
    qb = q.reshape(bsz, nb, Q_BLOCK, C_KV_HEADS, C_GROUP, dh)
    kp = jnp.pad(k, ((0, 0), (Q_BLOCK, 0), (0, 0), (0, 0))).reshape(bsz, nb + 1, Q_BLOCK, C_KV_HEADS, dh)
    vp = jnp.pad(v, ((0, 0), (Q_BLOCK, 0), (0, 0), (0, 0))).reshape(bsz, nb + 1, Q_BLOCK, C_KV_HEADS, dh)
    kc = jnp.concatenate([kp[:, :-1], kp[:, 1:]], axis=2)
    vc = jnp.concatenate([vp[:, :-1], vp[:, 1:]], axis=2)
    sc = jnp.einsum('bnqcgd,bnkcd->bncgqk', qb, kc).astype(jnp.float32) * scale
    i = jnp.arange(Q_BLOCK)[:, None]
    j = jnp.arange(2 * Q_BLOCK)[None, :]
    dist = (i - j + Q_BLOCK).astype(jnp.float32)
    kpos = jnp.arange(nb)[:, None, None] * Q_BLOCK - Q_BLOCK + j[None]
    valid = (dist >= 0) & (dist < WINDOW) & (kpos >= 0)
    slopes = jnp.asarray(_alibi_slopes(C_HEADS)).reshape(C_KV_HEADS, C_GROUP)
    sc = sc - slopes[:, :, None, None] * dist
    sc = jnp.where(valid[None, :, None, None], sc, -jnp.inf)
    sink = sinks.astype(jnp.float32).reshape(C_KV_HEADS, C_GROUP)[:, :, None, None]
    m = jnp.maximum(jnp.max(sc, axis=-1, keepdims=True), sink)
    p = jnp.exp(sc - m)
    denom = jnp.sum(p, axis=-1, keepdims=True) + jnp.exp(sink - m)
    p = (p / denom).astype(v.dtype)
    o = jnp.einsum('bncgqk,bnkcd->bnqcgd', p, vc)
    return o.reshape(bsz, s, C_HEADS, dh)


def _even_layer(x, w_in, conv_w, conv_b, w_gate_a, b_gate_a, w_gate_x, b_gate_x, lru_lambda, w_out):
    bsz, s, _ = x.shape
    hproj = jnp.einsum('bsd,de->bse', x, w_in)
    splits = np.cumsum([LRU_WIDTH, LRU_WIDTH, SB_WIDTH, SB_WIDTH, SB_WIDTH])
    a_x, a_g, b_q, b_k, b_v, b_g = jnp.split(hproj, splits, axis=-1)
    a_h = _rg_lru(_causal_dwconv(a_x, conv_w, conv_b), w_gate_a, b_gate_a, w_gate_x, b_gate_x, lru_lambda)
    y_a = a_h * jax.nn.silu(a_g)
    shp = (bsz, s, SB_HEADS, SB_HEAD_DIM)
    o_b = _stick_breaking(b_q.reshape(shp), b_k.reshape(shp), b_v.reshape(shp)).reshape(bsz, s, SB_WIDTH)
    y_b = o_b * jax.nn.silu(b_g)
    y = jnp.concatenate([y_a, y_b], axis=-1)
    return jnp.einsum('bse,ed->bsd', y, w_out)


def _odd_layer(x, w_in, sinks, w_out):
    bsz, s, _ = x.shape
    hproj = jnp.einsum('bsd,de->bse', x, w_in)
    splits = np.cumsum([C_WIDTH, C_KV_WIDTH, C_KV_WIDTH])
    q, k, v, g = jnp.split(hproj, splits, axis=-1)
    o = _swa_sinks_alibi(q.reshape(bsz, s, C_HEADS, C_HEAD_DIM),
                         k.reshape(bsz, s, C_KV_HEADS, C_HEAD_DIM),
                         v.reshape(bsz, s, C_KV_HEADS, C_HEAD_DIM), sinks).reshape(bsz, s, C_WIDTH)
    y = o * jax.nn.silu(g)
    return jnp.einsum('bse,ed->bsd', y, w_out)


def setup_inputs(seed: int = 0) -> dict:
    key = jax.random.key(seed)
    ks = jax.random.split(key, 20)
    f32 = jnp.float32
    nrm = lambda k, shp, sc: jax.random.normal(k, shp, f32) * sc
    x = jax.random.normal(ks[0], (BATCH, SEQ, D_MODEL), f32)
    e_w_in = nrm(ks[1], (N_EVEN, D_MODEL, EVEN_IN), D_MODEL ** -0.5)
    e_conv_w = nrm(ks[2], (N_EVEN, CONV_WIDTH, LRU_WIDTH), CONV_WIDTH ** -0.5)
    e_conv_b = nrm(ks[3], (N_EVEN, LRU_WIDTH), 0.02)
    e_w_gate_a = nrm(ks[4], (N_EVEN, LRU_BLOCKS, LRU_BLOCK, LRU_BLOCK), LRU_BLOCK ** -0.5)
    e_b_gate_a = nrm(ks[5], (N_EVEN, LRU_WIDTH), 0.02)
    e_w_gate_x = nrm(ks[6], (N_EVEN, LRU_BLOCKS, LRU_BLOCK, LRU_BLOCK), LRU_BLOCK ** -0.5)
    e_b_gate_x = nrm(ks[7], (N_EVEN, LRU_WIDTH), 0.02)
    u = jax.random.uniform(ks[8], (N_EVEN, LRU_WIDTH), f32, 0.9, 0.999)
    a0 = u ** (1.0 / LRU_C)
    e_lru_lambda = jnp.log(a0) - jnp.log1p(-a0)
    e_w_out = nrm(ks[9], (N_EVEN, EVEN_MIX, D_MODEL), EVEN_MIX ** -0.5 * DEEPNORM_BETA)
    e_ln_g = 1.0 + nrm(ks[10], (N_EVEN, D_MODEL), 0.02)
    e_ln_b = nrm(ks[11], (N_EVEN, D_MODEL), 0.02)
    o_w_in = nrm(ks[12], (N_ODD, D_MODEL, ODD_IN), D_MODEL ** -0.5)
    o_sinks = nrm(ks[13], (N_ODD, C_HEADS), 1.0)
    o_w_out = nrm(ks[14], (N_ODD, C_WIDTH, D_MODEL), C_WIDTH ** -0.5 * DEEPNORM_BETA)
    o_ln_g = 1.0 + nrm(ks[15], (N_ODD, D_MODEL), 0.02)
    o_ln_b = nrm(ks[16], (N_ODD, D_MODEL), 0.02)
    return {"x": x, "e_w_in": e_w_in, "e_conv_w": e_conv_w, "e_conv_b": e_conv_b,
            "e_w_gate_a": e_w_gate_a, "e_b_gate_a": e_b_gate_a, "e_w_gate_x": e_w_gate_x,
            "e_b_gate_x": e_b_gate_x, "e_lru_lambda": e_lru_lambda, "e_w_out": e_w_out,
            "e_ln_g": e_ln_g, "e_ln_b": e_ln_b, "o_w_in": o_w_in, "o_sinks": o_sinks,
            "o_w_out": o_w_out, "o_ln_g": o_ln_g, "o_ln_b": o_ln_b}


def reference(x, e_w_in, e_conv_w, e_conv_b, e_w_gate_a, e_b_gate_a, e_w_gate_x, e_b_gate_x,
              e_lru_lambda, e_w_out, e_ln_g, e_ln_b, o_w_in, o_sinks, o_w_out, o_ln_g, o_ln_b):
    for layer in range(DEPTH):
        li = layer // 2
        if layer % 2 == 0:
            y = _even_layer(x, e_w_in[li], e_conv_w[li], e_conv_b[li], e_w_gate_a[li], e_b_gate_a[li],
                            e_w_gate_x[li], e_b_gate_x[li], e_lru_lambda[li], e_w_out[li])
            x = _layer_norm(DEEPNORM_ALPHA * x + y, e_ln_g[li], e_ln_b[li])
        else:
            y = _odd_layer(x, o_w_in[li], o_sinks[li], o_w_out[li])
            x = _layer_norm(DEEPNORM_ALPHA * x + y, o_ln_g[li], o_ln_b[li])
    return x
```

```python
import numpy as np
from contextlib import ExitStack
import concourse.bass as bass
import concourse.mybir as mybir

F32 = mybir.dt.float32
BF16 = mybir.dt.bfloat16
AF = mybir.ActivationFunctionType
ALU = mybir.AluOpType
AX = mybir.AxisListType

ENGS = ("pe", "act", "dve", "pool", "sp")
SAME_ENGINE_SYNC = True


class Op:
    __slots__ = ("eng", "fn", "deps", "needs_inc", "inc_val", "dma_sem", "dma_val", "idx", "is_dma")

    def __init__(self, eng, fn, is_dma=False, dma_sem=None):
        self.eng = eng
        self.fn = fn
        self.deps = []
        self.needs_inc = False
        self.inc_val = None
        self.is_dma = is_dma
        self.dma_sem = dma_sem
        self.dma_val = None


class Prog:
    def __init__(self, nc):
        self.nc = nc
        self.ops = {e: [] for e in ENGS}
        self.res = {}
        self.stack = ExitStack()
        self.esem = {}
        self.dma_chan = {}
        self.nsem = 0
        self.chan_last = {}

    def new_sem(self, name):
        self.nsem += 1
        return self.stack.enter_context(self.nc.semaphore(name))

    def chan(self, key):
        if key not in self.dma_chan:
            self.dma_chan[key] = [self.new_sem("dc%d" % len(self.dma_chan)), 0]
        return self.dma_chan[key]

    def _rec(self, o, reads, writes):
        deps = []
        for r in reads:
            st = self.res.setdefault(r, [None, []])
            if st[0] is not None:
                deps.append(st[0])
        for w in writes:
            st = self.res.setdefault(w, [None, []])
            if st[0] is not None:
                deps.append(st[0])
            deps.extend(st[1])
        for r in reads:
            self.res[r][1].append(o)
        for w in writes:
            st = self.res[w]
            st[0] = o
            st[1] = []
        seen = set()
        for d in deps:
            if d is o or id(d) in seen:
                continue
            seen.add(id(d))
            if (not d.is_dma) and d.eng == o.eng and (d.eng == "pe" or not SAME_ENGINE_SYNC):
                continue
            o.deps.append(d)
            if not d.is_dma:
                d.needs_inc = True
        self.ops[o.eng].append(o)
        return o

    def op(self, eng, fn, reads=(), writes=()):
        return self._rec(Op(eng, fn), reads, writes)

    def dma(self, eng, out, in_, reads=(), writes=(), chan=None, **kw):
        c = self.chan(chan)
        c[1] += 16
        o = Op(eng, lambda e, out=out, in_=in_, kw=kw: e.dma_start(out=out, in_=in_, **kw), is_dma=True, dma_sem=c[0])
        o.dma_val = c[1]
        self.chan_last[chan] = o
        return self._rec(o, reads, writes)

    def barrier(self):
        lasts = []
        for e in ENGS:
            for o in reversed(self.ops[e]):
                if not o.is_dma:
                    o.needs_inc = True
                    lasts.append(o)
                    break
        dmas = list(self.chan_last.values())
        for e in ENGS:
            b = Op(e, lambda eng: None)
            b.deps = [o for o in lasts if o.eng != e] + dmas
            self.ops[e].append(b)

    def emit(self):
        nc = self.nc
        for e in ENGS:
            self.esem[e] = self.new_sem("es_" + e)
            cnt = 0
            for o in self.ops[e]:
                if o.needs_inc and not o.is_dma:
                    cnt += 1
                    o.inc_val = cnt
        esem = self.esem

        def run(e, engobj):
            waited = {}
            for o in self.ops[e]:
                for d in o.deps:
                    if d.is_dma:
                        sem, val = d.dma_sem, d.dma_val
                    else:
                        sem, val = esem[d.eng], d.inc_val
                    k = id(sem)
                    if waited.get(k, 0) < val:
                        engobj.wait_ge(sem, val)
                        waited[k] = val
                inst = o.fn(engobj)
                if o.is_dma:
                    inst.then_inc(o.dma_sem, 16)
                elif o.needs_inc:
                    inst.then_inc(esem[e], 1)

        with nc.Block() as block:
            @block.tensor
            def _(eng):
                run("pe", eng)

            @block.scalar
            def _(eng):
                run("act", eng)

            @block.vector
            def _(eng):
                run("dve", eng)

            @block.gpsimd
            def _(eng):
                run("pool", eng)

            @block.sync
            def _(eng):
                run("sp", eng)
        self.stack.close()

POOLENG = 'dve'
LWIN = 3
POOL3 = False


def interleave(gens, window=2):
    gens = list(gens)
    active = []
    nxt = 0
    DONE = object()
    while active or nxt < len(gens):
        while len(active) < window and nxt < len(gens):
            active.append(gens[nxt])
            nxt += 1
        for g in list(active):
            if next(g, DONE) is DONE:
                assert active[0] is g, "generators must finish in admission order (slot reuse safety)"
                active.remove(g)


def record_A(nc, P, stk, cfg, xT, w_sb, w_lru, conv_w, vecs, wga, wgx, cmask, cmats, kmask_d, ctxf_d, ydst):
    NTW, OWN, HALO = cfg["NTW"], cfg["own_from"], cfg["halo"]
    NH, NG = cfg["n_heads"], cfg["n_lru_groups"]
    ycol = cfg["ycol"]
    yrow_a, yrow_b = cfg["yrow_a"], cfg["yrow_b"]
    S = NTW * 512
    NB = S // 128

    def sb(name, shape, dt=F32):
        return stk.enter_context(nc.sbuf_tensor(name, shape, dt)).ap()
    banks = [stk.enter_context(nc.psum_tensor("bank%d" % i, [128, 512], F32)).ap() for i in range(8)]
    B = lambda i: ("bank", i)

    cm = sb("cm", [128, 2048], BF16)
    cmt = sb("cmt", [128, 384], BF16)
    P.dma("pool", cm, cmask, writes=["cm"], chan="cm")
    P.dma("pool", cmt, cmats, writes=["cmt"], chan="cmt")
    ident = cmt[:, 0:128]
    trineg = cmt[:, 128:256]
    restneg = cmt[:, 256:384]
    kmask = sb("kmask_s", [128, NTW])
    P.dma("sp", kmask, kmask_d, writes=["kmask"], chan="kmask")
    ctxf = sb("ctxf_s", [128, 1])
    P.dma("sp", ctxf, ctxf_d, writes=["ctxf"], chan="ctxf")

    out_dmas = []
    lru_hook = [None]
    xT_v = xT.rearrange("(c p) t -> p c t", p=128)
    NXS = 4
    xts = [sb("xt%d" % i, [128, 8, 512], BF16) for i in range(NXS)]
    xcnt = [0]

    def load_x(i):
        s = xcnt[0] % NXS
        xcnt[0] += 1
        P.dma("pool", xts[s], xT_v[:, :, i * 512:(i + 1) * 512], writes=[("xt", s)], chan=("xt", s))
        return s

    Wt = [sb("Wt%d" % i, [128, 8, 512], BF16) for i in range(2)]
    kT = sb("kT0", [128, S], BF16)
    V = sb("V0", [128, NB, 128], BF16)
    qTb = [sb("qT%d" % i, [128, 512], BF16) for i in range(3)]
    sgb = [sb("sg%d" % i, [128, 512], F32) for i in range(3)]
    NE, NL, NE2, NW = 4, 4, 3, 3
    eb = [sb("e%d" % i, [128, 512], F32) for i in range(NE)]
    Lb = [sb("L%d" % i, [128, 512], BF16) for i in range(NL)]
    E2b = [sb("E2%d" % i, [128, 512], F32) for i in range(NE2)]
    wb = [sb("w%d" % i, [128, 512], BF16) for i in range(NW)]
    yb = [sb("yb%d" % i, [128, 512], BF16) for i in range(3)]
    vTs = sb("vTs", [128, 512], BF16)
    SCALE = 1.0 / np.sqrt(128.0)
    ZB = [0, 1]
    CB = 2
    OB = [3, 3]
    IB = [5, 6]
    ibc = [0]

    def next_ib():
        b = IB[ibc[0] % len(IB)]
        ibc[0] += 1
        return b

    def sb_pass(hl):
        ws = hl % 2
        W = Wt[ws]
        P.dma("pool", W, w_sb[hl].rearrange("(c p) n -> p c n", p=128), writes=[("W", ws)], chan=("W", ws))
        xslot = {}
        first_q = OWN - 1 if HALO else OWN
        cons = ([OWN - 1] + list(range(OWN - 2, -1, -1)) if HALO else []) + list(range(OWN, NTW))
        cpos = {t: k for k, t in enumerate(cons)}
        lptr = [0]

        def ensure_loaded(t):
            while lptr[0] <= min(cpos[t] + 1, len(cons) - 1):
                tt = cons[lptr[0]]
                xslot[tt] = load_x(tt)
                lptr[0] += 1

        def inproj_groups(i):
            xs = xslot[i]
            x = xts[xs]
            qs = i % 3

            def g_q():
                b = next_ib()
                for c in range(8):
                    P.op("pe", lambda e, c=c, b=b: e.matmul(banks[b], lhsT=W[:, c, 0:128], rhs=x[:, c, :], start=(c == 0), stop=(c == 7)),
                         reads=[("xt", xs), ("W", ws)], writes=[B(b)])
                P.op("dve", lambda e, b=b: e.tensor_scalar(out=qTb[qs], in0=banks[b], scalar1=float(SCALE), scalar2=None, op0=ALU.mult),
                     reads=[B(b)], writes=[("qT", qs)])

            def g_k():
                b = next_ib()
                for c in range(8):
                    P.op("pe", lambda e, c=c, b=b: e.matmul(banks[b], lhsT=W[:, c, 128:256], rhs=x[:, c, :], start=(c == 0), stop=(c == 7)),
                         reads=[("xt", xs), ("W", ws)], writes=[B(b)])
                P.op("dve", lambda e, b=b: e.tensor_copy(out=kT[:, i * 512:(i + 1) * 512], in_=banks[b]),
                     reads=[B(b)], writes=[("kT", i)])

            def g_g():
                b = next_ib()
                S_ = sgb[qs]
                for c in range(8):
                    P.op("pe", lambda e, c=c, b=b: e.matmul(banks[b], lhsT=W[:, c, 384:512], rhs=x[:, c, :], start=(c == 0), stop=(c == 7)),
                         reads=[("xt", xs), ("W", ws)], writes=[B(b)])
                P.op("act", lambda e, b=b: e.activation(out=S_, in_=banks[b], func=AF.Exp, scale=-1.0), reads=[B(b)], writes=[("sg", qs)])
                P.op("dve", lambda e: e.tensor_scalar_add(out=S_, in0=S_, scalar1=1.0), reads=[("sg", qs)], writes=[("sg", qs)])
                P.op("dve", lambda e: e.reciprocal(out=S_, in_=S_), reads=[("sg", qs)], writes=[("sg", qs)])
                P.op("dve", lambda e, b=b: e.tensor_tensor(out=S_, in0=S_, in1=banks[b], op=ALU.mult), reads=[("sg", qs), B(b)], writes=[("sg", qs)])

            def g_v():
                b = next_ib()
                for c in range(8):
                    P.op("pe", lambda e, c=c, b=b: e.matmul(banks[b], lhsT=W[:, c, 256:384], rhs=x[:, c, :], start=(c == 0), stop=(c == 7)),
                         reads=[("xt", xs), ("W", ws)], writes=[B(b)])
                P.op("dve", lambda e, b=b: e.tensor_copy(out=vTs, in_=banks[b]), reads=[B(b)], writes=["vTs"])

            def g_v2():
                b2 = next_ib()
                tb2 = banks[b2].bitcast(BF16)
                for j in range(4):
                    P.op("pe", lambda e, j=j, tb2=tb2: e.transpose(tb2[:, j * 128:(j + 1) * 128], vTs[:, j * 128:(j + 1) * 128], ident), reads=["vTs", "cmt"], writes=[B(b2)])
                P.op("dve", lambda e, tb2=tb2: e.tensor_copy(out=V[:, 4 * i:4 * i + 4, :], in_=tb2[:, 0:512].rearrange("p (j d) -> p j d", j=4)),
                     reads=[B(b2)], writes=[("V", i)])
            if i >= first_q:
                return [g_v, g_q, g_v2, g_k, g_g]
            return [g_v, g_k, g_v2]

        ensure_loaded(cons[0])
        for g in inproj_groups(cons[0]):
            g()
        pend = list(cons[1:OWN + 1]) if HALO else []
        queued = set(pend) | {cons[0]}

        tasks = []
        if HALO:
            ih = OWN - 1
            tasks += [(ih, kb, 384, 128) for kb in reversed(range(4 * ih + 4))]
        for i in range(OWN, NTW):
            tasks += [(i, kb, 0, 512) for kb in reversed(range(4 * i + 4))]
        NTK = len(tasks)
        side = []

        def first_of_tile(n):
            i, kb, c0, qw = tasks[n]
            return kb == 4 * i + 3

        def last_of_tile(n):
            return tasks[n][1] == 0

        for s in range(NTK + 5):
            if s < NTK:
                i, kb, c0, qw = tasks[s]
                if first_of_tile(s):
                    ni = i + 1
                    if ni < NTW and ni > first_q and ni not in queued:
                        queued.add(ni)
                        pend.append(ni)
                zb = ZB[s % 2]
                diag = kb >= 4 * i
                qs = i % 3
                P.op("pe", lambda e, zb=zb, kb=kb, qs=qs, diag=diag, c0=c0, qw=qw: e.matmul(banks[zb][:, 0:qw], lhsT=kT[:, kb * 128:(kb + 1) * 128], rhs=qTb[qs][:, c0:c0 + qw], start=True, stop=not diag),
                     reads=[("kT", kb // 4), ("qT", qs)], writes=[B(zb)])
                if diag:
                    r = kb - 4 * i
                    P.op("pe", lambda e, zb=zb, r=r, c0=c0, qw=qw: e.matmul(banks[zb][:, 0:qw], lhsT=ident, rhs=cm[:, r * 512 + c0:r * 512 + c0 + qw], start=False, stop=True),
                         reads=["cm", "cmt"], writes=[B(zb)])
            n = s - 3
            if 0 <= n < NTK:
                qw = tasks[n][3]
                P.op("act", lambda e, n=n, qw=qw: e.activation(out=E2b[n % NE2][:, 0:qw], in_=banks[CB][:, 0:qw], func=AF.Exp),
                     reads=[B(CB)], writes=[("E2", n % NE2)])
            n = s - 1
            if 0 <= n < NTK:
                zb = ZB[n % 2]
                kb, qw = tasks[n][1], tasks[n][3]
                P.op("act", lambda e, n=n, zb=zb, kb=kb, qw=qw: e.activation(out=eb[n % NE][:, 0:qw], in_=banks[zb][:, 0:qw], func=AF.Exp, bias=kmask[:, kb // 4:kb // 4 + 1]),
                     reads=[B(zb), "kmask"], writes=[("e", n % NE)])
                P.op("act", lambda e, n=n, qw=qw: e.activation(out=Lb[n % NL][:, 0:qw], in_=eb[n % NE][:, 0:qw], func=AF.Ln, bias=1.0),
                     reads=[("e", n % NE)], writes=[("L", n % NL)])
            n = s - 3
            if 0 <= n < NTK and not last_of_tile(n):
                qw = tasks[n][3]
                P.op("pe", lambda e, n=n, qw=qw: e.matmul(banks[CB][:, 0:qw], lhsT=restneg, rhs=Lb[n % NL][:, 0:qw], start=False, stop=True, skip_group_check=True),
                     reads=[("L", n % NL), "cmt"], writes=[B(CB)])
            n = s - 2
            if 0 <= n < NTK:
                qw = tasks[n][3]
                P.op("pe", lambda e, n=n, st=first_of_tile(n), qw=qw: e.matmul(banks[CB][:, 0:qw], lhsT=trineg, rhs=Lb[n % NL][:, 0:qw], start=st, stop=True, skip_group_check=True),
                     reads=[("L", n % NL), "cmt"], writes=[B(CB)])
            n = s - 3
            if 0 <= n < NTK:
                qw = tasks[n][3]
                P.op("dve", lambda e, n=n, qw=qw: e.tensor_tensor(out=wb[n % NW][:, 0:qw], in0=eb[n % NE][:, 0:qw], in1=E2b[n % NE2][:, 0:qw], op=ALU.mult),
                     reads=[("e", n % NE), ("E2", n % NE2)], writes=[("w", n % NW)])
            n = s - 4
            if 0 <= n < NTK:
                i, kb, c0, qw = tasks[n]
                ob = OB[i % 2]
                P.op("pe", lambda e, n=n, kb=kb, ob=ob, st=first_of_tile(n), sp=last_of_tile(n), qw=qw: e.matmul(banks[ob][:, 0:qw], lhsT=V[:, kb, :], rhs=wb[n % NW][:, 0:qw], start=st, stop=sp),
                     reads=[("V", kb // 4), ("w", n % NW)], writes=[B(ob)])
                if last_of_tile(n):
                    qs = i % 3
                    P.op("dve", lambda e, ob=ob, qs=qs, c0=c0, qw=qw: e.tensor_tensor(out=yb[qs][:, 0:qw], in0=banks[ob][:, 0:qw], in1=sgb[qs][:, c0:c0 + qw], op=ALU.mult),
                         reads=[B(ob), ("sg", qs)], writes=[("yb", qs)])
                    row0 = yrow_b + hl * 128
                    col0 = 0 if qw == 128 else ycol(i)
                    out_dmas.append(P.dma("sp", ydst[row0:row0 + 128, col0:col0 + qw], yb[qs][:, 0:qw], reads=[("yb", qs)], chan=("yb", qs)))
            if not side and pend:
                ti = pend.pop(0)
                ensure_loaded(ti)
                side.extend(inproj_groups(ti))
            if side:
                side.pop(0)()
            lru_hook[0](2 if s % 6 == 0 else 1)
        while side or pend:
            if not side:
                ti = pend.pop(0)
                ensure_loaded(ti)
                side.extend(inproj_groups(ti))
            side.pop(0)()

    Wl = sb("Wl", [128, 8, 1024], BF16)
    Wa = sb("Wa", [128, 4, 128], BF16)
    Wx = sb("Wx", [128, 4, 128], BF16)
    cw = sb("cw", [128, 4, 4])
    vc = sb("vc", [128, 4, 4])
    nb = sb("nb", [128, 4, 2])
    cch = sb("cch", [128, 4, 3])
    axbuf = [sb("axbuf%d" % c, [128, 515]) for c in range(4)]
    hst = sb("hst", [128, 4])
    T = {}
    for nm, dt in [("xc", F32), ("xcb", BF16), ("er", F32), ("ei", F32), ("a", F32), ("a2", F32), ("u", F32), ("h", F32), ("eg", F32), ("yl", BF16)]:
        T[nm] = [sb("l_%s%d" % (nm, k), [128, 512], dt) for k in range(2)]
    xtl = [sb("xtl%d" % i, [128, 8, 512], BF16) for i in range(2)]
    xlcnt = [0]

    def load_xl(i):
        s = xlcnt[0] % 2
        xlcnt[0] += 1
        P.dma("pool", xtl[s], xT_v[:, :, i * 512:(i + 1) * 512], writes=[("xtl", s)], chan=("xtl", s))
        return s


    def lru_pass(grp):
        yield
        P.dma("pool", Wl, w_lru[grp].rearrange("(c p) n -> p c n", p=128), writes=["Wl"], chan="Wl")
        P.dma("pool", Wa, wga[grp].rearrange("n d e -> d n e"), writes=["Wa"], chan="Wa")
        P.dma("pool", Wx, wgx[grp].rearrange("n d e -> d n e"), writes=["Wx"], chan="Wx")
        P.dma("sp", cw, conv_w[grp].rearrange("(c p) k -> p c k", p=128), writes=["cw"], chan="cw")
        P.dma("sp", vc, vecs[grp].rearrange("(c p) k -> p c k", p=128), writes=["vc"], chan="vc")
        P.op("dve", lambda e: e.tensor_scalar(out=nb, in0=vc[:, :, 1:3], scalar1=-1.0, scalar2=None, op0=ALU.mult), reads=["vc"], writes=["nb"])
        P.op("act", lambda e: e.activation(out=cch[:, :, 2:3], in_=vc[:, :, 3:4], func=AF.Exp, scale=-1.0), reads=["vc"], writes=["cch"])
        P.op("act", lambda e: e.activation(out=cch[:, :, 2:3], in_=cch[:, :, 2:3], func=AF.Ln, bias=1.0), reads=["cch"], writes=["cch"])
        P.op("dve", lambda e: e.tensor_scalar(out=cch[:, :, 0:1], in0=cch[:, :, 2:3], scalar1=-8.0, scalar2=None, op0=ALU.mult), reads=["cch"], writes=["cch"])
        P.op("dve", lambda e: e.tensor_scalar(out=cch[:, :, 1:2], in0=cch[:, :, 2:3], scalar1=-16.0, scalar2=None, op0=ALU.mult), reads=["cch"], writes=["cch"])
        for c in range(4):
            P.op("pool", lambda e, c=c: e.memset(axbuf[c], 0.0), writes=[("axbuf", c)])
        P.op("pool", lambda e: e.memset(hst, 0.0), writes=[("hst", c) for c in range(4)])
        yield
        xslot = {0: load_xl(0)}
        first_y = OWN - 1 if HALO else OWN

        def chunk_gen(i, c, k):
            if c == 0 and i + 1 < NTW:
                xslot[i + 1] = load_xl(i + 1)
            xs = xslot[i]
            x = xtl[xs]
            need_y = i >= first_y
            is_ctx = i < OWN
            bax, bag, br, bi = 4, 7, 4, 7
            R = lambda nm: (nm, k)
            ab = axbuf[c]
            xc, xcb, er, ei, a, a2, u, h, eg, yl = [T[nm][k] for nm in ("xc", "xcb", "er", "ei", "a", "a2", "u", "h", "eg", "yl")]
            for kc in range(8):
                P.op("pe", lambda e, kc=kc: e.matmul(banks[bax], lhsT=Wl[:, kc, c * 128:(c + 1) * 128], rhs=x[:, kc, :], start=(kc == 0), stop=(kc == 7)),
                     reads=[("xtl", xs), "Wl"], writes=[B(bax)])
            yield
            P.op("dve", lambda e: e.tensor_copy(out=ab[:, 3:515], in_=banks[bax]), reads=[B(bax)], writes=[("axbuf", c)])
            yield
            if POOL3:
                P.op("pool", lambda e: e.tensor_scalar(out=xc, in0=ab[:, 3:515], scalar1=cw[:, c, 3:4], scalar2=vc[:, c, 0:1], op0=ALU.mult, op1=ALU.add),
                     reads=[("axbuf", c), "cw", "vc"], writes=[R("xc")])
            else:
                P.op("dve", lambda e: e.tensor_scalar(out=xc, in0=ab[:, 3:515], scalar1=cw[:, c, 3:4], scalar2=vc[:, c, 0:1], op0=ALU.mult, op1=ALU.add),
                     reads=[("axbuf", c), "cw", "vc"], writes=[R("xc")])
            yield
            for tap in (2, 1, 0):
                P.op("dve", lambda e, tap=tap: e.scalar_tensor_tensor(out=xc, in0=ab[:, tap:tap + 512], scalar=cw[:, c, tap:tap + 1], in1=xc, op0=ALU.mult, op1=ALU.add),
                     reads=[("axbuf", c), "cw", R("xc")], writes=[R("xc")])
                yield
            if POOL3:
                P.op("pool", lambda e: e.tensor_copy(out=xcb, in_=xc), reads=[R("xc")], writes=[R("xcb")])
                P.op("pool", lambda e: e.tensor_copy(out=ab[:, 0:3], in_=ab[:, 512:515]), reads=[("axbuf", c)], writes=[("axbuf", c)])
            else:
                P.op("dve", lambda e: e.tensor_copy(out=xcb, in_=xc), reads=[R("xc")], writes=[R("xcb")])
                P.op("dve", lambda e: e.tensor_copy(out=ab[:, 0:3], in_=ab[:, 512:515]), reads=[("axbuf", c)], writes=[("axbuf", c)])
            yield
            P.op("pe", lambda e: e.matmul(banks[br], lhsT=Wa[:, c, :], rhs=xcb, start=True, stop=True), reads=["Wa", R("xcb")], writes=[B(br)])
            P.op("pe", lambda e: e.matmul(banks[bi], lhsT=Wx[:, c, :], rhs=xcb, start=True, stop=True), reads=["Wx", R("xcb")], writes=[B(bi)])
            yield
            P.op("act", lambda e: e.activation(out=er, in_=banks[br], func=AF.Exp, scale=-1.0, bias=nb[:, c, 0:1]), reads=[B(br), "nb"], writes=[R("er")])
            yield
            P.op("act", lambda e: e.activation(out=ei, in_=banks[bi], func=AF.Exp, scale=-1.0, bias=nb[:, c, 1:2]), reads=[B(bi), "nb"], writes=[R("ei")])
            yield
            P.op(POOLENG, lambda e: e.tensor_scalar_add(out=er, in0=er, scalar1=1.0), reads=[R("er")], writes=[R("er")])
            yield
            P.op("dve", lambda e: e.reciprocal(out=er, in_=er), reads=[R("er")], writes=[R("er")])
            yield
            P.op("act", lambda e: e.activation(out=a, in_=er, func=AF.Exp, scale=cch[:, c, 0:1]), reads=[R("er"), "cch"], writes=[R("a")])
            yield
            P.op("dve", lambda e: e.tensor_tensor(out=a2, in0=a, in1=a, op=ALU.mult), reads=[R("a")], writes=[R("a2")])
            yield
            P.op("act", lambda e: e.activation(out=a2, in_=a2, func=AF.Ln, scale=-1.0, bias=1.0), reads=[R("a2")], writes=[R("a2")])
            yield
            P.op("act", lambda e: e.activation(out=a2, in_=a2, func=AF.Exp, scale=0.5), reads=[R("a2")], writes=[R("a2")])
            yield
            P.op(POOLENG, lambda e: e.tensor_scalar_add(out=ei, in0=ei, scalar1=1.0), reads=[R("ei")], writes=[R("ei")])
            yield
            P.op(POOLENG, lambda e: e.tensor_tensor(out=u, in0=xc, in1=a2, op=ALU.mult), reads=[R("xc"), R("a2")], writes=[R("u")])
            yield
            P.op("dve", lambda e: e.reciprocal(out=ei, in_=ei), reads=[R("ei")], writes=[R("ei")])
            yield
            if is_ctx:
                P.op("dve", lambda e: e.scalar_tensor_tensor(out=u, in0=u, scalar=ctxf[:, 0:1], in1=ei, op0=ALU.mult, op1=ALU.mult), reads=[R("u"), R("ei"), "ctxf"], writes=[R("u")])
            else:
                P.op("dve", lambda e: e.tensor_tensor(out=u, in0=u, in1=ei, op=ALU.mult), reads=[R("u"), R("ei")], writes=[R("u")])
            yield
            P.op("dve", lambda e: e.tensor_tensor_scan(out=h, data0=a, data1=u, initial=hst[:, c:c + 1], op0=ALU.mult, op1=ALU.add),
                 reads=[R("a"), R("u"), ("hst", c)], writes=[R("h")])
            yield
            P.op("dve", lambda e: e.tensor_copy(out=hst[:, c:c + 1], in_=h[:, 511:512]), reads=[R("h")], writes=[("hst", c)])
            yield
            if need_y:
                for kc in range(8):
                    P.op("pe", lambda e, kc=kc: e.matmul(banks[bag], lhsT=Wl[:, kc, 512 + c * 128:512 + (c + 1) * 128], rhs=x[:, kc, :], start=(kc == 0), stop=(kc == 7)),
                         reads=[("xtl", xs), "Wl"], writes=[B(bag)])
                yield
                P.op("act", lambda e: e.activation(out=eg, in_=banks[bag], func=AF.Exp, scale=-1.0), reads=[B(bag)], writes=[R("eg")])
                yield
                P.op(POOLENG, lambda e: e.tensor_scalar_add(out=eg, in0=eg, scalar1=1.0), reads=[R("eg")], writes=[R("eg")])
                yield
                P.op("dve", lambda e: e.reciprocal(out=eg, in_=eg), reads=[R("eg")], writes=[R("eg")])
                yield
                P.op("dve", lambda e: e.tensor_tensor(out=eg, in0=banks[bag], in1=eg, op=ALU.mult), reads=[R("eg"), B(bag)], writes=[R("eg")])
                yield
                P.op("dve", lambda e: e.tensor_tensor(out=yl, in0=h, in1=eg, op=ALU.mult), reads=[R("eg"), R("h")], writes=[R("yl")])
                yield
                row0 = yrow_a + grp * 512 + c * 128
                if i >= OWN:
                    out_dmas.append(P.dma("sp", ydst[row0:row0 + 128, ycol(i):ycol(i) + 512], yl, reads=[R("yl")], chan=("yl", k)))
                else:
                    out_dmas.append(P.dma("sp", ydst[row0:row0 + 128, 0:128], yl[:, 384:512], reads=[R("yl")], chan=("yl", k)))
                yield

        it = 0
        for i in range(NTW):
            for c in range(4):
                yield from chunk_gen(i, c, it % 2)
                it += 1

    def lru_master():
        for grp in range(NG):
            yield from lru_pass(grp)

    lru_gen = lru_master()
    lru_done = [False]

    def lru_advance(nsteps):
        for _ in range(nsteps):
            if lru_done[0]:
                return
            if next(lru_gen, "DONE") == "DONE":
                lru_done[0] = True

    lru_hook[0] = lru_advance
    for hl in range(NH):
        sb_pass(hl)
    while not lru_done[0]:
        lru_advance(64)
    return out_dmas


def build_F(S):
    nc = bass.Bass("TRN2", target_bir_lowering=False)
    NB = S // 128
    HALF = S // 2
    TOKB = HALF + 128
    dr = lambda name, shape, dt=F32, kind="ExternalInput": nc.dram_tensor(name, shape, dt, kind=kind).ap()
    xT = dr("xT", [1024, S])
    w_sb = dr("w_sb", [8, 1024, 512])
    w_lru = dr("w_lru", [2, 1024, 1024])
    conv_w = dr("conv_w", [2, 512, 4])
    vecs = dr("vecs", [2, 512, 4])
    wga = dr("wga", [2, 4, 128, 128])
    wgx = dr("wgx", [2, 4, 128, 128])
    cmask = dr("cmask", [128, 2048])
    cmats = dr("cmats", [128, 384])
    kmask = dr("kmask", [128, S // 512])
    ctxf = dr("ctxf", [128, 1])
    x_in = dr("x_in", [TOKB, 1024])
    w_o0 = dr("w_o0", [2048, 1024])
    w_i1 = dr("w_i1", [1024, 2688])
    w_o1 = dr("w_o1", [1024, 1024])
    lnp = dr("lnp", [128, 4096])
    sinks = dr("sinks", [128, 16])
    abias_d = dr("abias", [128, 4096])
    abias0_d = dr("abias0", [128, 4096])
    ident_d = dr("ident", [128, 128])
    out = dr("out", [HALF, 1024], F32, "ExternalOutput")
    yint = nc.dram_tensor("yint", [2048, TOKB], BF16).ap()
    NTW = S // 512
    OWN = NTW // 2
    P = Prog(nc)
    cfg = dict(NTW=NTW, own_from=OWN, halo=True, n_heads=8, n_lru_groups=2, ycol=lambda i: 128 + (i - OWN) * 512, yrow_a=0, yrow_b=1024)
    with ExitStack() as stk:
        record_A(nc, P, stk, cfg, xT, w_sb, w_lru, conv_w, vecs, wga, wgx, cmask, cmats, kmask, ctxf, yint)
        P.barrier()
        P.emit()
    P2 = Prog(nc)
    outs = record_B(nc, P2, HALF // 128, yint, x_in, w_o0, w_i1, w_o1, lnp, sinks, abias_d, abias0_d, ident_d, out)
    fin = P2.op("sp", lambda eng: None)
    fin.deps = list(outs)
    P2.emit()
    return nc


R_OLD = 1
R_YOUNG = 1
ALPHA = float(4 ** 0.25)
EPS = 1e-5


def record_B(nc, P, TB, yT_in, x_in, w_o0, w_i1, w_o1, lnp, sinks, abias_d, abias0_d, ident_d, out, ykey=None):
    NBK = TB + 1
    sb = lambda name, shape, dt=F32: nc.alloc_sbuf_tensor(name, shape, dt).ap()
    banks = [nc.alloc_psum_tensor("bankB%d" % i, [128, 512], F32).ap() for i in range(8)]
    B = lambda i: ("bankB", i)

    Wo0 = sb("Wo0", [128, 16, 1024], BF16)
    Wi1 = sb("Wi1", [128, 8, 2688], BF16)
    Wo1 = sb("Wo1", [128, 8, 1024], BF16)
    for h in range(2):
        P.dma("pool", Wo0[:, 8 * h:8 * h + 8, :], w_o0[1024 * h:1024 * (h + 1)].rearrange("(c p) n -> p c n", p=128), writes=["Wo0"], chan=("Wo0", h))
    P.dma("pool", Wi1[:, 0:4, :], w_i1[0:512].rearrange("(c p) n -> p c n", p=128), writes=["Wi1"], chan=("Wi1", 0))
    P.dma("pool", Wi1[:, 4:8, :], w_i1[512:1024].rearrange("(c p) n -> p c n", p=128), writes=["Wi1"], chan=("Wi1", 1))
    P.dma("pool", Wo1, w_o1.rearrange("(c p) n -> p c n", p=128), writes=["Wo1"], chan="Wo1")
    lnb = sb("lnb", [128, 4, 1024])
    P.dma("sp", lnb, lnp.rearrange("p (k n) -> p k n", k=4), writes=["lnb"], chan="lnb")
    snk = sb("snk", [128, 16])
    P.dma("sp", snk, sinks, writes=["snk"], chan="snk")
    abT = sb("abT_s", [128, 16, 2, 128])
    abT0 = sb("abT0_s", [128, 16, 128])
    P.dma("sp", abT, abias_d.rearrange("p (i h q) -> p i h q", i=16, h=2), writes=["abT"], chan="abias")
    P.dma("sp", abT0, abias0_d.rearrange("p (i h q) -> p i h q", i=16, h=2)[:, :, 0, :], writes=["abT0"], chan="abias0")
    cvec = sb("cvec", [128, 16])
    esk = sb("esk", [128, 16])
    P.op("dve", lambda e: e.tensor_scalar_max(out=cvec, in0=snk, scalar1=0.0), reads=["snk"], writes=["cvec"])
    P.op("dve", lambda e: e.tensor_tensor(out=esk, in0=snk, in1=cvec, op=ALU.subtract), reads=["snk", "cvec"], writes=["esk"])
    P.op("act", lambda e: e.activation(out=esk, in_=esk, func=AF.Exp), reads=["esk"], writes=["esk"])
    for i16 in range(16):
        P.op("dve", lambda e, i16=i16: e.tensor_scalar(out=abT[:, i16, :, :], in0=abT[:, i16, :, :], scalar1=cvec[:, i16:i16 + 1], scalar2=None, op0=ALU.subtract), reads=["abT", "cvec"], writes=["abT"])
        P.op("dve", lambda e, i16=i16: e.tensor_scalar(out=abT0[:, i16, :], in0=abT0[:, i16, :], scalar1=cvec[:, i16:i16 + 1], scalar2=None, op0=ALU.subtract), reads=["abT0", "cvec"], writes=["abT0"])
    ident = sb("ident_s", [128, 128], BF16)
    P.dma("pool", ident, ident_d, writes=["identB"], chan="identB")

    yin = [sb("yin%d" % i, [128, 16, 128], BF16) for i in range(2)]
    xin = [sb("xin%d" % i, [128, 1024]) for i in range(2)]
    x1 = [sb("x1_%d" % i, [128, 1024]) for i in range(2)]
    x1b = sb("x1b", [128, 1024], BF16)
    x1T = sb("x1T", [128, 8, 128], BF16)
    qT = [sb("qTB%d" % i, [128, 8, 128], BF16) for i in range(2)]
    kT = sb("kTr", [128, 4, 3, 128], BF16)
    vv = sb("vr", [128, 3, 2, 65], BF16)
    sg = [sb("sgB%d" % i, [128, 1024]) for i in range(2)]
    scb = [sb("scb%d" % i, [128, 2, 512]) for i in range(2)]
    pT = [sb("pT%d" % i, [128, 2, 512], BF16) for i in range(2)]
    st = [sb("st%d" % i, [128, 16]) for i in range(2)]
    y1 = sb("y1", [128, 1024], BF16)
    y1T = sb("y1T", [128, 8, 128], BF16)
    ob = [sb("ob%d" % i, [128, 1024]) for i in range(2)]
    stats = [sb("stats%d" % i, [128, 2, 6]) for i in range(2)]
    mv = [sb("mv%d" % i, [128, 4]) for i in range(2)]
    P.op("pool", lambda e: e.memset(vv, 1.0), writes=[("vv", 0), ("vv", 1), ("vv", 2)])
    out_dmas = []
    yT_v = yT_in.rearrange("(c p) t -> p c t", p=128)

    def layer_norm(buf, key, gi, sid):
        S_, M_ = stats[sid], mv[sid]
        sk, mk = ("stats", sid), ("mv", sid)
        for hh in range(2):
            P.op("dve", lambda e, hh=hh: e.bn_stats(out=S_[:, hh, :], in_=buf[:, hh * 512:(hh + 1) * 512]), reads=[key], writes=[sk])
        yield
        P.op("dve", lambda e: e.bn_aggr(out=M_[:, 0:2], in_=S_.rearrange("p a b -> p (a b)")), reads=[sk], writes=[mk])
        yield
        P.op("dve", lambda e: e.tensor_scalar_add(out=M_[:, 2:3], in0=M_[:, 1:2], scalar1=EPS), reads=[mk], writes=[mk])
        yield
        P.op("act", lambda e: e.activation(out=M_[:, 2:3], in_=M_[:, 2:3], func=AF.Ln), reads=[mk], writes=[mk])
        yield
        P.op("act", lambda e: e.activation(out=M_[:, 2:3], in_=M_[:, 2:3], func=AF.Exp, scale=-0.5), reads=[mk], writes=[mk])
        yield
        P.op("dve", lambda e: e.scalar_tensor_tensor(out=M_[:, 3:4], in0=M_[:, 0:1], scalar=-1.0, in1=M_[:, 2:3], op0=ALU.mult, op1=ALU.mult), reads=[mk], writes=[mk])
        yield
        P.op("act", lambda e: e.activation(out=buf, in_=buf, func=AF.Identity, scale=M_[:, 2:3], bias=M_[:, 3:4]), reads=[key, mk], writes=[key])
        yield
        P.op("pool", lambda e: e.tensor_tensor(out=buf, in0=buf, in1=lnb[:, gi, :], op=ALU.mult), reads=[key, "lnb"], writes=[key])
        yield
        P.op("pool", lambda e: e.tensor_tensor(out=buf, in0=buf, in1=lnb[:, gi + 1, :], op=ALU.add), reads=[key, "lnb"], writes=[key])
        yield

    def load_blk(j):
        ys = j % 2
        extra = list(ykey(j)) if ykey is not None else []
        for q4 in range(4):
            P.dma("sp", yin[ys][:, 4 * q4:4 * q4 + 4, :], yT_v[:, 4 * q4:4 * q4 + 4, j * 128:(j + 1) * 128], reads=extra, writes=[("yin", ys)], chan=("yin", ys, q4))
        P.dma("sp", xin[ys], x_in[j * 128:(j + 1) * 128, :], writes=[("xin", ys)], chan=("xin", ys))

    def blk(j):
        ys = j % 2
        xs = j % 2
        X1 = x1[xs]
        x1k = ("x1", xs)
        if j == 0:
            load_blk(0)
        if j + 1 < NBK:
            load_blk(j + 1)
        for kc in range(16):
            for hh in range(2):
                P.op("pe", lambda e, hh=hh, kc=kc: e.matmul(banks[hh], lhsT=yin[ys][:, kc, :], rhs=Wo0[:, kc, hh * 512:(hh + 1) * 512], start=(kc == 0), stop=(kc == 15)),
                     reads=[("yin", ys), "Wo0"], writes=[B(hh)])
            if kc % 2 == 1:
                yield
        for hh in range(2):
            P.op("dve", lambda e, hh=hh: e.scalar_tensor_tensor(out=X1[:, hh * 512:(hh + 1) * 512], in0=xin[ys][:, hh * 512:(hh + 1) * 512], scalar=ALPHA, in1=banks[hh], op0=ALU.mult, op1=ALU.add),
                 reads=[("xin", ys), B(hh)], writes=[x1k])
            yield
        yield from layer_norm(X1, x1k, 0, 0)
        P.op("pool", lambda e: e.tensor_copy(out=x1b, in_=X1), reads=[x1k], writes=["x1b"])
        yield
        tb = banks[2].bitcast(BF16)
        for c in range(8):
            P.op("pe", lambda e, c=c: e.transpose(tb[:, c * 128:(c + 1) * 128], x1b[:, c * 128:(c + 1) * 128], ident), reads=["x1b", "identB"], writes=[B(2)])
            if c % 4 == 3:
                yield
        P.op("act", lambda e: e.activation(out=x1T.rearrange("p c t -> p (c t)"), in_=tb, func=AF.Copy), reads=[B(2)], writes=["x1T"])
        yield
        ks = j % 3
        for c2 in range(4):
            for kc in range(8):
                P.op("pe", lambda e, c2=c2, kc=kc: e.matmul(banks[3][:, c2 * 128:(c2 + 1) * 128], lhsT=Wi1[:, kc, 1024 + c2 * 128:1024 + (c2 + 1) * 128], rhs=x1T[:, kc, :], start=(kc == 0), stop=(kc == 7)),
                     reads=["x1T", "Wi1"], writes=[B(3)])
            yield
        for kc in range(8):
            P.op("pe", lambda e, kc=kc: e.matmul(banks[2][:, 0:128], lhsT=x1T[:, kc, :], rhs=Wi1[:, kc, 1536:1664], start=(kc == 0), stop=(kc == 7)),
                 reads=["x1T", "Wi1"], writes=[B(2)])
        yield
        P.op("dve", lambda e: e.tensor_copy(out=kT[:, :, ks, :], in_=banks[3].rearrange("p (c t) -> p c t", c=4)), reads=[B(3)], writes=[("kT", ks)])
        yield
        P.op("dve", lambda e: e.tensor_copy(out=vv[:, ks, :, 0:64], in_=banks[2][:, 0:128].rearrange("p (c d) -> p c d", c=2)), reads=[B(2)], writes=[("vv", ks)])
        yield
        if j == 0:
            yield "S2"
            return
        qs = j % 2
        Q = qT[qs]
        qk = ("qTB", qs)
        SG = sg[qs]
        sgk = ("sgB", qs)
        for c in range(8):
            bq = c // 4
            for kc in range(8):
                P.op("pe", lambda e, c=c, kc=kc, bq=bq: e.matmul(banks[bq][:, (c % 4) * 128:(c % 4 + 1) * 128], lhsT=Wi1[:, kc, c * 128:(c + 1) * 128], rhs=x1T[:, kc, :], start=(kc == 0), stop=(kc == 7)),
                     reads=["x1T", "Wi1"], writes=[B(bq)])
            yield
        for hh in range(2):
            P.op("dve", lambda e, hh=hh: e.tensor_scalar(out=Q[:, 4 * hh:4 * hh + 4, :], in0=banks[hh].rearrange("p (c t) -> p c t", c=4), scalar1=0.125, scalar2=None, op0=ALU.mult),
                 reads=[B(hh)], writes=[qk])
            yield
        for kc in range(8):
            for hh in range(2):
                P.op("pe", lambda e, hh=hh, kc=kc: e.matmul(banks[hh], lhsT=x1T[:, kc, :], rhs=Wi1[:, kc, 1664 + hh * 512:1664 + (hh + 1) * 512], start=(kc == 0), stop=(kc == 7)),
                     reads=["x1T", "Wi1"], writes=[B(hh)])
            if kc % 2 == 1:
                yield
        for hh in range(2):
            P.op("act", lambda e, hh=hh: e.activation(out=SG[:, hh * 512:(hh + 1) * 512], in_=banks[hh], func=AF.Exp, scale=-1.0), reads=[B(hh)], writes=[sgk])
            yield
        P.op("dve", lambda e: e.tensor_scalar_add(out=SG, in0=SG, scalar1=1.0), reads=[sgk], writes=[sgk])
        yield
        P.op("dve", lambda e: e.reciprocal(out=SG, in_=SG), reads=[sgk], writes=[sgk])
        yield
        for hh in range(2):
            P.op("dve", lambda e, hh=hh: e.tensor_tensor(out=SG[:, hh * 512:(hh + 1) * 512], in0=SG[:, hh * 512:(hh + 1) * 512], in1=banks[hh], op=ALU.mult), reads=[sgk, B(hh)], writes=[sgk])
            yield
        yield "S2"
        ps = (j - 1) % 3
        tb6 = banks[6].bitcast(BF16)

        def grp_gen(gi):
            c, par = gi // 2, gi % 2
            sl = gi % 2
            S_ = scb[sl]
            PT = pT[sl]
            T_ = st[sl]
            ob_ = 6 + sl
            for half, slot in ((0, ps), (1, ks)):
                P.op("pe", lambda e, half=half, slot=slot: e.matmul(banks[4 + half], lhsT=kT[:, 2 * c + par, slot, :], rhs=Q[:, 4 * c:4 * c + 4, :].rearrange("p a q -> p (a q)"), start=True, stop=True),
                     reads=[qk, ("kT", slot)], writes=[B(4 + half)])
            yield
            for half in range(2):
                if half == 0 and j == 1:
                    bsrc, bkey = abT0[:, 4 * gi:4 * gi + 4, :], "abT0"
                else:
                    bsrc, bkey = abT[:, 4 * gi:4 * gi + 4, half, :], "abT"
                P.op("dve", lambda e, half=half, bsrc=bsrc: e.tensor_tensor(out=S_[:, half, :].rearrange("p (a q) -> p a q", a=4), in0=banks[4 + half].rearrange("p (a q) -> p a q", a=4), in1=bsrc, op=ALU.add),
                     reads=[B(4 + half), bkey], writes=[("scb", sl)])
                yield
            for half in range(2):
                P.op("act", lambda e, half=half: e.activation(out=PT[:, half, :], in_=S_[:, half, :], func=AF.Exp), reads=[("scb", sl)], writes=[("pT", sl)])
                yield
            for jj in range(4):
                for half, slot in ((0, ps), (1, ks)):
                    P.op("pe", lambda e, jj=jj, half=half, slot=slot: e.matmul(banks[ob_][:, jj * 65:(jj + 1) * 65], lhsT=PT[:, half, jj * 128:(jj + 1) * 128],
                                                                            rhs=vv[:, slot, c, :], start=(half == 0), stop=(half == 1)),
                         reads=[("pT", sl), ("vv", slot)], writes=[B(ob_)])
            yield
            ov = banks[ob_][:, 0:260].rearrange("p (a d) -> p a d", a=4)
            P.op("dve", lambda e: e.tensor_tensor(out=T_[:, 0:4], in0=ov[:, :, 64], in1=esk[:, 4 * gi:4 * gi + 4], op=ALU.add), reads=[B(ob_), "esk"], writes=[("st", sl)])
            yield
            P.op("dve", lambda e: e.reciprocal(out=T_[:, 4:8], in_=T_[:, 0:4]), reads=[("st", sl)], writes=[("st", sl)])
            yield
            for jj in range(4):
                h = 8 * c + 2 * jj + par
                P.op("dve", lambda e, jj=jj, h=h: e.scalar_tensor_tensor(out=y1[:, h * 64:(h + 1) * 64], in0=ov[:, jj, 0:64], scalar=T_[:, 4 + jj:5 + jj],
                                                                     in1=SG[:, h * 64:(h + 1) * 64], op0=ALU.mult, op1=ALU.mult),
                     reads=[B(ob_), ("st", sl), sgk], writes=["y1"])
                yield

        for gi in range(4):
            yield from grp_gen(gi)
        for c in range(8):
            P.op("pe", lambda e, c=c: e.transpose(tb6[:, c * 128:(c + 1) * 128], y1[:, c * 128:(c + 1) * 128], ident), reads=["y1", "identB"], writes=[B(6)])
            if c % 4 == 3:
                yield
        P.op("act", lambda e: e.activation(out=y1T.rearrange("p c t -> p (c t)"), in_=tb6, func=AF.Copy), reads=[B(6)], writes=["y1T"])
        yield
        for kc in range(8):
            for hh in range(2):
                P.op("pe", lambda e, hh=hh, kc=kc: e.matmul(banks[4 + hh], lhsT=y1T[:, kc, :], rhs=Wo1[:, kc, hh * 512:(hh + 1) * 512], start=(kc == 0), stop=(kc == 7)),
                     reads=["y1T", "Wo1"], writes=[B(4 + hh)])
            if kc % 2 == 1:
                yield
        os_ = j % 2
        OB = ob[os_]
        obk = ("ob", os_)
        for hh in range(2):
            P.op("dve", lambda e, hh=hh: e.scalar_tensor_tensor(out=OB[:, hh * 512:(hh + 1) * 512], in0=X1[:, hh * 512:(hh + 1) * 512], scalar=ALPHA, in1=banks[4 + hh], op0=ALU.mult, op1=ALU.add),
                 reads=[x1k, B(4 + hh)], writes=[obk])
            yield
        yield from layer_norm(OB, obk, 2, 1)
        out_dmas.append(P.dma("sp", out[(j - 1) * 128:j * 128, :], OB, reads=[obk], chan=("ob", os_)))
        yield

    gens = [blk(j) for j in range(NBK)]
    older = None
    younger = None
    paused = False
    nxt = 0
    DONE = object()
    while True:
        if younger is None and nxt < NBK:
            younger = gens[nxt]
            nxt += 1
            paused = False
        if older is None and younger is None:
            break
        if older is None and paused:
            older, younger, paused = younger, None, False
            continue
        for _ in range(R_OLD):
            if older is not None:
                if next(older, DONE) is DONE:
                    older = None
        for _ in range(R_YOUNG):
            if younger is not None and not paused:
                r = next(younger, DONE)
                if r is DONE:
                    younger = None
                elif r == "S2":
                    paused = True
    return out_dmas


import ml_dtypes
from concourse.bass_utils import run_bass_kernel_spmd

S_FULL = 8192
_CACHE = {}


def _consts_A():
    p = np.arange(128)[:, None]; f = np.arange(512)[None, :]
    cmask = np.zeros((128, 2048), np.float32)
    for r in range(4):
        cmask[:, r * 512:(r + 1) * 512] = np.where(128 * r + p < f, 0.0, -30000.0)
    j = np.arange(128)[:, None]; s = np.arange(128)[None, :]
    ident = np.eye(128, dtype=np.float32)
    trineg = np.where(j >= s, -1.0, 0.0).astype(np.float32)
    restneg = np.where(j < s, -1.0, 0.0).astype(np.float32)
    return cmask, np.concatenate([ident, trineg, restneg], axis=1)


def _consts_B():
    k = np.arange(128)[:, None]; q = np.arange(128)[None, :]
    slopes = np.array([2.0 ** (-8.0 * (i + 1) / 16) for i in range(16)], np.float32)
    ab = np.zeros((128, 16, 2, 128), np.float32)
    for idx in range(16):
        g, jj = idx // 4, idx % 4
        c, par = g // 2, g % 2
        h = 8 * c + 2 * jj + par
        for half in range(2):
            dist = (q - k + 128).astype(np.float32) if half == 0 else (q - k).astype(np.float32)
            valid = (dist >= 0) & (dist < 128)
            ab[:, idx, half, :] = np.where(valid, -slopes[h] * dist, -30000.0)
    ab0 = ab.copy(); ab0[:, :, 0, :] = -30000.0
    return ab.reshape(128, 4096), ab0.reshape(128, 4096)


def _sinks_group_order(s16):
    out = np.zeros(16, np.float32)
    for idx in range(16):
        g, jj = idx // 4, idx % 4
        c, par = g // 2, g % 2
        out[idx] = s16[8 * c + 2 * jj + par]
    return out


def _shared_inputs(inp):
    w_in = inp["e_w_in"][0]
    w_sb = np.zeros((8, 1024, 512), np.float32)
    for h in range(8):
        for k, base in enumerate([2048, 3072, 4096, 5120]):
            w_sb[h, :, k * 128:(k + 1) * 128] = w_in[:, base + h * 128: base + (h + 1) * 128]
    w_lru = np.stack([np.concatenate([w_in[:, 512 * q:512 * q + 512], w_in[:, 1024 + 512 * q:1024 + 512 * q + 512]], axis=1) for q in range(2)])
    conv_w = np.stack([np.ascontiguousarray(inp["e_conv_w"][0][:, 512 * q:512 * q + 512].T) for q in range(2)])
    vecs = np.stack([np.stack([inp["e_conv_b"][0][512 * q:512 * q + 512], inp["e_b_gate_a"][0][512 * q:512 * q + 512],
                               inp["e_b_gate_x"][0][512 * q:512 * q + 512], inp["e_lru_lambda"][0][512 * q:512 * q + 512]], axis=1) for q in range(2)])
    cmask, cmats = _consts_A()
    w = inp["o_w_in"][0]
    q = w[:, 0:1024]; k = w[:, 1024:1152]; v = w[:, 1152:1280]; gg = w[:, 1280:2304]
    z64 = np.zeros((1024, 64), np.float32)
    w_i1 = np.concatenate([q, k[:, 0:64], z64, z64, k[:, 0:64], k[:, 64:128], z64, z64, k[:, 64:128], v, gg], axis=1)
    lnp = np.concatenate([inp["e_ln_g"][0], inp["e_ln_b"][0], inp["o_ln_g"][0], inp["o_ln_b"][0]])[None, :].repeat(128, 0)
    sinks = _sinks_group_order(inp["o_sinks"][0])[None, :].repeat(128, 0)
    ab, ab0 = _consts_B()
    return {"w_sb": w_sb, "w_lru": np.ascontiguousarray(w_lru), "conv_w": np.ascontiguousarray(conv_w), "vecs": np.ascontiguousarray(vecs),
            "wga": np.ascontiguousarray(inp["e_w_gate_a"][0].reshape(2, 4, 128, 128)), "wgx": np.ascontiguousarray(inp["e_w_gate_x"][0].reshape(2, 4, 128, 128)),
            "cmask": cmask, "cmats": cmats, "w_o0": np.ascontiguousarray(inp["e_w_out"][0]), "w_i1": np.ascontiguousarray(w_i1),
            "w_o1": np.ascontiguousarray(inp["o_w_out"][0]), "lnp": np.ascontiguousarray(lnp), "sinks": np.ascontiguousarray(sinks),
            "abias": ab, "ident": np.eye(128, dtype=np.float32)}, ab0


def _core_inputs(inp, shared, ab0, b, g, S):
    HALF = S // 2
    x = inp["x"][b, :S]
    xw = np.zeros((S, 1024), np.float32)
    if g == 0:
        xw[HALF:] = x[:HALF]
    else:
        xw[:] = x
    kmask = np.zeros((128, S // 512), np.float32)
    if g == 0:
        kmask[:, :S // 1024] = -30000.0
    x_in = np.zeros((HALF + 128, 1024), np.float32)
    t0 = g * HALF - 128
    lo = max(t0, 0)
    x_in[lo - t0:] = x[lo:t0 + HALF + 128]
    d = dict(shared)
    d.update({"xT": np.ascontiguousarray(xw.T), "kmask": kmask, "ctxf": np.full((128, 1), float(g), np.float32),
              "x_in": x_in, "abias0": ab0 if g == 0 else shared["abias"]})
    return d


def kernel(**inputs):
    inp = {k: np.asarray(v, dtype=np.float32) for k, v in inputs.items()}
    S = S_FULL
    cores = [(b, g) for b in range(4) for g in range(2)]
    if "F" not in _CACHE:
        _CACHE["F"] = build_F(S)
    shared, ab0 = _shared_inputs(inp)
    res = run_bass_kernel_spmd(_CACHE["F"], [_core_inputs(inp, shared, ab0, b, g, S) for b, g in cores], core_ids=list(range(8)))
    out = np.zeros((4, S, 1024), np.float32)
    for ci, (b, g) in enumerate(cores):
        out[b, g * (S // 2):(g + 1) * (S // 2)] = res.results[ci]["out"]
    return out
```

```python
import numpy as np
from contextlib import ExitStack
import concourse.bass as bass
import concourse.mybir as mybir

F32 = mybir.dt.float32
BF16 = mybir.dt.bfloat16
AF = mybir.ActivationFunctionType
ALU = mybir.AluOpType
AX = mybir.AxisListType

ENGS = ("pe", "act", "dve", "pool", "sp")
SAME_ENGINE_SYNC = True


class Op:
    __slots__ = ("eng", "fn", "deps", "needs_inc", "inc_val", "dma_sem", "dma_val", "idx", "is_dma")

    def __init__(self, eng, fn, is_dma=False, dma_sem=None):
        self.eng = eng
        self.fn = fn
        self.deps = []
        self.needs_inc = False
        self.inc_val = None
        self.is_dma = is_dma
        self.dma_sem = dma_sem
        self.dma_val = None


class Prog:
    def __init__(self, nc):
        self.nc = nc
        self.ops = {e: [] for e in ENGS}
        self.res = {}
        self.stack = ExitStack()
        self.esem = {}
        self.dma_chan = {}
        self.nsem = 0
        self.chan_last = {}

    def new_sem(self, name):
        self.nsem += 1
        return self.stack.enter_context(self.nc.semaphore(name))

    def chan(self, key):
        if key not in self.dma_chan:
            self.dma_chan[key] = [self.new_sem("dc%d" % len(self.dma_chan)), 0]
        return self.dma_chan[key]

    def _rec(self, o, reads, writes):
        deps = []
        for r in reads:
            st = self.res.setdefault(r, [None, []])
            if st[0] is not None:
                deps.append(st[0])
        for w in writes:
            st = self.res.setdefault(w, [None, []])
            if st[0] is not None:
                deps.append(st[0])
            deps.extend(st[1])
        for r in reads:
            self.res[r][1].append(o)
        for w in writes:
            st = self.res[w]
            st[0] = o
            st[1] = []
        seen = set()
        for d in deps:
            if d is o or id(d) in seen:
                continue
            seen.add(id(d))
            if (not d.is_dma) and d.eng == o.eng and (d.eng == "pe" or not SAME_ENGINE_SYNC):
                continue
            o.deps.append(d)
            if not d.is_dma:
                d.needs_inc = True
        self.ops[o.eng].append(o)
        return o

    def op(self, eng, fn, reads=(), writes=()):
        return self._rec(Op(eng, fn), reads, writes)

    def dma(self, eng, out, in_, reads=(), writes=(), chan=None, **kw):
        c = self.chan(chan)
        c[1] += 16
        o = Op(eng, lambda e, out=out, in_=in_, kw=kw: e.dma_start(out=out, in_=in_, **kw), is_dma=True, dma_sem=c[0])
        o.dma_val = c[1]
        self.chan_last[chan] = o
        return self._rec(o, reads, writes)

    def barrier(self):
        lasts = []
        for e in ENGS:
            for o in reversed(self.ops[e]):
                if not o.is_dma:
                    o.needs_inc = True
                    lasts.append(o)
                    break
        dmas = list(self.chan_last.values())
        for e in ENGS:
            b = Op(e, lambda eng: None)
            b.deps = [o for o in lasts if o.eng != e] + dmas
            self.ops[e].append(b)

    def emit(self):
        nc = self.nc
        for e in ENGS:
            self.esem[e] = self.new_sem("es_" + e)
            cnt = 0
            for o in self.ops[e]:
                if o.needs_inc and not o.is_dma:
                    cnt += 1
                    o.inc_val = cnt
        esem = self.esem

        def run(e, engobj):
            waited = {}
            for o in self.ops[e]:
                for d in o.deps:
                    if d.is_dma:
                        sem, val = d.dma_sem, d.dma_val
                    else:
                        sem, val = esem[d.eng], d.inc_val
                    k = id(sem)
                    if waited.get(k, 0) < val:
                        engobj.wait_ge(sem, val)
                        waited[k] = val
                inst = o.fn(engobj)
                if o.is_dma:
                    inst.then_inc(o.dma_sem, 16)
                elif o.needs_inc:
                    inst.then_inc(esem[e], 1)

        with nc.Block() as block:
            @block.tensor
            def _(eng):
                run("pe", eng)

            @block.scalar
            def _(eng):
                run("act", eng)

            @block.vector
            def _(eng):
                run("dve", eng)

            @block.gpsimd
            def _(eng):
                run("pool", eng)

            @block.sync
            def _(eng):
                run("sp", eng)
        self.stack.close()

POOLENG = 'dve'
LWIN = 3
POOL3 = False


def interleave(gens, window=2):
    gens = list(gens)
    active = []
    nxt = 0
    DONE = object()
    while active or nxt < len(gens):
        while len(active) < window and nxt < len(gens):
            active.append(gens[nxt])
            nxt += 1
        for g in list(active):
            if next(g, DONE) is DONE:
                assert active[0] is g, "generators must finish in admission order (slot reuse safety)"
                active.remove(g)


def record_A(nc, P, stk, cfg, xT, w_sb, w_lru, conv_w, vecs, wga, wgx, cmask, cmats, kmask_d, ctxf_d, ydst):
    NTW, OWN, HALO = cfg["NTW"], cfg["own_from"], cfg["halo"]
    NH, NG = cfg["n_heads"], cfg["n_lru_groups"]
    ycol = cfg["ycol"]
    yrow_a, yrow_b = cfg["yrow_a"], cfg["yrow_b"]
    S = NTW * 512
    NB = S // 128

    def sb(name, shape, dt=F32):
        return stk.enter_context(nc.sbuf_tensor(name, shape, dt)).ap()
    banks = [stk.enter_context(nc.psum_tensor("bank%d" % i, [128, 512], F32)).ap() for i in range(8)]
    B = lambda i: ("bank", i)

    cm = sb("cm", [128, 2048], BF16)
    cmt = sb("cmt", [128, 384], BF16)
    P.dma("pool", cm, cmask, writes=["cm"], chan="cm")
    P.dma("pool", cmt, cmats, writes=["cmt"], chan="cmt")
    ident = cmt[:, 0:128]
    trineg = cmt[:, 128:256]
    restneg = cmt[:, 256:384]
    kmask = sb("kmask_s", [128, NTW])
    P.dma("sp", kmask, kmask_d, writes=["kmask"], chan="kmask")
    ctxf = sb("ctxf_s", [128, 1])
    P.dma("sp", ctxf, ctxf_d, writes=["ctxf"], chan="ctxf")

    out_dmas = []
    lru_hook = [None]
    xT_v = xT.rearrange("(c p) t -> p c t", p=128)
    NXS = 4
    xts = [sb("xt%d" % i, [128, 8, 512], BF16) for i in range(NXS)]
    xcnt = [0]

    def load_x(i):
        s = xcnt[0] % NXS
        xcnt[0] += 1
        P.dma("pool", xts[s], xT_v[:, :, i * 512:(i + 1) * 512], writes=[("xt", s)], chan=("xt", s))
        return s

    Wt = [sb("Wt%d" % i, [128, 8, 512], BF16) for i in range(2)]
    kT = sb("kT0", [128, S], BF16)
    V = sb("V0", [128, NB, 128], BF16)
    qTb = [sb("qT%d" % i, [128, 512], BF16) for i in range(3)]
    sgb = [sb("sg%d" % i, [128, 512], F32) for i in range(3)]
    NE, NL, NE2, NW = 4, 4, 3, 3
    eb = [sb("e%d" % i, [128, 512], F32) for i in range(NE)]
    Lb = [sb("L%d" % i, [128, 512], BF16) for i in range(NL)]
    E2b = [sb("E2%d" % i, [128, 512], F32) for i in range(NE2)]
    wb = [sb("w%d" % i, [128, 512], BF16) for i in range(NW)]
    yb = [sb("yb%d" % i, [128, 512], BF16) for i in range(3)]
    vTs = sb("vTs", [128, 512], BF16)
    SCALE = 1.0 / np.sqrt(128.0)
    ZB = [0, 1]
    CB = 2
    OB = [3, 3]
    IB = [5, 6]
    ibc = [0]

    def next_ib():
        b = IB[ibc[0] % len(IB)]
        ibc[0] += 1
        return b

    def sb_pass(hl):
        ws = hl % 2
        W = Wt[ws]
        P.dma("pool", W, w_sb[hl].rearrange("(c p) n -> p c n", p=128), writes=[("W", ws)], chan=("W", ws))
        xslot = {}
        first_q = OWN - 1 if HALO else OWN
        cons = ([OWN - 1] + list(range(OWN - 2, -1, -1)) if HALO else []) + list(range(OWN, NTW))
        cpos = {t: k for k, t in enumerate(cons)}
        lptr = [0]

        def ensure_loaded(t):
            while lptr[0] <= min(cpos[t] + 1, len(cons) - 1):
                tt = cons[lptr[0]]
                xslot[tt] = load_x(tt)
                lptr[0] += 1

        def inproj_groups(i):
            xs = xslot[i]
            x = xts[xs]
            qs = i % 3

            def g_q():
                b = next_ib()
                for c in range(8):
                    P.op("pe", lambda e, c=c, b=b: e.matmul(banks[b], lhsT=W[:, c, 0:128], rhs=x[:, c, :], start=(c == 0), stop=(c == 7)),
                         reads=[("xt", xs), ("W", ws)], writes=[B(b)])
                P.op("dve", lambda e, b=b: e.tensor_scalar(out=qTb[qs], in0=banks[b], scalar1=float(SCALE), scalar2=None, op0=ALU.mult),
                     reads=[B(b)], writes=[("qT", qs)])

            def g_k():
                b = next_ib()
                for c in range(8):
                    P.op("pe", lambda e, c=c, b=b: e.matmul(banks[b], lhsT=W[:, c, 128:256], rhs=x[:, c, :], start=(c == 0), stop=(c == 7)),
                         reads=[("xt", xs), ("W", ws)], writes=[B(b)])
                P.op("dve", lambda e, b=b: e.tensor_copy(out=kT[:, i * 512:(i + 1) * 512], in_=banks[b]),
                     reads=[B(b)], writes=[("kT", i)])

            def g_g():
                b = next_ib()
                S_ = sgb[qs]
                for c in range(8):
                    P.op("pe", lambda e, c=c, b=b: e.matmul(banks[b], lhsT=W[:, c, 384:512], rhs=x[:, c, :], start=(c == 0), stop=(c == 7)),
                         reads=[("xt", xs), ("W", ws)], writes=[B(b)])
                P.op("act", lambda e, b=b: e.activation(out=S_, in_=banks[b], func=AF.Exp, scale=-1.0), reads=[B(b)], writes=[("sg", qs)])
                P.op("dve", lambda e: e.tensor_scalar_add(out=S_, in0=S_, scalar1=1.0), reads=[("sg", qs)], writes=[("sg", qs)])
                P.op("dve", lambda e: e.reciprocal(out=S_, in_=S_), reads=[("sg", qs)], writes=[("sg", qs)])
                P.op("dve", lambda e, b=b: e.tensor_tensor(out=S_, in0=S_, in1=banks[b], op=ALU.mult), reads=[("sg", qs), B(b)], writes=[("sg", qs)])

            def g_v():
                b = next_ib()
                for c in range(8):
                    P.op("pe", lambda e, c=c, b=b: e.matmul(banks[b], lhsT=W[:, c, 256:384], rhs=x[:, c, :], start=(c == 0), stop=(c == 7)),
                         reads=[("xt", xs), ("W", ws)], writes=[B(b)])
                P.op("dve", lambda e, b=b: e.tensor_copy(out=vTs, in_=banks[b]), reads=[B(b)], writes=["vTs"])

            def g_v2():
                b2 = next_ib()
                tb2 = banks[b2].bitcast(BF16)
                for j in range(4):
                    P.op("pe", lambda e, j=j, tb2=tb2: e.transpose(tb2[:, j * 128:(j + 1) * 128], vTs[:, j * 128:(j + 1) * 128], ident), reads=["vTs", "cmt"], writes=[B(b2)])
                P.op("dve", lambda e, tb2=tb2: e.tensor_copy(out=V[:, 4 * i:4 * i + 4, :], in_=tb2[:, 0:512].rearrange("p (j d) -> p j d", j=4)),
                     reads=[B(b2)], writes=[("V", i)])
            if i >= first_q:
                return [g_v, g_q, g_v2, g_k, g_g]
            return [g_v, g_k, g_v2]

        ensure_loaded(cons[0])
        for g in inproj_groups(cons[0]):
            g()
        pend = list(cons[1:OWN + 1]) if HALO else []
        queued = set(pend) | {cons[0]}

        tasks = []
        if HALO:
            ih = OWN - 1
            tasks += [(ih, kb, 384, 128) for kb in reversed(range(4 * ih + 4))]
        for i in range(OWN, NTW):
            tasks += [(i, kb, 0, 512) for kb in reversed(range(4 * i + 4))]
        NTK = len(tasks)
        side = []

        def first_of_tile(n):
            i, kb, c0, qw = tasks[n]
            return kb == 4 * i + 3

        def last_of_tile(n):
            return tasks[n][1] == 0

        for s in range(NTK + 5):
            if s < NTK:
                i, kb, c0, qw = tasks[s]
                if first_of_tile(s):
                    ni = i + 1
                    if ni < NTW and ni > first_q and ni not in queued:
                        queued.add(ni)
                        pend.append(ni)
                zb = ZB[s % 2]
                diag = kb >= 4 * i
                qs = i % 3
                P.op("pe", lambda e, zb=zb, kb=kb, qs=qs, diag=diag, c0=c0, qw=qw: e.matmul(banks[zb][:, 0:qw], lhsT=kT[:, kb * 128:(kb + 1) * 128], rhs=qTb[qs][:, c0:c0 + qw], start=True, stop=not diag),
                     reads=[("kT", kb // 4), ("qT", qs)], writes=[B(zb)])
                if diag:
                    r = kb - 4 * i
                    P.op("pe", lambda e, zb=zb, r=r, c0=c0, qw=qw: e.matmul(banks[zb][:, 0:qw], lhsT=ident, rhs=cm[:, r * 512 + c0:r * 512 + c0 + qw], start=False, stop=True),
                         reads=["cm", "cmt"], writes=[B(zb)])
            n = s - 3
            if 0 <= n < NTK:
                qw = tasks[n][3]
                P.op("act", lambda e, n=n, qw=qw: e.activation(out=E2b[n % NE2][:, 0:qw], in_=banks[CB][:, 0:qw], func=AF.Exp),
                     reads=[B(CB)], writes=[("E2", n % NE2)])
            n = s - 1
            if 0 <= n < NTK:
                zb = ZB[n % 2]
                kb, qw = tasks[n][1], tasks[n][3]
                P.op("act", lambda e, n=n, zb=zb, kb=kb, qw=qw: e.activation(out=eb[n % NE][:, 0:qw], in_=banks[zb][:, 0:qw], func=AF.Exp, bias=kmask[:, kb // 4:kb // 4 + 1]),
                     reads=[B(zb), "kmask"], writes=[("e", n % NE)])
                P.op("act", lambda e, n=n, qw=qw: e.activation(out=Lb[n % NL][:, 0:qw], in_=eb[n % NE][:, 0:qw], func=AF.Ln, bias=1.0),
                     reads=[("e", n % NE)], writes=[("L", n % NL)])
            n = s - 3
            if 0 <= n < NTK and not last_of_tile(n):
                qw = tasks[n][3]
                P.op("pe", lambda e, n=n, qw=qw: e.matmul(banks[CB][:, 0:qw], lhsT=restneg, rhs=Lb[n % NL][:, 0:qw], start=False, stop=True, skip_group_check=True),
                     reads=[("L", n % NL), "cmt"], writes=[B(CB)])
            n = s - 2
            if 0 <= n < NTK:
                qw = tasks[n][3]
                P.op("pe", lambda e, n=n, st=first_of_tile(n), qw=qw: e.matmul(banks[CB][:, 0:qw], lhsT=trineg, rhs=Lb[n % NL][:, 0:qw], start=st, stop=True, skip_group_check=True),
                     reads=[("L", n % NL), "cmt"], writes=[B(CB)])
            n = s - 3
            if 0 <= n < NTK:
                qw = tasks[n][3]
                P.op("dve", lambda e, n=n, qw=qw: e.tensor_tensor(out=wb[n % NW][:, 0:qw], in0=eb[n % NE][:, 0:qw], in1=E2b[n % NE2][:, 0:qw], op=ALU.mult),
                     reads=[("e", n % NE), ("E2", n % NE2)], writes=[("w", n % NW)])
            n = s - 4
            if 0 <= n < NTK:
                i, kb, c0, qw = tasks[n]
                ob = OB[i % 2]
                P.op("pe", lambda e, n=n, kb=kb, ob=ob, st=first_of_tile(n), sp=last_of_tile(n), qw=qw: e.matmul(banks[ob][:, 0:qw], lhsT=V[:, kb, :], rhs=wb[n % NW][:, 0:qw], start=st, stop=sp),
                     reads=[("V", kb // 4), ("w", n % NW)], writes=[B(ob)])
                if last_of_tile(n):
                    qs = i % 3
                    P.op("dve", lambda e, ob=ob, qs=qs, c0=c0, qw=qw: e.tensor_tensor(out=yb[qs][:, 0:qw], in0=banks[ob][:, 0:qw], in1=sgb[qs][:, c0:c0 + qw], op=ALU.mult),
                         reads=[B(ob), ("sg", qs)], writes=[("yb", qs)])
                    row0 = yrow_b + hl * 128
                    col0 = 0 if qw == 128 else ycol(i)
                    out_dmas.append(P.dma("sp", ydst[row0:row0 + 128, col0:col0 + qw], yb[qs][:, 0:qw], reads=[("yb", qs)], chan=("yb", qs)))
            if not side and pend:
                ti = pend.pop(0)
                ensure_loaded(ti)
                side.extend(inproj_groups(ti))
            if side:
                side.pop(0)()
            lru_hook[0](2 if s % 6 == 0 else 1)
        while side or pend:
            if not side:
                ti = pend.pop(0)
                ensure_loaded(ti)
                side.extend(inproj_groups(ti))
            side.pop(0)()

    Wl = sb("Wl", [128, 8, 1024], BF16)
    Wa = sb("Wa", [128, 4, 128], BF16)
    Wx = sb("Wx", [128, 4, 128], BF16)
    cw = sb("cw", [128, 4, 4])
    vc = sb("vc", [128, 4, 4])
    nb = sb("nb", [128, 4, 2])
    cch = sb("cch", [128, 4, 3])
    axbuf = [sb("axbuf%d" % c, [128, 515]) for c in range(4)]
    hst = sb("hst", [128, 4])
    T = {}
    for nm, dt in [("xc", F32), ("xcb", BF16), ("er", F32), ("ei", F32), ("a", F32), ("a2", F32), ("u", F32), ("h", F32), ("eg", F32), ("yl", BF16)]:
        T[nm] = [sb("l_%s%d" % (nm, k), [128, 512], dt) for k in range(2)]
    xtl = [sb("xtl%d" % i, [128, 8, 512], BF16) for i in range(2)]
    xlcnt = [0]

    def load_xl(i):
        s = xlcnt[0] % 2
        xlcnt[0] += 1
        P.dma("pool", xtl[s], xT_v[:, :, i * 512:(i + 1) * 512], writes=[("xtl", s)], chan=("xtl", s))
        return s


    def lru_pass(grp):
        yield
        P.dma("pool", Wl, w_lru[grp].rearrange("(c p) n -> p c n", p=128), writes=["Wl"], chan="Wl")
        P.dma("pool", Wa, wga[grp].rearrange("n d e -> d n e"), writes=["Wa"], chan="Wa")
        P.dma("pool", Wx, wgx[grp].rearrange("n d e -> d n e"), writes=["Wx"], chan="Wx")
        P.dma("sp", cw, conv_w[grp].rearrange("(c p) k -> p c k", p=128), writes=["cw"], chan="cw")
        P.dma("sp", vc, vecs[grp].rearrange("(c p) k -> p c k", p=128), writes=["vc"], chan="vc")
        P.op("dve", lambda e: e.tensor_scalar(out=nb, in0=vc[:, :, 1:3], scalar1=-1.0, scalar2=None, op0=ALU.mult), reads=["vc"], writes=["nb"])
        P.op("act", lambda e: e.activation(out=cch[:, :, 2:3], in_=vc[:, :, 3:4], func=AF.Exp, scale=-1.0), reads=["vc"], writes=["cch"])
        P.op("act", lambda e: e.activation(out=cch[:, :, 2:3], in_=cch[:, :, 2:3], func=AF.Ln, bias=1.0), reads=["cch"], writes=["cch"])
        P.op("dve", lambda e: e.tensor_scalar(out=cch[:, :, 0:1], in0=cch[:, :, 2:3], scalar1=-8.0, scalar2=None, op0=ALU.mult), reads=["cch"], writes=["cch"])
        P.op("dve", lambda e: e.tensor_scalar(out=cch[:, :, 1:2], in0=cch[:, :, 2:3], scalar1=-16.0, scalar2=None, op0=ALU.mult), reads=["cch"], writes=["cch"])
        for c in range(4):
            P.op("pool", lambda e, c=c: e.memset(axbuf[c], 0.0), writes=[("axbuf", c)])
        P.op("pool", lambda e: e.memset(hst, 0.0), writes=[("hst", c) for c in range(4)])
        yield
        xslot = {0: load_xl(0)}
        first_y = OWN - 1 if HALO else OWN

        def chunk_gen(i, c, k):
            if c == 0 and i + 1 < NTW:
                xslot[i + 1] = load_xl(i + 1)
            xs = xslot[i]
            x = xtl[xs]
            need_y = i >= first_y
            is_ctx = i < OWN
            bax, bag, br, bi = 4, 7, 4, 7
            R = lambda nm: (nm, k)
            ab = axbuf[c]
            xc, xcb, er, ei, a, a2, u, h, eg, yl = [T[nm][k] for nm in ("xc", "xcb", "er", "ei", "a", "a2", "u", "h", "eg", "yl")]
            for kc in range(8):
                P.op("pe", lambda e, kc=kc: e.matmul(banks[bax], lhsT=Wl[:, kc, c * 128:(c + 1) * 128], rhs=x[:, kc, :], start=(kc == 0), stop=(kc == 7)),
                     reads=[("xtl", xs), "Wl"], writes=[B(bax)])
            yield
            P.op("dve", lambda e: e.tensor_copy(out=ab[:, 3:515], in_=banks[bax]), reads=[B(bax)], writes=[("axbuf", c)])
            yield
            if POOL3:
                P.op("pool", lambda e: e.tensor_scalar(out=xc, in0=ab[:, 3:515], scalar1=cw[:, c, 3:4], scalar2=vc[:, c, 0:1], op0=ALU.mult, op1=ALU.add),
                     reads=[("axbuf", c), "cw", "vc"], writes=[R("xc")])
            else:
                P.op("dve", lambda e: e.tensor_scalar(out=xc, in0=ab[:, 3:515], scalar1=cw[:, c, 3:4], scalar2=vc[:, c, 0:1], op0=ALU.mult, op1=ALU.add),
                     reads=[("axbuf", c), "cw", "vc"], writes=[R("xc")])
            yield
            for tap in (2, 1, 0):
                P.op("dve", lambda e, tap=tap: e.scalar_tensor_tensor(out=xc, in0=ab[:, tap:tap + 512], scalar=cw[:, c, tap:tap + 1], in1=xc, op0=ALU.mult, op1=ALU.add),
                     reads=[("axbuf", c), "cw", R("xc")], writes=[R("xc")])
                yield
            if POOL3:
                P.op("pool", lambda e: e.tensor_copy(out=xcb, in_=xc), reads=[R("xc")], writes=[R("xcb")])
                P.op("pool", lambda e: e.tensor_copy(out=ab[:, 0:3], in_=ab[:, 512:515]), reads=[("axbuf", c)], writes=[("axbuf", c)])
            else:
                P.op("dve", lambda e: e.tensor_copy(out=xcb, in_=xc), reads=[R("xc")], writes=[R("xcb")])
                P.op("dve", lambda e: e.tensor_copy(out=ab[:, 0:3], in_=ab[:, 512:515]), reads=[("axbuf", c)], writes=[("axbuf", c)])
            yield
            P.op("pe", lambda e: e.matmul(banks[br], lhsT=Wa[:, c, :], rhs=xcb, start=True, stop=True), reads=["Wa", R("xcb")], writes=[B(br)])
            P.op("pe", lambda e: e.matmul(banks[bi], lhsT=Wx[:, c, :], rhs=xcb, start=True, stop=True), reads=["Wx", R("xcb")], writes=[B(bi)])
            yield
            P.op("act", lambda e: e.activation(out=er, in_=banks[br], func=AF.Exp, scale=-1.0, bias=nb[:, c, 0:1]), reads=[B(br), "nb"], writes=[R("er")])
            yield
            P.op("act", lambda e: e.activation(out=ei, in_=banks[bi], func=AF.Exp, scale=-1.0, bias=nb[:, c, 1:2]), reads=[B(bi), "nb"], writes=[R("ei")])
            yield
            P.op(POOLENG, lambda e: e.tensor_scalar_add(out=er, in0=er, scalar1=1.0), reads=[R("er")], writes=[R("er")])
            yield
            P.op("dve", lambda e: e.reciprocal(out=er, in_=er), reads=[R("er")], writes=[R("er")])
            yield
            P.op("act", lambda e: e.activation(out=a, in_=er, func=AF.Exp, scale=cch[:, c, 0:1]), reads=[R("er"), "cch"], writes=[R("a")])
            yield
            P.op("dve", lambda e: e.tensor_tensor(out=a2, in0=a, in1=a, op=ALU.mult), reads=[R("a")], writes=[R("a2")])
            yield
            P.op("act", lambda e: e.activation(out=a2, in_=a2, func=AF.Ln, scale=-1.0, bias=1.0), reads=[R("a2")], writes=[R("a2")])
            yield
            P.op("act", lambda e: e.activation(out=a2, in_=a2, func=AF.Exp, scale=0.5), reads=[R("a2")], writes=[R("a2")])
            yield
            P.op(POOLENG, lambda e: e.tensor_scalar_add(out=ei, in0=ei, scalar1=1.0), reads=[R("ei")], writes=[R("ei")])
            yield
            P.op(POOLENG, lambda e: e.tensor_tensor(out=u, in0=xc, in1=a2, op=ALU.mult), reads=[R("xc"), R("a2")], writes=[R("u")])
            yield
            P.op("dve", lambda e: e.reciprocal(out=ei, in_=ei), reads=[R("ei")], writes=[R("ei")])
            yield
            if is_ctx:
                P.op("dve", lambda e: e.scalar_tensor_tensor(out=u, in0=u, scalar=ctxf[:, 0:1], in1=ei, op0=ALU.mult, op1=ALU.mult), reads=[R("u"), R("ei"), "ctxf"], writes=[R("u")])
            else:
                P.op("dve", lambda e: e.tensor_tensor(out=u, in0=u, in1=ei, op=ALU.mult), reads=[R("u"), R("ei")], writes=[R("u")])
            yield
            P.op("dve", lambda e: e.tensor_tensor_scan(out=h, data0=a, data1=u, initial=hst[:, c:c + 1], op0=ALU.mult, op1=ALU.add),
                 reads=[R("a"), R("u"), ("hst", c)], writes=[R("h")])
            yield
            P.op("dve", lambda e: e.tensor_copy(out=hst[:, c:c + 1], in_=h[:, 511:512]), reads=[R("h")], writes=[("hst", c)])
            yield
            if need_y:
                for kc in range(8):
                    P.op("pe", lambda e, kc=kc: e.matmul(banks[bag], lhsT=Wl[:, kc, 512 + c * 128:512 + (c + 1) * 128], rhs=x[:, kc, :], start=(kc == 0), stop=(kc == 7)),
                         reads=[("xtl", xs), "Wl"], writes=[B(bag)])
                yield
                P.op("act", lambda e: e.activation(out=eg, in_=banks[bag], func=AF.Exp, scale=-1.0), reads=[B(bag)], writes=[R("eg")])
                yield
                P.op(POOLENG, lambda e: e.tensor_scalar_add(out=eg, in0=eg, scalar1=1.0), reads=[R("eg")], writes=[R("eg")])
                yield
                P.op("dve", lambda e: e.reciprocal(out=eg, in_=eg), reads=[R("eg")], writes=[R("eg")])
                yield
                P.op("dve", lambda e: e.tensor_tensor(out=eg, in0=banks[bag], in1=eg, op=ALU.mult), reads=[R("eg"), B(bag)], writes=[R("eg")])
                yield
                P.op("dve", lambda e: e.tensor_tensor(out=yl, in0=h, in1=eg, op=ALU.mult), reads=[R("eg"), R("h")], writes=[R("yl")])
                yield
                row0 = yrow_a + grp * 512 + c * 128
                if i >= OWN:
                    out_dmas.append(P.dma("sp", ydst[row0:row0 + 128, ycol(i):ycol(i) + 512], yl, reads=[R("yl")], chan=("yl", k)))
                else:
                    out_dmas.append(P.dma("sp", ydst[row0:row0 + 128, 0:128], yl[:, 384:512], reads=[R("yl")], chan=("yl", k)))
                yield

        it = 0
        for i in range(NTW):
            for c in range(4):
                yield from chunk_gen(i, c, it % 2)
                it += 1

    def lru_master():
        for grp in range(NG):
            yield from lru_pass(grp)

    lru_gen = lru_master()
    lru_done = [False]

    def lru_advance(nsteps):
        for _ in range(nsteps):
            if lru_done[0]:
                return
            if next(lru_gen, "DONE") == "DONE":
                lru_done[0] = True

    lru_hook[0] = lru_advance
    for hl in range(NH):
        sb_pass(hl)
    while not lru_done[0]:
        lru_advance(64)
    return out_dmas


def build_F(S):
    nc = bass.Bass("TRN2", target_bir_lowering=False)
    NB = S // 128
    HALF = S // 2
    TOKB = HALF + 128
    dr = lambda name, shape, dt=F32, kind="ExternalInput": nc.dram_tensor(name, shape, dt, kind=kind).ap()
    xT = dr("xT", [1024, S])
    w_sb = dr("w_sb", [8, 1024, 512])
    w_lru = dr("w_lru", [2, 1024, 1024])
    conv_w = dr("conv_w", [2, 512, 4])
    vecs = dr("vecs", [2, 512, 4])
    wga = dr("wga", [2, 4, 128, 128])
    wgx = dr("wgx", [2, 4, 128, 128])
    cmask = dr("cmask", [128, 2048])
    cmats = dr("cmats", [128, 384])
    kmask = dr("kmask", [128, S // 512])
    ctxf = dr("ctxf", [128, 1])
    x_in = dr("x_in", [TOKB, 1024])
    w_o0 = dr("w_o0", [2048, 1024])
    w_i1 = dr("w_i1", [1024, 2688])
    w_o1 = dr("w_o1", [1024, 1024])
    lnp = dr("lnp", [128, 4096])
    sinks = dr("sinks", [128, 16])
    abias_d = dr("abias", [128, 4096])
    abias0_d = dr("abias0", [128, 4096])
    ident_d = dr("ident", [128, 128])
    out = dr("out", [HALF, 1024], F32, "ExternalOutput")
    yint = nc.dram_tensor("yint", [2048, TOKB], BF16).ap()
    NTW = S // 512
    OWN = NTW // 2
    P = Prog(nc)
    cfg = dict(NTW=NTW, own_from=OWN, halo=True, n_heads=8, n_lru_groups=2, ycol=lambda i: 128 + (i - OWN) * 512, yrow_a=0, yrow_b=1024)
    with ExitStack() as stk:
        record_A(nc, P, stk, cfg, xT, w_sb, w_lru, conv_w, vecs, wga, wgx, cmask, cmats, kmask, ctxf, yint)
        P.barrier()
        P.emit()
    P2 = Prog(nc)
    outs = record_B(nc, P2, HALF // 128, yint, x_in, w_o0, w_i1, w_o1, lnp, sinks, abias_d, abias0_d, ident_d, out)
    fin = P2.op("sp", lambda eng: None)
    fin.deps = list(outs)
    P2.emit()
    return nc


R_OLD = 1
R_YOUNG = 1
ALPHA = float(4 ** 0.25)
EPS = 1e-5


def record_B(nc, P, TB, yT_in, x_in, w_o0, w_i1, w_o1, lnp, sinks, abias_d, abias0_d, ident_d, out, ykey=None):
    NBK = TB + 1
    sb = lambda name, shape, dt=F32: nc.alloc_sbuf_tensor(name, shape, dt).ap()
    banks = [nc.alloc_psum_tensor("bankB%d" % i, [128, 512], F32).ap() for i in range(8)]
    B = lambda i: ("bankB", i)

    Wo0 = sb("Wo0", [128, 16, 1024], BF16)
    Wi1 = sb("Wi1", [128, 8, 2688], BF16)
    Wo1 = sb("Wo1", [128, 8, 1024], BF16)
    for h in range(2):
        P.dma("pool", Wo0[:, 8 * h:8 * h + 8, :], w_o0[1024 * h:1024 * (h + 1)].rearrange("(c p) n -> p c n", p=128), writes=["Wo0"], chan=("Wo0", h))
    P.dma("pool", Wi1[:, 0:4, :], w_i1[0:512].rearrange("(c p) n -> p c n", p=128), writes=["Wi1"], chan=("Wi1", 0))
    P.dma("pool", Wi1[:, 4:8, :], w_i1[512:1024].rearrange("(c p) n -> p c n", p=128), writes=["Wi1"], chan=("Wi1", 1))
    P.dma("pool", Wo1, w_o1.rearrange("(c p) n -> p c n", p=128), writes=["Wo1"], chan="Wo1")
    lnb = sb("lnb", [128, 4, 1024])
    P.dma("sp", lnb, lnp.rearrange("p (k n) -> p k n", k=4), writes=["lnb"], chan="lnb")
    snk = sb("snk", [128, 16])
    P.dma("sp", snk, sinks, writes=["snk"], chan="snk")
    abT = sb("abT_s", [128, 16, 2, 128])
    abT0 = sb("abT0_s", [128, 16, 128])
    P.dma("sp", abT, abias_d.rearrange("p (i h q) -> p i h q", i=16, h=2), writes=["abT"], chan="abias")
    P.dma("sp", abT0, abias0_d.rearrange("p (i h q) -> p i h q", i=16, h=2)[:, :, 0, :], writes=["abT0"], chan="abias0")
    cvec = sb("cvec", [128, 16])
    esk = sb("esk", [128, 16])
    P.op("dve", lambda e: e.tensor_scalar_max(out=cvec, in0=snk, scalar1=0.0), reads=["snk"], writes=["cvec"])
    P.op("dve", lambda e: e.tensor_tensor(out=esk, in0=snk, in1=cvec, op=ALU.subtract), reads=["snk", "cvec"], writes=["esk"])
    P.op("act", lambda e: e.activation(out=esk, in_=esk, func=AF.Exp), reads=["esk"], writes=["esk"])
    for i16 in range(16):
        P.op("dve", lambda e, i16=i16: e.tensor_scalar(out=abT[:, i16, :, :], in0=abT[:, i16, :, :], scalar1=cvec[:, i16:i16 + 1], scalar2=None, op0=ALU.subtract), reads=["abT", "cvec"], writes=["abT"])
        P.op("dve", lambda e, i16=i16: e.tensor_scalar(out=abT0[:, i16, :], in0=abT0[:, i16, :], scalar1=cvec[:, i16:i16 + 1], scalar2=None, op0=ALU.subtract), reads=["abT0", "cvec"], writes=["abT0"])
    ident = sb("ident_s", [128, 128], BF16)
    P.dma("pool", ident, ident_d, writes=["identB"], chan="identB")

    yin = [sb("yin%d" % i, [128, 16, 128], BF16) for i in range(2)]
    xin = [sb("xin%d" % i, [128, 1024]) for i in range(2)]
    x1 = [sb("x1_%d" % i, [128, 1024]) for i in range(2)]
    x1b = sb("x1b", [128, 1024], BF16)
    x1T = sb("x1T", [128, 8, 128], BF16)
    qT = [sb("qTB%d" % i, [128, 8, 128], BF16) for i in range(2)]
    kT = sb("kTr", [128, 4, 3, 128], BF16)
    vv = sb("vr", [128, 3, 2, 65], BF16)
    sg = [sb("sgB%d" % i, [128, 1024]) for i in range(2)]
    scb = [sb("scb%d" % i, [128, 2, 512]) for i in range(2)]
    pT = [sb("pT%d" % i, [128, 2, 512], BF16) for i in range(2)]
    st = [sb("st%d" % i, [128, 16]) for i in range(2)]
    y1 = sb("y1", [128, 1024], BF16)
    y1T = sb("y1T", [128, 8, 128], BF16)
    ob = [sb("ob%d" % i, [128, 1024]) for i in range(2)]
    stats = [sb("stats%d" % i, [128, 2, 6]) for i in range(2)]
    mv = [sb("mv%d" % i, [128, 4]) for i in range(2)]
    P.op("pool", lambda e: e.memset(vv, 1.0), writes=[("vv", 0), ("vv", 1), ("vv", 2)])
    out_dmas = []
    yT_v = yT_in.rearrange("(c p) t -> p c t", p=128)

    def layer_norm(buf, key, gi, sid):
        S_, M_ = stats[sid], mv[sid]
        sk, mk = ("stats", sid), ("mv", sid)
        for hh in range(2):
            P.op("dve", lambda e, hh=hh: e.bn_stats(out=S_[:, hh, :], in_=buf[:, hh * 512:(hh + 1) * 512]), reads=[key], writes=[sk])
        yield
        P.op("dve", lambda e: e.bn_aggr(out=M_[:, 0:2], in_=S_.rearrange("p a b -> p (a b)")), reads=[sk], writes=[mk])
        yield
        P.op("dve", lambda e: e.tensor_scalar_add(out=M_[:, 2:3], in0=M_[:, 1:2], scalar1=EPS), reads=[mk], writes=[mk])
        yield
        P.op("act", lambda e: e.activation(out=M_[:, 2:3], in_=M_[:, 2:3], func=AF.Ln), reads=[mk], writes=[mk])
        yield
        P.op("act", lambda e: e.activation(out=M_[:, 2:3], in_=M_[:, 2:3], func=AF.Exp, scale=-0.5), reads=[mk], writes=[mk])
        yield
        P.op("dve", lambda e: e.scalar_tensor_tensor(out=M_[:, 3:4], in0=M_[:, 0:1], scalar=-1.0, in1=M_[:, 2:3], op0=ALU.mult, op1=ALU.mult), reads=[mk], writes=[mk])
        yield
        P.op("act", lambda e: e.activation(out=buf, in_=buf, func=AF.Identity, scale=M_[:, 2:3], bias=M_[:, 3:4]), reads=[key, mk], writes=[key])
        yield
        P.op("pool", lambda e: e.tensor_tensor(out=buf, in0=buf, in1=lnb[:, gi, :], op=ALU.mult), reads=[key, "lnb"], writes=[key])
        yield
        P.op("pool", lambda e: e.tensor_tensor(out=buf, in0=buf, in1=lnb[:, gi + 1, :], op=ALU.add), reads=[key, "lnb"], writes=[key])
        yield

    def load_blk(j):
        ys = j % 2
        extra = list(ykey(j)) if ykey is not None else []
        for q4 in range(4):
            P.dma("sp", yin[ys][:, 4 * q4:4 * q4 + 4, :], yT_v[:, 4 * q4:4 * q4 + 4, j * 128:(j + 1) * 128], reads=extra, writes=[("yin", ys)], chan=("yin", ys, q4))
        P.dma("sp", xin[ys], x_in[j * 128:(j + 1) * 128, :], writes=[("xin", ys)], chan=("xin", ys))

    def blk(j):
        ys = j % 2
        xs = j % 2
        X1 = x1[xs]
        x1k = ("x1", xs)
        if j == 0:
            load_blk(0)
        if j + 1 < NBK:
            load_blk(j + 1)
        for hh in range(2):
            for kc in range(16):
                P.op("pe", lambda e, hh=hh, kc=kc: e.matmul(banks[hh], lhsT=yin[ys][:, kc, :], rhs=Wo0[:, kc, hh * 512:(hh + 1) * 512], start=(kc == 0), stop=(kc == 15)),
                     reads=[("yin", ys), "Wo0"], writes=[B(hh)])
                if kc % 4 == 3:
                    yield
        for hh in range(2):
            P.op("dve", lambda e, hh=hh: e.scalar_tensor_tensor(out=X1[:, hh * 512:(hh + 1) * 512], in0=xin[ys][:, hh * 512:(hh + 1) * 512], scalar=ALPHA, in1=banks[hh], op0=ALU.mult, op1=ALU.add),
                 reads=[("xin", ys), B(hh)], writes=[x1k])
            yield
        yield from layer_norm(X1, x1k, 0, 0)
        P.op("pool", lambda e: e.tensor_copy(out=x1b, in_=X1), reads=[x1k], writes=["x1b"])
        yield
        tb = banks[2].bitcast(BF16)
        for c in range(8):
            P.op("pe", lambda e, c=c: e.transpose(tb[:, c * 128:(c + 1) * 128], x1b[:, c * 128:(c + 1) * 128], ident), reads=["x1b", "identB"], writes=[B(2)])
            if c % 4 == 3:
                yield
        P.op("act", lambda e: e.activation(out=x1T.rearrange("p c t -> p (c t)"), in_=tb, func=AF.Copy), reads=[B(2)], writes=["x1T"])
        yield
        ks = j % 3
        for c2 in range(4):
            for kc in range(8):
                P.op("pe", lambda e, c2=c2, kc=kc: e.matmul(banks[3][:, c2 * 128:(c2 + 1) * 128], lhsT=Wi1[:, kc, 1024 + c2 * 128:1024 + (c2 + 1) * 128], rhs=x1T[:, kc, :], start=(kc == 0), stop=(kc == 7)),
                     reads=["x1T", "Wi1"], writes=[B(3)])
            yield
        for kc in range(8):
            P.op("pe", lambda e, kc=kc: e.matmul(banks[2][:, 0:128], lhsT=x1T[:, kc, :], rhs=Wi1[:, kc, 1536:1664], start=(kc == 0), stop=(kc == 7)),
                 reads=["x1T", "Wi1"], writes=[B(2)])
        yield
        P.op("dve", lambda e: e.tensor_copy(out=kT[:, :, ks, :], in_=banks[3].rearrange("p (c t) -> p c t", c=4)), reads=[B(3)], writes=[("kT", ks)])
        yield
        P.op("dve", lambda e: e.tensor_copy(out=vv[:, ks, :, 0:64], in_=banks[2][:, 0:128].rearrange("p (c d) -> p c d", c=2)), reads=[B(2)], writes=[("vv", ks)])
        yield
        if j == 0:
            yield "S2"
            return
        qs = j % 2
        Q = qT[qs]
        qk = ("qTB", qs)
        SG = sg[qs]
        sgk = ("sgB", qs)
        for c in range(8):
            bq = c // 4
            for kc in range(8):
                P.op("pe", lambda e, c=c, kc=kc, bq=bq: e.matmul(banks[bq][:, (c % 4) * 128:(c % 4 + 1) * 128], lhsT=Wi1[:, kc, c * 128:(c + 1) * 128], rhs=x1T[:, kc, :], start=(kc == 0), stop=(kc == 7)),
                     reads=["x1T", "Wi1"], writes=[B(bq)])
            yield
        for hh in range(2):
            P.op("dve", lambda e, hh=hh: e.tensor_scalar(out=Q[:, 4 * hh:4 * hh + 4, :], in0=banks[hh].rearrange("p (c t) -> p c t", c=4), scalar1=0.125, scalar2=None, op0=ALU.mult),
                 reads=[B(hh)], writes=[qk])
            yield
        for hh in range(2):
            for kc in range(8):
                P.op("pe", lambda e, hh=hh, kc=kc: e.matmul(banks[hh], lhsT=x1T[:, kc, :], rhs=Wi1[:, kc, 1664 + hh * 512:1664 + (hh + 1) * 512], start=(kc == 0), stop=(kc == 7)),
                     reads=["x1T", "Wi1"], writes=[B(hh)])
                if kc % 4 == 3:
                    yield
        for hh in range(2):
            P.op("act", lambda e, hh=hh: e.activation(out=SG[:, hh * 512:(hh + 1) * 512], in_=banks[hh], func=AF.Exp, scale=-1.0), reads=[B(hh)], writes=[sgk])
            yield
        P.op("dve", lambda e: e.tensor_scalar_add(out=SG, in0=SG, scalar1=1.0), reads=[sgk], writes=[sgk])
        yield
        P.op("dve", lambda e: e.reciprocal(out=SG, in_=SG), reads=[sgk], writes=[sgk])
        yield
        for hh in range(2):
            P.op("dve", lambda e, hh=hh: e.tensor_tensor(out=SG[:, hh * 512:(hh + 1) * 512], in0=SG[:, hh * 512:(hh + 1) * 512], in1=banks[hh], op=ALU.mult), reads=[sgk, B(hh)], writes=[sgk])
            yield
        yield "S2"
        ps = (j - 1) % 3
        tb6 = banks[6].bitcast(BF16)

        def grp_gen(gi):
            c, par = gi // 2, gi % 2
            sl = gi % 2
            S_ = scb[sl]
            PT = pT[sl]
            T_ = st[sl]
            ob_ = 6 + sl
            for half, slot in ((0, ps), (1, ks)):
                P.op("pe", lambda e, half=half, slot=slot: e.matmul(banks[4 + half], lhsT=kT[:, 2 * c + par, slot, :], rhs=Q[:, 4 * c:4 * c + 4, :].rearrange("p a q -> p (a q)"), start=True, stop=True),
                     reads=[qk, ("kT", slot)], writes=[B(4 + half)])
            yield
            for half in range(2):
                if half == 0 and j == 1:
                    bsrc, bkey = abT0[:, 4 * gi:4 * gi + 4, :], "abT0"
                else:
                    bsrc, bkey = abT[:, 4 * gi:4 * gi + 4, half, :], "abT"
                P.op("dve", lambda e, half=half, bsrc=bsrc: e.tensor_tensor(out=S_[:, half, :].rearrange("p (a q) -> p a q", a=4), in0=banks[4 + half].rearrange("p (a q) -> p a q", a=4), in1=bsrc, op=ALU.add),
                     reads=[B(4 + half), bkey], writes=[("scb", sl)])
                yield
            for half in range(2):
                P.op("act", lambda e, half=half: e.activation(out=PT[:, half, :], in_=S_[:, half, :], func=AF.Exp), reads=[("scb", sl)], writes=[("pT", sl)])
                yield
            for jj in range(4):
                for half, slot in ((0, ps), (1, ks)):
                    P.op("pe", lambda e, jj=jj, half=half, slot=slot: e.matmul(banks[ob_][:, jj * 65:(jj + 1) * 65], lhsT=PT[:, half, jj * 128:(jj + 1) * 128],
                                                                            rhs=vv[:, slot, c, :], start=(half == 0), stop=(half == 1)),
                         reads=[("pT", sl), ("vv", slot)], writes=[B(ob_)])
            yield
            ov = banks[ob_][:, 0:260].rearrange("p (a d) -> p a d", a=4)
            P.op("dve", lambda e: e.tensor_tensor(out=T_[:, 0:4], in0=ov[:, :, 64], in1=esk[:, 4 * gi:4 * gi + 4], op=ALU.add), reads=[B(ob_), "esk"], writes=[("st", sl)])
            yield
            P.op("dve", lambda e: e.reciprocal(out=T_[:, 4:8], in_=T_[:, 0:4]), reads=[("st", sl)], writes=[("st", sl)])
            yield
            for jj in range(4):
                h = 8 * c + 2 * jj + par
                P.op("dve", lambda e, jj=jj, h=h: e.scalar_tensor_tensor(out=y1[:, h * 64:(h + 1) * 64], in0=ov[:, jj, 0:64], scalar=T_[:, 4 + jj:5 + jj],
                                                                     in1=SG[:, h * 64:(h + 1) * 64], op0=ALU.mult, op1=ALU.mult),
                     reads=[B(ob_), ("st", sl), sgk], writes=["y1"])
                yield

        for gi in range(4):
            yield from grp_gen(gi)
        for c in range(8):
            P.op("pe", lambda e, c=c: e.transpose(tb6[:, c * 128:(c + 1) * 128], y1[:, c * 128:(c + 1) * 128], ident), reads=["y1", "identB"], writes=[B(6)])
            if c % 4 == 3:
                yield
        P.op("act", lambda e: e.activation(out=y1T.rearrange("p c t -> p (c t)"), in_=tb6, func=AF.Copy), reads=[B(6)], writes=["y1T"])
        yield
        for hh in range(2):
            for kc in range(8):
                P.op("pe", lambda e, hh=hh, kc=kc: e.matmul(banks[4 + hh], lhsT=y1T[:, kc, :], rhs=Wo1[:, kc, hh * 512:(hh + 1) * 512], start=(kc == 0), stop=(kc == 7)),
                     reads=["y1T", "Wo1"], writes=[B(4 + hh)])
                if kc % 4 == 3:
                    yield
        os_ = j % 2
        OB = ob[os_]
        obk = ("ob", os_)
        for hh in range(2):
            P.op("dve", lambda e, hh=hh: e.scalar_tensor_tensor(out=OB[:, hh * 512:(hh + 1) * 512], in0=X1[:, hh * 512:(hh + 1) * 512], scalar=ALPHA, in1=banks[4 + hh], op0=ALU.mult, op1=ALU.add),
                 reads=[x1k, B(4 + hh)], writes=[obk])
            yield
        yield from layer_norm(OB, obk, 2, 1)
        out_dmas.append(P.dma("sp", out[(j - 1) * 128:j * 128, :], OB, reads=[obk], chan=("ob", os_)))
        yield

    gens = [blk(j) for j in range(NBK)]
    older = None
    younger = None
    paused = False
    nxt = 0
    DONE = object()
    while True:
        if younger is None and nxt < NBK:
            younger = gens[nxt]
            nxt += 1
            paused = False
        if older is None and younger is None:
            break
        if older is None and paused:
            older, younger, paused = younger, None, False
            continue
        for _ in range(R_OLD):
            if older is not None:
                if next(older, DONE) is DONE:
                    older = None
        for _ in range(R_YOUNG):
            if younger is not None and not paused:
                r = next(younger, DONE)
                if r is DONE:
                    younger = None
                elif r == "S2":
                    paused = True
    return out_dmas


import ml_dtypes
from concourse.bass_utils import run_bass_kernel_spmd

S_FULL = 8192
_CACHE = {}


def _consts_A():
    p = np.arange(128)[:, None]; f = np.arange(512)[None, :]
    cmask = np.zeros((128, 2048), np.float32)
    for r in range(4):
        cmask[:, r * 512:(r + 1) * 512] = np.where(128 * r + p < f, 0.0, -30000.0)
    j = np.arange(128)[:, None]; s = np.arange(128)[None, :]
    ident = np.eye(128, dtype=np.float32)
    trineg = np.where(j >= s, -1.0, 0.0).astype(np.float32)
    restneg = np.where(j < s, -1.0, 0.0).astype(np.float32)
    return cmask, np.concatenate([ident, trineg, restneg], axis=1)


def _consts_B():
    k = np.arange(128)[:, None]; q = np.arange(128)[None, :]
    slopes = np.array([2.0 ** (-8.0 * (i + 1) / 16) for i in range(16)], np.float32)
    ab = np.zeros((128, 16, 2, 128), np.float32)
    for idx in range(16):
        g, jj = idx // 4, idx % 4
        c, par = g // 2, g % 2
        h = 8 * c + 2 * jj + par
        for half in range(2):
            dist = (q - k + 128).astype(np.float32) if half == 0 else (q - k).astype(np.float32)
            valid = (dist >= 0) & (dist < 128)
            ab[:, idx, half, :] = np.where(valid, -slopes[h] * dist, -30000.0)
    ab0 = ab.copy(); ab0[:, :, 0, :] = -30000.0
    return ab.reshape(128, 4096), ab0.reshape(128, 4096)


def _sinks_group_order(s16):
    out = np.zeros(16, np.float32)
    for idx in range(16):
        g, jj = idx // 4, idx % 4
        c, par = g // 2, g % 2
        out[idx] = s16[8 * c + 2 * jj + par]
    return out


def _shared_inputs(inp):
    w_in = inp["e_w_in"][0]
    w_sb = np.zeros((8, 1024, 512), np.float32)
    for h in range(8):
        for k, base in enumerate([2048, 3072, 4096, 5120]):
            w_sb[h, :, k * 128:(k + 1) * 128] = w_in[:, base + h * 128: base + (h + 1) * 128]
    w_lru = np.stack([np.concatenate([w_in[:, 512 * q:512 * q + 512], w_in[:, 1024 + 512 * q:1024 + 512 * q + 512]], axis=1) for q in range(2)])
    conv_w = np.stack([np.ascontiguousarray(inp["e_conv_w"][0][:, 512 * q:512 * q + 512].T) for q in range(2)])
    vecs = np.stack([np.stack([inp["e_conv_b"][0][512 * q:512 * q + 512], inp["e_b_gate_a"][0][512 * q:512 * q + 512],
                               inp["e_b_gate_x"][0][512 * q:512 * q + 512], inp["e_lru_lambda"][0][512 * q:512 * q + 512]], axis=1) for q in range(2)])
    cmask, cmats = _consts_A()
    w = inp["o_w_in"][0]
    q = w[:, 0:1024]; k = w[:, 1024:1152]; v = w[:, 1152:1280]; gg = w[:, 1280:2304]
    z64 = np.zeros((1024, 64), np.float32)
    w_i1 = np.concatenate([q, k[:, 0:64], z64, z64, k[:, 0:64], k[:, 64:128], z64, z64, k[:, 64:128], v, gg], axis=1)
    lnp = np.concatenate([inp["e_ln_g"][0], inp["e_ln_b"][0], inp["o_ln_g"][0], inp["o_ln_b"][0]])[None, :].repeat(128, 0)
    sinks = _sinks_group_order(inp["o_sinks"][0])[None, :].repeat(128, 0)
    ab, ab0 = _consts_B()
    return {"w_sb": w_sb, "w_lru": np.ascontiguousarray(w_lru), "conv_w": np.ascontiguousarray(conv_w), "vecs": np.ascontiguousarray(vecs),
            "wga": np.ascontiguousarray(inp["e_w_gate_a"][0].reshape(2, 4, 128, 128)), "wgx": np.ascontiguousarray(inp["e_w_gate_x"][0].reshape(2, 4, 128, 128)),
            "cmask": cmask, "cmats": cmats, "w_o0": np.ascontiguousarray(inp["e_w_out"][0]), "w_i1": np.ascontiguousarray(w_i1),
            "w_o1": np.ascontiguousarray(inp["o_w_out"][0]), "lnp": np.ascontiguousarray(lnp), "sinks": np.ascontiguousarray(sinks),
            "abias": ab, "ident": np.eye(128, dtype=np.float32)}, ab0


def _core_inputs(inp, shared, ab0, b, g, S):
    HALF = S // 2
    x = inp["x"][b, :S]
    xw = np.zeros((S, 1024), np.float32)
    if g == 0:
        xw[HALF:] = x[:HALF]
    else:
        xw[:] = x
    kmask = np.zeros((128, S // 512), np.float32)
    if g == 0:
        kmask[:, :S // 1024] = -30000.0
    x_in = np.zeros((HALF + 128, 1024), np.float32)
    t0 = g * HALF - 128
    lo = max(t0, 0)
    x_in[lo - t0:] = x[lo:t0 + HALF + 128]
    d = dict(shared)
    d.update({"xT": np.ascontiguousarray(xw.T), "kmask": kmask, "ctxf": np.full((128, 1), float(g), np.float32),
              "x_in": x_in, "abias0": ab0 if g == 0 else shared["abias"]})
    return d


def kernel(**inputs):
    inp = {k: np.asarray(v, dtype=np.float32) for k, v in inputs.items()}
    S = S_FULL
    cores = [(b, g) for b in range(4) for g in range(2)]
    if "F" not in _CACHE:
        _CACHE["F"] = build_F(S)
    shared, ab0 = _shared_inputs(inp)
    res = run_bass_kernel_spmd(_CACHE["F"], [_core_inputs(inp, shared, ab0, b, g, S) for b, g in cores], core_ids=list(range(8)))
    out = np.zeros((4, S, 1024), np.float32)
    for ci, (b, g) in enumerate(cores):
        out[b, g * (S // 2):(g + 1) * (S // 2)] = res.results[ci]["out"]
    return out
```

```python
import numpy as np
from contextlib import ExitStack
import concourse.bass as bass
import concourse.mybir as mybir

F32 = mybir.dt.float32
BF16 = mybir.dt.bfloat16
AF = mybir.ActivationFunctionType
ALU = mybir.AluOpType
AX = mybir.AxisListType

ENGS = ("pe", "act", "dve", "pool", "sp")
SAME_ENGINE_SYNC = True


class Op:
    __slots__ = ("eng", "fn", "deps", "needs_inc", "inc_val", "dma_sem", "dma_val", "idx", "is_dma")

    def __init__(self, eng, fn, is_dma=False, dma_sem=None):
        self.eng = eng
        self.fn = fn
        self.deps = []
        self.needs_inc = False
        self.inc_val = None
        self.is_dma = is_dma
        self.dma_sem = dma_sem
        self.dma_val = None


class Prog:
    def __init__(self, nc):
        self.nc = nc
        self.ops = {e: [] for e in ENGS}
        self.res = {}
        self.stack = ExitStack()
        self.esem = {}
        self.dma_chan = {}
        self.nsem = 0
        self.chan_last = {}

    def new_sem(self, name):
        self.nsem += 1
        return self.stack.enter_context(self.nc.semaphore(name))

    def chan(self, key):
        if key not in self.dma_chan:
            self.dma_chan[key] = [self.new_sem("dc%d" % len(self.dma_chan)), 0]
        return self.dma_chan[key]

    def _rec(self, o, reads, writes):
        deps = []
        for r in reads:
            st = self.res.setdefault(r, [None, []])
            if st[0] is not None:
                deps.append(st[0])
        for w in writes:
            st = self.res.setdefault(w, [None, []])
            if st[0] is not None:
                deps.append(st[0])
            deps.extend(st[1])
        for r in reads:
            self.res[r][1].append(o)
        for w in writes:
            st = self.res[w]
            st[0] = o
            st[1] = []
        seen = set()
        for d in deps:
            if d is o or id(d) in seen:
                continue
            seen.add(id(d))
            if (not d.is_dma) and d.eng == o.eng and (d.eng == "pe" or not SAME_ENGINE_SYNC):
                continue
            o.deps.append(d)
            if not d.is_dma:
                d.needs_inc = True
        self.ops[o.eng].append(o)
        return o

    def op(self, eng, fn, reads=(), writes=()):
        return self._rec(Op(eng, fn), reads, writes)

    def dma(self, eng, out, in_, reads=(), writes=(), chan=None, **kw):
        c = self.chan(chan)
        c[1] += 16
        o = Op(eng, lambda e, out=out, in_=in_, kw=kw: e.dma_start(out=out, in_=in_, **kw), is_dma=True, dma_sem=c[0])
        o.dma_val = c[1]
        self.chan_last[chan] = o
        return self._rec(o, reads, writes)

    def barrier(self):
        lasts = []
        for e in ENGS:
            for o in reversed(self.ops[e]):
                if not o.is_dma:
                    o.needs_inc = True
                    lasts.append(o)
                    break
        dmas = list(self.chan_last.values())
        for e in ENGS:
            b = Op(e, lambda eng: None)
            b.deps = [o for o in lasts if o.eng != e] + dmas
            self.ops[e].append(b)

    def emit(self):
        nc = self.nc
        for e in ENGS:
            self.esem[e] = self.new_sem("es_" + e)
            cnt = 0
            for o in self.ops[e]:
                if o.needs_inc and not o.is_dma:
                    cnt += 1
                    o.inc_val = cnt
        esem = self.esem

        def run(e, engobj):
            waited = {}
            for o in self.ops[e]:
                for d in o.deps:
                    if d.is_dma:
                        sem, val = d.dma_sem, d.dma_val
                    else:
                        sem, val = esem[d.eng], d.inc_val
                    k = id(sem)
                    if waited.get(k, 0) < val:
                        engobj.wait_ge(sem, val)
                        waited[k] = val
                inst = o.fn(engobj)
                if o.is_dma:
                    inst.then_inc(o.dma_sem, 16)
                elif o.needs_inc:
                    inst.then_inc(esem[e], 1)

        with nc.Block() as block:
            @block.tensor
            def _(eng):
                run("pe", eng)

            @block.scalar
            def _(eng):
                run("act", eng)

            @block.vector
            def _(eng):
                run("dve", eng)

            @block.gpsimd
            def _(eng):
                run("pool", eng)

            @block.sync
            def _(eng):
                run("sp", eng)
        self.stack.close()

POOLENG = 'dve'
LWIN = 3
POOL3 = False


def interleave(gens, window=2):
    gens = list(gens)
    active = []
    nxt = 0
    DONE = object()
    while active or nxt < len(gens):
        while len(active) < window and nxt < len(gens):
            active.append(gens[nxt])
            nxt += 1
        for g in list(active):
            if next(g, DONE) is DONE:
                assert active[0] is g, "generators must finish in admission order (slot reuse safety)"
                active.remove(g)


def record_A(nc, P, stk, cfg, xT, w_sb, w_lru, conv_w, vecs, wga, wgx, cmask, cmats, kmask_d, ctxf_d, ydst):
    NTW, OWN, HALO = cfg["NTW"], cfg["own_from"], cfg["halo"]
    NH, NG = cfg["n_heads"], cfg["n_lru_groups"]
    ycol = cfg["ycol"]
    yrow_a, yrow_b = cfg["yrow_a"], cfg["yrow_b"]
    S = NTW * 512
    NB = S // 128

    def sb(name, shape, dt=F32):
        return stk.enter_context(nc.sbuf_tensor(name, shape, dt)).ap()
    banks = [stk.enter_context(nc.psum_tensor("bank%d" % i, [128, 512], F32)).ap() for i in range(8)]
    B = lambda i: ("bank", i)

    cm = sb("cm", [128, 2048], BF16)
    cmt = sb("cmt", [128, 384], BF16)
    P.dma("pool", cm, cmask, writes=["cm"], chan="cm")
    P.dma("pool", cmt, cmats, writes=["cmt"], chan="cmt")
    ident = cmt[:, 0:128]
    trineg = cmt[:, 128:256]
    restneg = cmt[:, 256:384]
    kmask = sb("kmask_s", [128, NTW])
    P.dma("sp", kmask, kmask_d, writes=["kmask"], chan="kmask")
    ctxf = sb("ctxf_s", [128, 1])
    P.dma("sp", ctxf, ctxf_d, writes=["ctxf"], chan="ctxf")

    out_dmas = []
    lru_hook = [None]
    xT_v = xT.rearrange("(c p) t -> p c t", p=128)
    NXS = 4
    xts = [sb("xt%d" % i, [128, 8, 512], BF16) for i in range(NXS)]
    xcnt = [0]

    def load_x(i):
        s = xcnt[0] % NXS
        xcnt[0] += 1
        P.dma("pool", xts[s], xT_v[:, :, i * 512:(i + 1) * 512], writes=[("xt", s)], chan=("xt", s))
        return s

    Wt = [sb("Wt%d" % i, [128, 8, 512], BF16) for i in range(2)]
    kT = sb("kT0", [128, S], BF16)
    V = sb("V0", [128, NB, 128], BF16)
    qTb = [sb("qT%d" % i, [128, 512], BF16) for i in range(3)]
    sgb = [sb("sg%d" % i, [128, 512], F32) for i in range(3)]
    NE, NL, NE2, NW = 4, 4, 3, 3
    eb = [sb("e%d" % i, [128, 512], F32) for i in range(NE)]
    Lb = [sb("L%d" % i, [128, 512], BF16) for i in range(NL)]
    E2b = [sb("E2%d" % i, [128, 512], F32) for i in range(NE2)]
    wb = [sb("w%d" % i, [128, 512], BF16) for i in range(NW)]
    yb = [sb("yb%d" % i, [128, 512], BF16) for i in range(3)]
    vTs = sb("vTs", [128, 512], BF16)
    SCALE = 1.0 / np.sqrt(128.0)
    ZB = [0, 1]
    CB = 2
    OB = [3, 3]
    IB = [5, 6]
    ibc = [0]

    def next_ib():
        b = IB[ibc[0] % len(IB)]
        ibc[0] += 1
        return b

    def sb_pass(hl):
        ws = hl % 2
        W = Wt[ws]
        P.dma("pool", W, w_sb[hl].rearrange("(c p) n -> p c n", p=128), writes=[("W", ws)], chan=("W", ws))
        xslot = {}
        first_q = OWN - 1 if HALO else OWN
        cons = ([OWN - 1] + list(range(OWN - 2, -1, -1)) if HALO else []) + list(range(OWN, NTW))
        cpos = {t: k for k, t in enumerate(cons)}
        lptr = [0]

        def ensure_loaded(t):
            while lptr[0] <= min(cpos[t] + 1, len(cons) - 1):
                tt = cons[lptr[0]]
                xslot[tt] = load_x(tt)
                lptr[0] += 1

        def inproj_groups(i):
            xs = xslot[i]
            x = xts[xs]
            qs = i % 3

            def g_q():
                b = next_ib()
                for c in range(8):
                    P.op("pe", lambda e, c=c, b=b: e.matmul(banks[b], lhsT=W[:, c, 0:128], rhs=x[:, c, :], start=(c == 0), stop=(c == 7)),
                         reads=[("xt", xs), ("W", ws)], writes=[B(b)])
                P.op("dve", lambda e, b=b: e.tensor_scalar(out=qTb[qs], in0=banks[b], scalar1=float(SCALE), scalar2=None, op0=ALU.mult),
                     reads=[B(b)], writes=[("qT", qs)])

            def g_k():
                b = next_ib()
                for c in range(8):
                    P.op("pe", lambda e, c=c, b=b: e.matmul(banks[b], lhsT=W[:, c, 128:256], rhs=x[:, c, :], start=(c == 0), stop=(c == 7)),
                         reads=[("xt", xs), ("W", ws)], writes=[B(b)])
                P.op("dve", lambda e, b=b: e.tensor_copy(out=kT[:, i * 512:(i + 1) * 512], in_=banks[b]),
                     reads=[B(b)], writes=[("kT", i)])

            def g_g():
                b = next_ib()
                S_ = sgb[qs]
                for c in range(8):
                    P.op("pe", lambda e, c=c, b=b: e.matmul(banks[b], lhsT=W[:, c, 384:512], rhs=x[:, c, :], start=(c == 0), stop=(c == 7)),
                         reads=[("xt", xs), ("W", ws)], writes=[B(b)])
                P.op("act", lambda e, b=b: e.activation(out=S_, in_=banks[b], func=AF.Exp, scale=-1.0), reads=[B(b)], writes=[("sg", qs)])
                P.op("dve", lambda e: e.tensor_scalar_add(out=S_, in0=S_, scalar1=1.0), reads=[("sg", qs)], writes=[("sg", qs)])
                P.op("dve", lambda e: e.reciprocal(out=S_, in_=S_), reads=[("sg", qs)], writes=[("sg", qs)])
                P.op("dve", lambda e, b=b: e.tensor_tensor(out=S_, in0=S_, in1=banks[b], op=ALU.mult), reads=[("sg", qs), B(b)], writes=[("sg", qs)])

            def g_v():
                b = next_ib()
                for c in range(8):
                    P.op("pe", lambda e, c=c, b=b: e.matmul(banks[b], lhsT=W[:, c, 256:384], rhs=x[:, c, :], start=(c == 0), stop=(c == 7)),
                         reads=[("xt", xs), ("W", ws)], writes=[B(b)])
                P.op("dve", lambda e, b=b: e.tensor_copy(out=vTs, in_=banks[b]), reads=[B(b)], writes=["vTs"])

            def g_v2():
                b2 = next_ib()
                tb2 = banks[b2].bitcast(BF16)
                for j in range(4):
                    P.op("pe", lambda e, j=j, tb2=tb2: e.transpose(tb2[:, j * 128:(j + 1) * 128], vTs[:, j * 128:(j + 1) * 128], ident), reads=["vTs", "cmt"], writes=[B(b2)])
                P.op("dve", lambda e, tb2=tb2: e.tensor_copy(out=V[:, 4 * i:4 * i + 4, :], in_=tb2[:, 0:512].rearrange("p (j d) -> p j d", j=4)),
                     reads=[B(b2)], writes=[("V", i)])
            if i >= first_q:
                return [g_v, g_q, g_v2, g_k, g_g]
            return [g_v, g_k, g_v2]

        ensure_loaded(cons[0])
        for g in inproj_groups(cons[0]):
            g()
        pend = list(cons[1:OWN + 1]) if HALO else []
        queued = set(pend) | {cons[0]}

        tasks = []
        if HALO:
            ih = OWN - 1
            tasks += [(ih, kb, 384, 128) for kb in reversed(range(4 * ih + 4))]
        for i in range(OWN, NTW):
            tasks += [(i, kb, 0, 512) for kb in reversed(range(4 * i + 4))]
        NTK = len(tasks)
        side = []

        def first_of_tile(n):
            i, kb, c0, qw = tasks[n]
            return kb == 4 * i + 3

        def last_of_tile(n):
            return tasks[n][1] == 0

        for s in range(NTK + 5):
            if s < NTK:
                i, kb, c0, qw = tasks[s]
                if first_of_tile(s):
                    ni = i + 1
                    if ni < NTW and ni > first_q and ni not in queued:
                        queued.add(ni)
                        pend.append(ni)
                zb = ZB[s % 2]
                diag = kb >= 4 * i
                qs = i % 3
                P.op("pe", lambda e, zb=zb, kb=kb, qs=qs, diag=diag, c0=c0, qw=qw: e.matmul(banks[zb][:, 0:qw], lhsT=kT[:, kb * 128:(kb + 1) * 128], rhs=qTb[qs][:, c0:c0 + qw], start=True, stop=not diag),
                     reads=[("kT", kb // 4), ("qT", qs)], writes=[B(zb)])
                if diag:
                    r = kb - 4 * i
                    P.op("pe", lambda e, zb=zb, r=r, c0=c0, qw=qw: e.matmul(banks[zb][:, 0:qw], lhsT=ident, rhs=cm[:, r * 512 + c0:r * 512 + c0 + qw], start=False, stop=True),
                         reads=["cm", "cmt"], writes=[B(zb)])
            n = s - 3
            if 0 <= n < NTK:
                qw = tasks[n][3]
                P.op("act", lambda e, n=n, qw=qw: e.activation(out=E2b[n % NE2][:, 0:qw], in_=banks[CB][:, 0:qw], func=AF.Exp),
                     reads=[B(CB)], writes=[("E2", n % NE2)])
            n = s - 1
            if 0 <= n < NTK:
                zb = ZB[n % 2]
                kb, qw = tasks[n][1], tasks[n][3]
                P.op("act", lambda e, n=n, zb=zb, kb=kb, qw=qw: e.activation(out=eb[n % NE][:, 0:qw], in_=banks[zb][:, 0:qw], func=AF.Exp, bias=kmask[:, kb // 4:kb // 4 + 1]),
                     reads=[B(zb), "kmask"], writes=[("e", n % NE)])
                P.op("act", lambda e, n=n, qw=qw: e.activation(out=Lb[n % NL][:, 0:qw], in_=eb[n % NE][:, 0:qw], func=AF.Ln, bias=1.0),
                     reads=[("e", n % NE)], writes=[("L", n % NL)])
            n = s - 3
            if 0 <= n < NTK and not last_of_tile(n):
                qw = tasks[n][3]
                P.op("pe", lambda e, n=n, qw=qw: e.matmul(banks[CB][:, 0:qw], lhsT=restneg, rhs=Lb[n % NL][:, 0:qw], start=False, stop=True, skip_group_check=True),
                     reads=[("L", n % NL), "cmt"], writes=[B(CB)])
            n = s - 2
            if 0 <= n < NTK:
                qw = tasks[n][3]
                P.op("pe", lambda e, n=n, st=first_of_tile(n), qw=qw: e.matmul(banks[CB][:, 0:qw], lhsT=trineg, rhs=Lb[n % NL][:, 0:qw], start=st, stop=True, skip_group_check=True),
                     reads=[("L", n % NL), "cmt"], writes=[B(CB)])
            n = s - 3
            if 0 <= n < NTK:
                qw = tasks[n][3]
                P.op("dve", lambda e, n=n, qw=qw: e.tensor_tensor(out=wb[n % NW][:, 0:qw], in0=eb[n % NE][:, 0:qw], in1=E2b[n % NE2][:, 0:qw], op=ALU.mult),
                     reads=[("e", n % NE), ("E2", n % NE2)], writes=[("w", n % NW)])
            n = s - 4
            if 0 <= n < NTK:
                i, kb, c0, qw = tasks[n]
                ob = OB[i % 2]
                P.op("pe", lambda e, n=n, kb=kb, ob=ob, st=first_of_tile(n), sp=last_of_tile(n), qw=qw: e.matmul(banks[ob][:, 0:qw], lhsT=V[:, kb, :], rhs=wb[n % NW][:, 0:qw], start=st, stop=sp),
                     reads=[("V", kb // 4), ("w", n % NW)], writes=[B(ob)])
                if last_of_tile(n):
                    qs = i % 3
                    P.op("dve", lambda e, ob=ob, qs=qs, c0=c0, qw=qw: e.tensor_tensor(out=yb[qs][:, 0:qw], in0=banks[ob][:, 0:qw], in1=sgb[qs][:, c0:c0 + qw], op=ALU.mult),
                         reads=[B(ob), ("sg", qs)], writes=[("yb", qs)])
                    row0 = yrow_b + hl * 128
                    col0 = 0 if qw == 128 else ycol(i)
                    out_dmas.append(P.dma("sp", ydst[row0:row0 + 128, col0:col0 + qw], yb[qs][:, 0:qw], reads=[("yb", qs)], chan=("yb", qs)))
            if not side and pend:
                ti = pend.pop(0)
                ensure_loaded(ti)
                side.extend(inproj_groups(ti))
            if side:
                side.pop(0)()
            lru_hook[0](1)
        while side or pend:
            if not side:
                ti = pend.pop(0)
                ensure_loaded(ti)
                side.extend(inproj_groups(ti))
            side.pop(0)()

    Wl = sb("Wl", [128, 8, 1024], BF16)
    Wa = sb("Wa", [128, 4, 128], BF16)
    Wx = sb("Wx", [128, 4, 128], BF16)
    cw = sb("cw", [128, 4, 4])
    vc = sb("vc", [128, 4, 4])
    nb = sb("nb", [128, 4, 2])
    cch = sb("cch", [128, 4, 3])
    axbuf = [sb("axbuf%d" % c, [128, 515]) for c in range(4)]
    hst = sb("hst", [128, 4])
    T = {}
    for nm, dt in [("xc", F32), ("xcb", BF16), ("er", F32), ("ei", F32), ("a", F32), ("a2", F32), ("u", F32), ("h", F32), ("eg", F32), ("yl", BF16)]:
        T[nm] = [sb("l_%s%d" % (nm, k), [128, 512], dt) for k in range(2)]
    xtl = [sb("xtl%d" % i, [128, 8, 512], BF16) for i in range(2)]
    xlcnt = [0]

    def load_xl(i):
        s = xlcnt[0] % 2
        xlcnt[0] += 1
        P.dma("pool", xtl[s], xT_v[:, :, i * 512:(i + 1) * 512], writes=[("xtl", s)], chan=("xtl", s))
        return s


    def lru_pass(grp):
        yield
        P.dma("pool", Wl, w_lru[grp].rearrange("(c p) n -> p c n", p=128), writes=["Wl"], chan="Wl")
        P.dma("pool", Wa, wga[grp].rearrange("n d e -> d n e"), writes=["Wa"], chan="Wa")
        P.dma("pool", Wx, wgx[grp].rearrange("n d e -> d n e"), writes=["Wx"], chan="Wx")
        P.dma("sp", cw, conv_w[grp].rearrange("(c p) k -> p c k", p=128), writes=["cw"], chan="cw")
        P.dma("sp", vc, vecs[grp].rearrange("(c p) k -> p c k", p=128), writes=["vc"], chan="vc")
        P.op("dve", lambda e: e.tensor_scalar(out=nb, in0=vc[:, :, 1:3], scalar1=-1.0, scalar2=None, op0=ALU.mult), reads=["vc"], writes=["nb"])
        P.op("act", lambda e: e.activation(out=cch[:, :, 2:3], in_=vc[:, :, 3:4], func=AF.Exp, scale=-1.0), reads=["vc"], writes=["cch"])
        P.op("act", lambda e: e.activation(out=cch[:, :, 2:3], in_=cch[:, :, 2:3], func=AF.Ln, bias=1.0), reads=["cch"], writes=["cch"])
        P.op("dve", lambda e: e.tensor_scalar(out=cch[:, :, 0:1], in0=cch[:, :, 2:3], scalar1=-8.0, scalar2=None, op0=ALU.mult), reads=["cch"], writes=["cch"])
        P.op("dve", lambda e: e.tensor_scalar(out=cch[:, :, 1:2], in0=cch[:, :, 2:3], scalar1=-16.0, scalar2=None, op0=ALU.mult), reads=["cch"], writes=["cch"])
        for c in range(4):
            P.op("pool", lambda e, c=c: e.memset(axbuf[c], 0.0), writes=[("axbuf", c)])
        P.op("pool", lambda e: e.memset(hst, 0.0), writes=[("hst", c) for c in range(4)])
        yield
        xslot = {0: load_xl(0)}
        first_y = OWN - 1 if HALO else OWN

        def chunk_gen(i, c, k):
            if c == 0 and i + 1 < NTW:
                xslot[i + 1] = load_xl(i + 1)
            xs = xslot[i]
            x = xtl[xs]
            need_y = i >= first_y
            is_ctx = i < OWN
            bax, bag, br, bi = 4, 7, 4, 7
            R = lambda nm: (nm, k)
            ab = axbuf[c]
            xc, xcb, er, ei, a, a2, u, h, eg, yl = [T[nm][k] for nm in ("xc", "xcb", "er", "ei", "a", "a2", "u", "h", "eg", "yl")]
            for kc in range(8):
                P.op("pe", lambda e, kc=kc: e.matmul(banks[bax], lhsT=Wl[:, kc, c * 128:(c + 1) * 128], rhs=x[:, kc, :], start=(kc == 0), stop=(kc == 7)),
                     reads=[("xtl", xs), "Wl"], writes=[B(bax)])
            yield
            P.op("dve", lambda e: e.tensor_copy(out=ab[:, 3:515], in_=banks[bax]), reads=[B(bax)], writes=[("axbuf", c)])
            yield
            if POOL3:
                P.op("pool", lambda e: e.tensor_scalar(out=xc, in0=ab[:, 3:515], scalar1=cw[:, c, 3:4], scalar2=vc[:, c, 0:1], op0=ALU.mult, op1=ALU.add),
                     reads=[("axbuf", c), "cw", "vc"], writes=[R("xc")])
            else:
                P.op("dve", lambda e: e.tensor_scalar(out=xc, in0=ab[:, 3:515], scalar1=cw[:, c, 3:4], scalar2=vc[:, c, 0:1], op0=ALU.mult, op1=ALU.add),
                     reads=[("axbuf", c), "cw", "vc"], writes=[R("xc")])
            yield
            for tap in (2, 1, 0):
                P.op("dve", lambda e, tap=tap: e.scalar_tensor_tensor(out=xc, in0=ab[:, tap:tap + 512], scalar=cw[:, c, tap:tap + 1], in1=xc, op0=ALU.mult, op1=ALU.add),
                     reads=[("axbuf", c), "cw", R("xc")], writes=[R("xc")])
                yield
            if POOL3:
                P.op("pool", lambda e: e.tensor_copy(out=xcb, in_=xc), reads=[R("xc")], writes=[R("xcb")])
                P.op("pool", lambda e: e.tensor_copy(out=ab[:, 0:3], in_=ab[:, 512:515]), reads=[("axbuf", c)], writes=[("axbuf", c)])
            else:
                P.op("dve", lambda e: e.tensor_copy(out=xcb, in_=xc), reads=[R("xc")], writes=[R("xcb")])
                P.op("dve", lambda e: e.tensor_copy(out=ab[:, 0:3], in_=ab[:, 512:515]), reads=[("axbuf", c)], writes=[("axbuf", c)])
            yield
            P.op("pe", lambda e: e.matmul(banks[br], lhsT=Wa[:, c, :], rhs=xcb, start=True, stop=True), reads=["Wa", R("xcb")], writes=[B(br)])
            P.op("pe", lambda e: e.matmul(banks[bi], lhsT=Wx[:, c, :], rhs=xcb, start=True, stop=True), reads=["Wx", R("xcb")], writes=[B(bi)])
            yield
            P.op("act", lambda e: e.activation(out=er, in_=banks[br], func=AF.Exp, scale=-1.0, bias=nb[:, c, 0:1]), reads=[B(br), "nb"], writes=[R("er")])
            yield
            P.op("act", lambda e: e.activation(out=ei, in_=banks[bi], func=AF.Exp, scale=-1.0, bias=nb[:, c, 1:2]), reads=[B(bi), "nb"], writes=[R("ei")])
            yield
            P.op(POOLENG, lambda e: e.tensor_scalar_add(out=er, in0=er, scalar1=1.0), reads=[R("er")], writes=[R("er")])
            yield
            P.op("dve", lambda e: e.reciprocal(out=er, in_=er), reads=[R("er")], writes=[R("er")])
            yield
            P.op("act", lambda e: e.activation(out=a, in_=er, func=AF.Exp, scale=cch[:, c, 0:1]), reads=[R("er"), "cch"], writes=[R("a")])
            yield
            P.op("dve", lambda e: e.tensor_tensor(out=a2, in0=a, in1=a, op=ALU.mult), reads=[R("a")], writes=[R("a2")])
            yield
            P.op("act", lambda e: e.activation(out=a2, in_=a2, func=AF.Ln, scale=-1.0, bias=1.0), reads=[R("a2")], writes=[R("a2")])
            yield
            P.op("act", lambda e: e.activation(out=a2, in_=a2, func=AF.Exp, scale=0.5), reads=[R("a2")], writes=[R("a2")])
            yield
            P.op(POOLENG, lambda e: e.tensor_scalar_add(out=ei, in0=ei, scalar1=1.0), reads=[R("ei")], writes=[R("ei")])
            yield
            P.op(POOLENG, lambda e: e.tensor_tensor(out=u, in0=xc, in1=a2, op=ALU.mult), reads=[R("xc"), R("a2")], writes=[R("u")])
            yield
            P.op("dve", lambda e: e.reciprocal(out=ei, in_=ei), reads=[R("ei")], writes=[R("ei")])
            yield
            if is_ctx:
                P.op("dve", lambda e: e.scalar_tensor_tensor(out=u, in0=u, scalar=ctxf[:, 0:1], in1=ei, op0=ALU.mult, op1=ALU.mult), reads=[R("u"), R("ei"), "ctxf"], writes=[R("u")])
            else:
                P.op("dve", lambda e: e.tensor_tensor(out=u, in0=u, in1=ei, op=ALU.mult), reads=[R("u"), R("ei")], writes=[R("u")])
            yield
            P.op("dve", lambda e: e.tensor_tensor_scan(out=h, data0=a, data1=u, initial=hst[:, c:c + 1], op0=ALU.mult, op1=ALU.add),
                 reads=[R("a"), R("u"), ("hst", c)], writes=[R("h")])
            yield
            P.op("dve", lambda e: e.tensor_copy(out=hst[:, c:c + 1], in_=h[:, 511:512]), reads=[R("h")], writes=[("hst", c)])
            yield
            if need_y:
                for kc in range(8):
                    P.op("pe", lambda e, kc=kc: e.matmul(banks[bag], lhsT=Wl[:, kc, 512 + c * 128:512 + (c + 1) * 128], rhs=x[:, kc, :], start=(kc == 0), stop=(kc == 7)),
                         reads=[("xtl", xs), "Wl"], writes=[B(bag)])
                yield
                P.op("act", lambda e: e.activation(out=eg, in_=banks[bag], func=AF.Exp, scale=-1.0), reads=[B(bag)], writes=[R("eg")])
                yield
                P.op(POOLENG, lambda e: e.tensor_scalar_add(out=eg, in0=eg, scalar1=1.0), reads=[R("eg")], writes=[R("eg")])
                yield
                P.op("dve", lambda e: e.reciprocal(out=eg, in_=eg), reads=[R("eg")], writes=[R("eg")])
                yield
                P.op("dve", lambda e: e.tensor_tensor(out=eg, in0=banks[bag], in1=eg, op=ALU.mult), reads=[R("eg"), B(bag)], writes=[R("eg")])
                yield
                P.op("dve", lambda e: e.tensor_tensor(out=yl, in0=h, in1=eg, op=ALU.mult), reads=[R("eg"), R("h")], writes=[R("yl")])
                yield
                row0 = yrow_a + grp * 512 + c * 128
                if i >= OWN:
                    out_dmas.append(P.dma("sp", ydst[row0:row0 + 128, ycol(i):ycol(i) + 512], yl, reads=[R("yl")], chan=("yl", k)))
                else:
                    out_dmas.append(P.dma("sp", ydst[row0:row0 + 128, 0:128], yl[:, 384:512], reads=[R("yl")], chan=("yl", k)))
                yield

        it = 0
        for i in range(NTW):
            for c in range(4):
                yield from chunk_gen(i, c, it % 2)
                it += 1

    def lru_master():
        for grp in range(NG):
            yield from lru_pass(grp)

    lru_gen = lru_master()
    lru_done = [False]

    def lru_advance(nsteps):
        for _ in range(nsteps):
            if lru_done[0]:
                return
            if next(lru_gen, "DONE") == "DONE":
                lru_done[0] = True

    lru_hook[0] = lru_advance
    for hl in range(NH):
        sb_pass(hl)
    while not lru_done[0]:
        lru_advance(64)
    return out_dmas


def build_F(S):
    nc = bass.Bass("TRN2", target_bir_lowering=False)
    NB = S // 128
    HALF = S // 2
    TOKB = HALF + 128
    dr = lambda name, shape, dt=F32, kind="ExternalInput": nc.dram_tensor(name, shape, dt, kind=kind).ap()
    xT = dr("xT", [1024, S])
    w_sb = dr("w_sb", [8, 1024, 512])
    w_lru = dr("w_lru", [2, 1024, 1024])
    conv_w = dr("conv_w", [2, 512, 4])
    vecs = dr("vecs", [2, 512, 4])
    wga = dr("wga", [2, 4, 128, 128])
    wgx = dr("wgx", [2, 4, 128, 128])
    cmask = dr("cmask", [128, 2048])
    cmats = dr("cmats", [128, 384])
    kmask = dr("kmask", [128, S // 512])
    ctxf = dr("ctxf", [128, 1])
    x_in = dr("x_in", [TOKB, 1024])
    w_o0 = dr("w_o0", [2048, 1024])
    w_i1 = dr("w_i1", [1024, 2688])
    w_o1 = dr("w_o1", [1024, 1024])
    lnp = dr("lnp", [128, 4096])
    sinks = dr("sinks", [128, 16])
    abias_d = dr("abias", [128, 4096])
    abias0_d = dr("abias0", [128, 4096])
    ident_d = dr("ident", [128, 128])
    out = dr("out", [HALF, 1024], F32, "ExternalOutput")
    yint = nc.dram_tensor("yint", [2048, TOKB], BF16).ap()
    NTW = S // 512
    OWN = NTW // 2
    P = Prog(nc)
    cfg = dict(NTW=NTW, own_from=OWN, halo=True, n_heads=8, n_lru_groups=2, ycol=lambda i: 128 + (i - OWN) * 512, yrow_a=0, yrow_b=1024)
    with ExitStack() as stk:
        record_A(nc, P, stk, cfg, xT, w_sb, w_lru, conv_w, vecs, wga, wgx, cmask, cmats, kmask, ctxf, yint)
        P.barrier()
        P.emit()
    P2 = Prog(nc)
    outs = record_B(nc, P2, HALF // 128, yint, x_in, w_o0, w_i1, w_o1, lnp, sinks, abias_d, abias0_d, ident_d, out)
    fin = P2.op("sp", lambda eng: None)
    fin.deps = list(outs)
    P2.emit()
    return nc


R_OLD = 1
R_YOUNG = 1
ALPHA = float(4 ** 0.25)
EPS = 1e-5


def record_B(nc, P, TB, yT_in, x_in, w_o0, w_i1, w_o1, lnp, sinks, abias_d, abias0_d, ident_d, out, ykey=None):
    NBK = TB + 1
    sb = lambda name, shape, dt=F32: nc.alloc_sbuf_tensor(name, shape, dt).ap()
    banks = [nc.alloc_psum_tensor("bankB%d" % i, [128, 512], F32).ap() for i in range(8)]
    B = lambda i: ("bankB", i)

    Wo0 = sb("Wo0", [128, 16, 1024], BF16)
    Wi1 = sb("Wi1", [128, 8, 2688], BF16)
    Wo1 = sb("Wo1", [128, 8, 1024], BF16)
    for h in range(2):
        P.dma("pool", Wo0[:, 8 * h:8 * h + 8, :], w_o0[1024 * h:1024 * (h + 1)].rearrange("(c p) n -> p c n", p=128), writes=["Wo0"], chan=("Wo0", h))
    P.dma("pool", Wi1[:, 0:4, :], w_i1[0:512].rearrange("(c p) n -> p c n", p=128), writes=["Wi1"], chan=("Wi1", 0))
    P.dma("pool", Wi1[:, 4:8, :], w_i1[512:1024].rearrange("(c p) n -> p c n", p=128), writes=["Wi1"], chan=("Wi1", 1))
    P.dma("pool", Wo1, w_o1.rearrange("(c p) n -> p c n", p=128), writes=["Wo1"], chan="Wo1")
    lnb = sb("lnb", [128, 4, 1024])
    P.dma("sp", lnb, lnp.rearrange("p (k n) -> p k n", k=4), writes=["lnb"], chan="lnb")
    snk = sb("snk", [128, 16])
    P.dma("sp", snk, sinks, writes=["snk"], chan="snk")
    abT = sb("abT_s", [128, 16, 2, 128])
    abT0 = sb("abT0_s", [128, 16, 128])
    P.dma("sp", abT, abias_d.rearrange("p (i h q) -> p i h q", i=16, h=2), writes=["abT"], chan="abias")
    P.dma("sp", abT0, abias0_d.rearrange("p (i h q) -> p i h q", i=16, h=2)[:, :, 0, :], writes=["abT0"], chan="abias0")
    cvec = sb("cvec", [128, 16])
    esk = sb("esk", [128, 16])
    P.op("dve", lambda e: e.tensor_scalar_max(out=cvec, in0=snk, scalar1=0.0), reads=["snk"], writes=["cvec"])
    P.op("dve", lambda e: e.tensor_tensor(out=esk, in0=snk, in1=cvec, op=ALU.subtract), reads=["snk", "cvec"], writes=["esk"])
    P.op("act", lambda e: e.activation(out=esk, in_=esk, func=AF.Exp), reads=["esk"], writes=["esk"])
    for i16 in range(16):
        P.op("dve", lambda e, i16=i16: e.tensor_scalar(out=abT[:, i16, :, :], in0=abT[:, i16, :, :], scalar1=cvec[:, i16:i16 + 1], scalar2=None, op0=ALU.subtract), reads=["abT", "cvec"], writes=["abT"])
        P.op("dve", lambda e, i16=i16: e.tensor_scalar(out=abT0[:, i16, :], in0=abT0[:, i16, :], scalar1=cvec[:, i16:i16 + 1], scalar2=None, op0=ALU.subtract), reads=["abT0", "cvec"], writes=["abT0"])
    ident = sb("ident_s", [128, 128], BF16)
    P.dma("pool", ident, ident_d, writes=["identB"], chan="identB")

    yin = [sb("yin%d" % i, [128, 16, 128], BF16) for i in range(2)]
    xin = [sb("xin%d" % i, [128, 1024]) for i in range(2)]
    x1 = [sb("x1_%d" % i, [128, 1024]) for i in range(2)]
    x1b = sb("x1b", [128, 1024], BF16)
    x1T = sb("x1T", [128, 8, 128], BF16)
    qT = [sb("qTB%d" % i, [128, 8, 128], BF16) for i in range(2)]
    kT = sb("kTr", [128, 4, 3, 128], BF16)
    vv = sb("vr", [128, 3, 2, 65], BF16)
    sg = [sb("sgB%d" % i, [128, 1024]) for i in range(2)]
    scb = [sb("scb%d" % i, [128, 2, 512]) for i in range(2)]
    pT = [sb("pT%d" % i, [128, 2, 512], BF16) for i in range(2)]
    st = [sb("st%d" % i, [128, 16]) for i in range(2)]
    y1 = sb("y1", [128, 1024], BF16)
    y1T = sb("y1T", [128, 8, 128], BF16)
    ob = [sb("ob%d" % i, [128, 1024]) for i in range(2)]
    stats = [sb("stats%d" % i, [128, 2, 6]) for i in range(2)]
    mv = [sb("mv%d" % i, [128, 4]) for i in range(2)]
    P.op("pool", lambda e: e.memset(vv, 1.0), writes=[("vv", 0), ("vv", 1), ("vv", 2)])
    out_dmas = []
    yT_v = yT_in.rearrange("(c p) t -> p c t", p=128)

    def layer_norm(buf, key, gi, sid):
        S_, M_ = stats[sid], mv[sid]
        sk, mk = ("stats", sid), ("mv", sid)
        for hh in range(2):
            P.op("dve", lambda e, hh=hh: e.bn_stats(out=S_[:, hh, :], in_=buf[:, hh * 512:(hh + 1) * 512]), reads=[key], writes=[sk])
        yield
        P.op("dve", lambda e: e.bn_aggr(out=M_[:, 0:2], in_=S_.rearrange("p a b -> p (a b)")), reads=[sk], writes=[mk])
        yield
        P.op("dve", lambda e: e.tensor_scalar_add(out=M_[:, 2:3], in0=M_[:, 1:2], scalar1=EPS), reads=[mk], writes=[mk])
        yield
        P.op("act", lambda e: e.activation(out=M_[:, 2:3], in_=M_[:, 2:3], func=AF.Ln), reads=[mk], writes=[mk])
        yield
        P.op("act", lambda e: e.activation(out=M_[:, 2:3], in_=M_[:, 2:3], func=AF.Exp, scale=-0.5), reads=[mk], writes=[mk])
        yield
        P.op("dve", lambda e: e.scalar_tensor_tensor(out=M_[:, 3:4], in0=M_[:, 0:1], scalar=-1.0, in1=M_[:, 2:3], op0=ALU.mult, op1=ALU.mult), reads=[mk], writes=[mk])
        yield
        P.op("act", lambda e: e.activation(out=buf, in_=buf, func=AF.Identity, scale=M_[:, 2:3], bias=M_[:, 3:4]), reads=[key, mk], writes=[key])
        yield
        P.op("pool", lambda e: e.tensor_tensor(out=buf, in0=buf, in1=lnb[:, gi, :], op=ALU.mult), reads=[key, "lnb"], writes=[key])
        yield
        P.op("pool", lambda e: e.tensor_tensor(out=buf, in0=buf, in1=lnb[:, gi + 1, :], op=ALU.add), reads=[key, "lnb"], writes=[key])
        yield

    def load_blk(j):
        ys = j % 2
        extra = list(ykey(j)) if ykey is not None else []
        for q4 in range(4):
            P.dma("sp", yin[ys][:, 4 * q4:4 * q4 + 4, :], yT_v[:, 4 * q4:4 * q4 + 4, j * 128:(j + 1) * 128], reads=extra, writes=[("yin", ys)], chan=("yin", ys, q4))
        P.dma("sp", xin[ys], x_in[j * 128:(j + 1) * 128, :], writes=[("xin", ys)], chan=("xin", ys))

    def blk(j):
        ys = j % 2
        xs = j % 2
        X1 = x1[xs]
        x1k = ("x1", xs)
        if j == 0:
            load_blk(0)
        if j + 1 < NBK:
            load_blk(j + 1)
        for hh in range(2):
            for kc in range(16):
                P.op("pe", lambda e, hh=hh, kc=kc: e.matmul(banks[hh], lhsT=yin[ys][:, kc, :], rhs=Wo0[:, kc, hh * 512:(hh + 1) * 512], start=(kc == 0), stop=(kc == 15)),
                     reads=[("yin", ys), "Wo0"], writes=[B(hh)])
                if kc % 4 == 3:
                    yield
        for hh in range(2):
            P.op("dve", lambda e, hh=hh: e.scalar_tensor_tensor(out=X1[:, hh * 512:(hh + 1) * 512], in0=xin[ys][:, hh * 512:(hh + 1) * 512], scalar=ALPHA, in1=banks[hh], op0=ALU.mult, op1=ALU.add),
                 reads=[("xin", ys), B(hh)], writes=[x1k])
            yield
        yield from layer_norm(X1, x1k, 0, 0)
        P.op("pool", lambda e: e.tensor_copy(out=x1b, in_=X1), reads=[x1k], writes=["x1b"])
        yield
        tb = banks[2].bitcast(BF16)
        for c in range(8):
            P.op("pe", lambda e, c=c: e.transpose(tb[:, c * 128:(c + 1) * 128], x1b[:, c * 128:(c + 1) * 128], ident), reads=["x1b", "identB"], writes=[B(2)])
            if c % 4 == 3:
                yield
        P.op("act", lambda e: e.activation(out=x1T.rearrange("p c t -> p (c t)"), in_=tb, func=AF.Copy), reads=[B(2)], writes=["x1T"])
        yield
        ks = j % 3
        for c2 in range(4):
            for kc in range(8):
                P.op("pe", lambda e, c2=c2, kc=kc: e.matmul(banks[3][:, c2 * 128:(c2 + 1) * 128], lhsT=Wi1[:, kc, 1024 + c2 * 128:1024 + (c2 + 1) * 128], rhs=x1T[:, kc, :], start=(kc == 0), stop=(kc == 7)),
                     reads=["x1T", "Wi1"], writes=[B(3)])
            yield
        for kc in range(8):
            P.op("pe", lambda e, kc=kc: e.matmul(banks[2][:, 0:128], lhsT=x1T[:, kc, :], rhs=Wi1[:, kc, 1536:1664], start=(kc == 0), stop=(kc == 7)),
                 reads=["x1T", "Wi1"], writes=[B(2)])
        yield
        P.op("dve", lambda e: e.tensor_copy(out=kT[:, :, ks, :], in_=banks[3].rearrange("p (c t) -> p c t", c=4)), reads=[B(3)], writes=[("kT", ks)])
        yield
        P.op("dve", lambda e: e.tensor_copy(out=vv[:, ks, :, 0:64], in_=banks[2][:, 0:128].rearrange("p (c d) -> p c d", c=2)), reads=[B(2)], writes=[("vv", ks)])
        yield
        if j == 0:
            yield "S2"
            return
        qs = j % 2
        Q = qT[qs]
        qk = ("qTB", qs)
        SG = sg[qs]
        sgk = ("sgB", qs)
        for c in range(8):
            bq = c // 4
            for kc in range(8):
                P.op("pe", lambda e, c=c, kc=kc, bq=bq: e.matmul(banks[bq][:, (c % 4) * 128:(c % 4 + 1) * 128], lhsT=Wi1[:, kc, c * 128:(c + 1) * 128], rhs=x1T[:, kc, :], start=(kc == 0), stop=(kc == 7)),
                     reads=["x1T", "Wi1"], writes=[B(bq)])
            yield
        for hh in range(2):
            P.op("dve", lambda e, hh=hh: e.tensor_scalar(out=Q[:, 4 * hh:4 * hh + 4, :], in0=banks[hh].rearrange("p (c t) -> p c t", c=4), scalar1=0.125, scalar2=None, op0=ALU.mult),
                 reads=[B(hh)], writes=[qk])
            yield
        for hh in range(2):
            for kc in range(8):
                P.op("pe", lambda e, hh=hh, kc=kc: e.matmul(banks[hh], lhsT=x1T[:, kc, :], rhs=Wi1[:, kc, 1664 + hh * 512:1664 + (hh + 1) * 512], start=(kc == 0), stop=(kc == 7)),
                     reads=["x1T", "Wi1"], writes=[B(hh)])
                if kc % 4 == 3:
                    yield
        for hh in range(2):
            P.op("act", lambda e, hh=hh: e.activation(out=SG[:, hh * 512:(hh + 1) * 512], in_=banks[hh], func=AF.Exp, scale=-1.0), reads=[B(hh)], writes=[sgk])
            yield
        P.op("dve", lambda e: e.tensor_scalar_add(out=SG, in0=SG, scalar1=1.0), reads=[sgk], writes=[sgk])
        yield
        P.op("dve", lambda e: e.reciprocal(out=SG, in_=SG), reads=[sgk], writes=[sgk])
        yield
        for hh in range(2):
            P.op("dve", lambda e, hh=hh: e.tensor_tensor(out=SG[:, hh * 512:(hh + 1) * 512], in0=SG[:, hh * 512:(hh + 1) * 512], in1=banks[hh], op=ALU.mult), reads=[sgk, B(hh)], writes=[sgk])
            yield
        yield "S2"
        ps = (j - 1) % 3
        tb6 = banks[6].bitcast(BF16)

        def grp_gen(gi):
            c, par = gi // 2, gi % 2
            sl = gi % 2
            S_ = scb[sl]
            PT = pT[sl]
            T_ = st[sl]
            ob_ = 6 + sl
            for half, slot in ((0, ps), (1, ks)):
                P.op("pe", lambda e, half=half, slot=slot: e.matmul(banks[4 + half], lhsT=kT[:, 2 * c + par, slot, :], rhs=Q[:, 4 * c:4 * c + 4, :].rearrange("p a q -> p (a q)"), start=True, stop=True),
                     reads=[qk, ("kT", slot)], writes=[B(4 + half)])
            yield
            for half in range(2):
                if half == 0 and j == 1:
                    bsrc, bkey = abT0[:, 4 * gi:4 * gi + 4, :], "abT0"
                else:
                    bsrc, bkey = abT[:, 4 * gi:4 * gi + 4, half, :], "abT"
                P.op("dve", lambda e, half=half, bsrc=bsrc: e.tensor_tensor(out=S_[:, half, :].rearrange("p (a q) -> p a q", a=4), in0=banks[4 + half].rearrange("p (a q) -> p a q", a=4), in1=bsrc, op=ALU.add),
                     reads=[B(4 + half), bkey], writes=[("scb", sl)])
                yield
            for half in range(2):
                P.op("act", lambda e, half=half: e.activation(out=PT[:, half, :], in_=S_[:, half, :], func=AF.Exp), reads=[("scb", sl)], writes=[("pT", sl)])
                yield
            for jj in range(4):
                for half, slot in ((0, ps), (1, ks)):
                    P.op("pe", lambda e, jj=jj, half=half, slot=slot: e.matmul(banks[ob_][:, jj * 65:(jj + 1) * 65], lhsT=PT[:, half, jj * 128:(jj + 1) * 128],
                                                                            rhs=vv[:, slot, c, :], start=(half == 0), stop=(half == 1)),
                         reads=[("pT", sl), ("vv", slot)], writes=[B(ob_)])
            yield
            ov = banks[ob_][:, 0:260].rearrange("p (a d) -> p a d", a=4)
            P.op("dve", lambda e: e.tensor_tensor(out=T_[:, 0:4], in0=ov[:, :, 64], in1=esk[:, 4 * gi:4 * gi + 4], op=ALU.add), reads=[B(ob_), "esk"], writes=[("st", sl)])
            yield
            P.op("dve", lambda e: e.reciprocal(out=T_[:, 4:8], in_=T_[:, 0:4]), reads=[("st", sl)], writes=[("st", sl)])
            yield
            for jj in range(4):
                h = 8 * c + 2 * jj + par
                P.op("dve", lambda e, jj=jj, h=h: e.scalar_tensor_tensor(out=y1[:, h * 64:(h + 1) * 64], in0=ov[:, jj, 0:64], scalar=T_[:, 4 + jj:5 + jj],
                                                                     in1=SG[:, h * 64:(h + 1) * 64], op0=ALU.mult, op1=ALU.mult),
                     reads=[B(ob_), ("st", sl), sgk], writes=["y1"])
                yield

        for gi in range(4):
            yield from grp_gen(gi)
        for c in range(8):
            P.op("pe", lambda e, c=c: e.transpose(tb6[:, c * 128:(c + 1) * 128], y1[:, c * 128:(c + 1) * 128], ident), reads=["y1", "identB"], writes=[B(6)])
            if c % 4 == 3:
                yield
        P.op("act", lambda e: e.activation(out=y1T.rearrange("p c t -> p (c t)"), in_=tb6, func=AF.Copy), reads=[B(6)], writes=["y1T"])
        yield
        for hh in range(2):
            for kc in range(8):
                P.op("pe", lambda e, hh=hh, kc=kc: e.matmul(banks[4 + hh], lhsT=y1T[:, kc, :], rhs=Wo1[:, kc, hh * 512:(hh + 1) * 512], start=(kc == 0), stop=(kc == 7)),
                     reads=["y1T", "Wo1"], writes=[B(4 + hh)])
                if kc % 4 == 3:
                    yield
        os_ = j % 2
        OB = ob[os_]
        obk = ("ob", os_)
        for hh in range(2):
            P.op("dve", lambda e, hh=hh: e.scalar_tensor_tensor(out=OB[:, hh * 512:(hh + 1) * 512], in0=X1[:, hh * 512:(hh + 1) * 512], scalar=ALPHA, in1=banks[4 + hh], op0=ALU.mult, op1=ALU.add),
                 reads=[x1k, B(4 + hh)], writes=[obk])
            yield
        yield from layer_norm(OB, obk, 2, 1)
        out_dmas.append(P.dma("sp", out[(j - 1) * 128:j * 128, :], OB, reads=[obk], chan=("ob", os_)))
        yield

    gens = [blk(j) for j in range(NBK)]
    older = None
    younger = None
    paused = False
    nxt = 0
    DONE = object()
    while True:
        if younger is None and nxt < NBK:
            younger = gens[nxt]
            nxt += 1
            paused = False
        if older is None and younger is None:
            break
        if older is None and paused:
            older, younger, paused = younger, None, False
            continue
        for _ in range(R_OLD):
            if older is not None:
                if next(older, DONE) is DONE:
                    older = None
        for _ in range(R_YOUNG):
            if younger is not None and not paused:
                r = next(younger, DONE)
                if r is DONE:
                    younger = None
                elif r == "S2":
                    paused = True
    return out_dmas


import ml_dtypes
from concourse.bass_utils import run_bass_kernel_spmd

S_FULL = 8192
_CACHE = {}


def _consts_A():
    p = np.arange(128)[:, None]; f = np.arange(512)[None, :]
    cmask = np.zeros((128, 2048), np.float32)
    for r in range(4):
        cmask[:, r * 512:(r + 1) * 512] = np.where(128 * r + p < f, 0.0, -30000.0)
    j = np.arange(128)[:, None]; s = np.arange(128)[None, :]
    ident = np.eye(128, dtype=np.float32)
    trineg = np.where(j >= s, -1.0, 0.0).astype(np.float32)
    restneg = np.where(j < s, -1.0, 0.0).astype(np.float32)
    return cmask, np.concatenate([ident, trineg, restneg], axis=1)


def _consts_B():
    k = np.arange(128)[:, None]; q = np.arange(128)[None, :]
    slopes = np.array([2.0 ** (-8.0 * (i + 1) / 16) for i in range(16)], np.float32)
    ab = np.zeros((128, 16, 2, 128), np.float32)
    for idx in range(16):
        g, jj = idx // 4, idx % 4
        c, par = g // 2, g % 2
        h = 8 * c + 2 * jj + par
        for half in range(2):
            dist = (q - k + 128).astype(np.float32) if half == 0 else (q - k).astype(np.float32)
            valid = (dist >= 0) & (dist < 128)
            ab[:, idx, half, :] = np.where(valid, -slopes[h] * dist, -30000.0)
    ab0 = ab.copy(); ab0[:, :, 0, :] = -30000.0
    return ab.reshape(128, 4096), ab0.reshape(128, 4096)


def _sinks_group_order(s16):
    out = np.zeros(16, np.float32)
    for idx in range(16):
        g, jj = idx // 4, idx % 4
        c, par = g // 2, g % 2
        out[idx] = s16[8 * c + 2 * jj + par]
    return out


def _shared_inputs(inp):
    w_in = inp["e_w_in"][0]
    w_sb = np.zeros((8, 1024, 512), np.float32)
    for h in range(8):
        for k, base in enumerate([2048, 3072, 4096, 5120]):
            w_sb[h, :, k * 128:(k + 1) * 128] = w_in[:, base + h * 128: base + (h + 1) * 128]
    w_lru = np.stack([np.concatenate([w_in[:, 512 * q:512 * q + 512], w_in[:, 1024 + 512 * q:1024 + 512 * q + 512]], axis=1) for q in range(2)])
    conv_w = np.stack([np.ascontiguousarray(inp["e_conv_w"][0][:, 512 * q:512 * q + 512].T) for q in range(2)])
    vecs = np.stack([np.stack([inp["e_conv_b"][0][512 * q:512 * q + 512], inp["e_b_gate_a"][0][512 * q:512 * q + 512],
                               inp["e_b_gate_x"][0][512 * q:512 * q + 512], inp["e_lru_lambda"][0][512 * q:512 * q + 512]], axis=1) for q in range(2)])
    cmask, cmats = _consts_A()
    w = inp["o_w_in"][0]
    q = w[:, 0:1024]; k = w[:, 1024:1152]; v = w[:, 1152:1280]; gg = w[:, 1280:2304]
    z64 = np.zeros((1024, 64), np.float32)
    w_i1 = np.concatenate([q, k[:, 0:64], z64, z64, k[:, 0:64], k[:, 64:128], z64, z64, k[:, 64:128], v, gg], axis=1)
    lnp = np.concatenate([inp["e_ln_g"][0], inp["e_ln_b"][0], inp["o_ln_g"][0], inp["o_ln_b"][0]])[None, :].repeat(128, 0)
    sinks = _sinks_group_order(inp["o_sinks"][0])[None, :].repeat(128, 0)
    ab, ab0 = _consts_B()
    return {"w_sb": w_sb, "w_lru": np.ascontiguousarray(w_lru), "conv_w": np.ascontiguousarray(conv_w), "vecs": np.ascontiguousarray(vecs),
            "wga": np.ascontiguousarray(inp["e_w_gate_a"][0].reshape(2, 4, 128, 128)), "wgx": np.ascontiguousarray(inp["e_w_gate_x"][0].reshape(2, 4, 128, 128)),
            "cmask": cmask, "cmats": cmats, "w_o0": np.ascontiguousarray(inp["e_w_out"][0]), "w_i1": np.ascontiguousarray(w_i1),
            "w_o1": np.ascontiguousarray(inp["o_w_out"][0]), "lnp": np.ascontiguousarray(lnp), "sinks": np.ascontiguousarray(sinks),
            "abias": ab, "ident": np.eye(128, dtype=np.float32)}, ab0


def _core_inputs(inp, shared, ab0, b, g, S):
    HALF = S // 2
    x = inp["x"][b, :S]
    xw = np.zeros((S, 1024), np.float32)
    if g == 0:
        xw[HALF:] = x[:HALF]
    else:
        xw[:] = x
    kmask = np.zeros((128, S // 512), np.float32)
    if g == 0:
        kmask[:, :S // 1024] = -30000.0
    x_in = np.zeros((HALF + 128, 1024), np.float32)
    t0 = g * HALF - 128
    lo = max(t0, 0)
    x_in[lo - t0:] = x[lo:t0 + HALF + 128]
    d = dict(shared)
    d.update({"xT": np.ascontiguousarray(xw.T), "kmask": kmask, "ctxf": np.full((128, 1), float(g), np.float32),
              "x_in": x_in, "abias0": ab0 if g == 0 else shared["abias"]})
    return d


def kernel(**inputs):
    inp = {k: np.asarray(v, dtype=np.float32) for k, v in inputs.items()}
    S = S_FULL
    cores = [(b, g) for b in range(4) for g in range(2)]
    if "F" not in _CACHE:
        _CACHE["F"] = build_F(S)
    shared, ab0 = _shared_inputs(inp)
    res = run_bass_kernel_spmd(_CACHE["F"], [_core_inputs(inp, shared, ab0, b, g, S) for b, g in cores], core_ids=list(range(8)))
    out = np.zeros((4, S, 1024), np.float32)
    for ci, (b, g) in enumerate(cores):
        out[b, g * (S // 2):(g + 1) * (S // 2)] = res.results[ci]["out"]
    return out
```

```python
import numpy as np
from contextlib import ExitStack
import concourse.bass as bass
import concourse.mybir as mybir

F32 = mybir.dt.float32
BF16 = mybir.dt.bfloat16
AF = mybir.ActivationFunctionType
ALU = mybir.AluOpType
AX = mybir.AxisListType

ENGS = ("pe", "act", "dve", "pool", "sp")
SAME_ENGINE_SYNC = True


class Op:
    __slots__ = ("eng", "fn", "deps", "needs_inc", "inc_val", "dma_sem", "dma_val", "idx", "is_dma")

    def __init__(self, eng, fn, is_dma=False, dma_sem=None):
        self.eng = eng
        self.fn = fn
        self.deps = []
        self.needs_inc = False
        self.inc_val = None
        self.is_dma = is_dma
        self.dma_sem = dma_sem
        self.dma_val = None


class Prog:
    def __init__(self, nc):
        self.nc = nc
        self.ops = {e: [] for e in ENGS}
        self.res = {}
        self.stack = ExitStack()
        self.esem = {}
        self.dma_chan = {}
        self.nsem = 0
        self.chan_last = {}

    def new_sem(self, name):
        self.nsem += 1
        return self.stack.enter_context(self.nc.semaphore(name))

    def chan(self, key):
        if key not in self.dma_chan:
            self.dma_chan[key] = [self.new_sem("dc%d" % len(self.dma_chan)), 0]
        return self.dma_chan[key]

    def _rec(self, o, reads, writes):
        deps = []
        for r in reads:
            st = self.res.setdefault(r, [None, []])
            if st[0] is not None:
                deps.append(st[0])
        for w in writes:
            st = self.res.setdefault(w, [None, []])
            if st[0] is not None:
                deps.append(st[0])
            deps.extend(st[1])
        for r in reads:
            self.res[r][1].append(o)
        for w in writes:
            st = self.res[w]
            st[0] = o
            st[1] = []
        seen = set()
        for d in deps:
            if d is o or id(d) in seen:
                continue
            seen.add(id(d))
            if (not d.is_dma) and d.eng == o.eng and (d.eng == "pe" or not SAME_ENGINE_SYNC):
                continue
            o.deps.append(d)
            if not d.is_dma:
                d.needs_inc = True
        self.ops[o.eng].append(o)
        return o

    def op(self, eng, fn, reads=(), writes=()):
        return self._rec(Op(eng, fn), reads, writes)

    def dma(self, eng, out, in_, reads=(), writes=(), chan=None, **kw):
        c = self.chan(chan)
        c[1] += 16
        o = Op(eng, lambda e, out=out, in_=in_, kw=kw: e.dma_start(out=out, in_=in_, **kw), is_dma=True, dma_sem=c[0])
        o.dma_val = c[1]
        self.chan_last[chan] = o
        return self._rec(o, reads, writes)

    def barrier(self):
        lasts = []
        for e in ENGS:
            for o in reversed(self.ops[e]):
                if not o.is_dma:
                    o.needs_inc = True
                    lasts.append(o)
                    break
        dmas = list(self.chan_last.values())
        for e in ENGS:
            b = Op(e, lambda eng: None)
            b.deps = [o for o in lasts if o.eng != e] + dmas
            self.ops[e].append(b)

    def emit(self):
        nc = self.nc
        for e in ENGS:
            self.esem[e] = self.new_sem("es_" + e)
            cnt = 0
            for o in self.ops[e]:
                if o.needs_inc and not o.is_dma:
                    cnt += 1
                    o.inc_val = cnt
        esem = self.esem

        def run(e, engobj):
            waited = {}
            for o in self.ops[e]:
                for d in o.deps:
                    if d.is_dma:
                        sem, val = d.dma_sem, d.dma_val
                    else:
                        sem, val = esem[d.eng], d.inc_val
                    k = id(sem)
                    if waited.get(k, 0) < val:
                        engobj.wait_ge(sem, val)
                        waited[k] = val
                inst = o.fn(engobj)
                if o.is_dma:
                    inst.then_inc(o.dma_sem, 16)
                elif o.needs_inc:
                    inst.then_inc(esem[e], 1)

        with nc.Block() as block:
            @block.tensor
            def _(eng):
                run("pe", eng)

            @block.scalar
            def _(eng):
                run("act", eng)

            @block.vector
            def _(eng):
                run("dve", eng)

            @block.gpsimd
            def _(eng):
                run("pool", eng)

            @block.sync
            def _(eng):
                run("sp", eng)
        self.stack.close()

POOLENG = 'dve'
LWIN = 3
POOL3 = False


def interleave(gens, window=2):
    gens = list(gens)
    active = []
    nxt = 0
    DONE = object()
    while active or nxt < len(gens):
        while len(active) < window and nxt < len(gens):
            active.append(gens[nxt])
            nxt += 1
        for g in list(active):
            if next(g, DONE) is DONE:
                assert active[0] is g, "generators must finish in admission order (slot reuse safety)"
                active.remove(g)


def record_A(nc, P, stk, cfg, xT, w_sb, w_lru, conv_w, vecs, wga, wgx, cmask, cmats, kmask_d, ctxf_d, ydst):
    NTW, OWN, HALO = cfg["NTW"], cfg["own_from"], cfg["halo"]
    NH, NG = cfg["n_heads"], cfg["n_lru_groups"]
    ycol = cfg["ycol"]
    yrow_a, yrow_b = cfg["yrow_a"], cfg["yrow_b"]
    S = NTW * 512
    NB = S // 128

    def sb(name, shape, dt=F32):
        return stk.enter_context(nc.sbuf_tensor(name, shape, dt)).ap()
    banks = [stk.enter_context(nc.psum_tensor("bank%d" % i, [128, 512], F32)).ap() for i in range(8)]
    B = lambda i: ("bank", i)

    cm = sb("cm", [128, 2048], BF16)
    cmt = sb("cmt", [128, 384], BF16)
    P.dma("pool", cm, cmask, writes=["cm"], chan="cm")
    P.dma("pool", cmt, cmats, writes=["cmt"], chan="cmt")
    ident = cmt[:, 0:128]
    trineg = cmt[:, 128:256]
    restneg = cmt[:, 256:384]
    kmask = sb("kmask_s", [128, NTW])
    P.dma("sp", kmask, kmask_d, writes=["kmask"], chan="kmask")
    ctxf = sb("ctxf_s", [128, 1])
    P.dma("sp", ctxf, ctxf_d, writes=["ctxf"], chan="ctxf")

    out_dmas = []
    lru_hook = [None]
    xT_v = xT.rearrange("(c p) t -> p c t", p=128)
    NXS = 4
    xts = [sb("xt%d" % i, [128, 8, 512], BF16) for i in range(NXS)]
    xcnt = [0]

    def load_x(i):
        s = xcnt[0] % NXS
        xcnt[0] += 1
        P.dma("pool", xts[s], xT_v[:, :, i * 512:(i + 1) * 512], writes=[("xt", s)], chan=("xt", s))
        return s

    Wt = [sb("Wt%d" % i, [128, 8, 512], BF16) for i in range(2)]
    kT = sb("kT0", [128, S], BF16)
    V = sb("V0", [128, NB, 128], BF16)
    qTb = [sb("qT%d" % i, [128, 512], BF16) for i in range(3)]
    sgb = [sb("sg%d" % i, [128, 512], F32) for i in range(3)]
    NE, NL, NE2, NW = 4, 4, 3, 3
    eb = [sb("e%d" % i, [128, 512], F32) for i in range(NE)]
    Lb = [sb("L%d" % i, [128, 512], BF16) for i in range(NL)]
    E2b = [sb("E2%d" % i, [128, 512], F32) for i in range(NE2)]
    wb = [sb("w%d" % i, [128, 512], BF16) for i in range(NW)]
    yb = [sb("yb%d" % i, [128, 512], BF16) for i in range(3)]
    vTs = sb("vTs", [128, 512], BF16)
    SCALE = 1.0 / np.sqrt(128.0)
    ZB = [0, 1]
    CB = 2
    OB = [3, 3]
    IB = [5, 6]
    ibc = [0]

    def next_ib():
        b = IB[ibc[0] % len(IB)]
        ibc[0] += 1
        return b

    def sb_pass(hl):
        ws = hl % 2
        W = Wt[ws]
        P.dma("pool", W, w_sb[hl].rearrange("(c p) n -> p c n", p=128), writes=[("W", ws)], chan=("W", ws))
        xslot = {}
        first_q = OWN - 1 if HALO else OWN
        cons = ([OWN - 1] + list(range(OWN - 2, -1, -1)) if HALO else []) + list(range(OWN, NTW))
        cpos = {t: k for k, t in enumerate(cons)}
        lptr = [0]

        def ensure_loaded(t):
            while lptr[0] <= min(cpos[t] + 1, len(cons) - 1):
                tt = cons[lptr[0]]
                xslot[tt] = load_x(tt)
                lptr[0] += 1

        inproj_done = set()

        def inproj_groups(i):
            xs = xslot[i]
            x = xts[xs]
            qs = i % 3

            def mm_part(b_hold, col0, part, first):
                def f():
                    if first:
                        b_hold[0] = next_ib()
                    b = b_hold[0]
                    for c in range(part * 4, part * 4 + 4):
                        P.op("pe", lambda e, c=c, b=b: e.matmul(banks[b], lhsT=W[:, c, col0:col0 + 128], rhs=x[:, c, :], start=(c == 0), stop=(c == 7)),
                             reads=[("xt", xs), ("W", ws)], writes=[B(b)])
                return f

            bq, bk, bv, bg = [None], [None], [None], [None]

            def q_fin():
                mm_part(bq, 0, 1, False)()
                b = bq[0]
                P.op("dve", lambda e, b=b: e.tensor_scalar(out=qTb[qs], in0=banks[b], scalar1=float(SCALE), scalar2=None, op0=ALU.mult),
                     reads=[B(b)], writes=[("qT", qs)])

            def k_fin():
                mm_part(bk, 128, 1, False)()
                b = bk[0]
                P.op("dve", lambda e, b=b: e.tensor_copy(out=kT[:, i * 512:(i + 1) * 512], in_=banks[b]),
                     reads=[B(b)], writes=[("kT", i)])

            def v_fin():
                mm_part(bv, 256, 1, False)()
                b = bv[0]
                P.op("dve", lambda e, b=b: e.tensor_copy(out=vTs, in_=banks[b]), reads=[B(b)], writes=["vTs"])

            def g_v2():
                b2 = next_ib()
                tb2 = banks[b2].bitcast(BF16)
                for j in range(4):
                    P.op("pe", lambda e, j=j, tb2=tb2: e.transpose(tb2[:, j * 128:(j + 1) * 128], vTs[:, j * 128:(j + 1) * 128], ident), reads=["vTs", "cmt"], writes=[B(b2)])
                P.op("dve", lambda e, tb2=tb2: e.tensor_copy(out=V[:, 4 * i:4 * i + 4, :], in_=tb2[:, 0:512].rearrange("p (j d) -> p j d", j=4)),
                     reads=[B(b2)], writes=[("V", i)])

            def g_fin():
                mm_part(bg, 384, 1, False)()
                b = bg[0]
                S_ = sgb[qs]
                P.op("act", lambda e, b=b: e.activation(out=S_, in_=banks[b], func=AF.Exp, scale=-1.0), reads=[B(b)], writes=[("sg", qs)])
                P.op("dve", lambda e: e.tensor_scalar_add(out=S_, in0=S_, scalar1=1.0), reads=[("sg", qs)], writes=[("sg", qs)])
                P.op("dve", lambda e: e.reciprocal(out=S_, in_=S_), reads=[("sg", qs)], writes=[("sg", qs)])
                P.op("dve", lambda e, b=b: e.tensor_tensor(out=S_, in0=S_, in1=banks[b], op=ALU.mult), reads=[("sg", qs), B(b)], writes=[("sg", qs)])

            def done_mark(f):
                def g():
                    f()
                    inproj_done.add(i)
                return g

            v_items = [mm_part(bv, 256, 0, True), v_fin]
            k_items = [mm_part(bk, 128, 0, True), k_fin]
            if i >= first_q:
                return v_items + [mm_part(bq, 0, 0, True), q_fin, g_v2] + k_items + [mm_part(bg, 384, 0, True), done_mark(g_fin)]
            return v_items + k_items + [done_mark(g_v2)]

        ensure_loaded(cons[0])
        for g in inproj_groups(cons[0]):
            g()
        pend = list(cons[1:OWN + 1]) if HALO else []
        queued = set(pend) | {cons[0]}

        tasks = []
        if HALO:
            ih = OWN - 1
            tasks += [(ih, kb, 384, 128) for kb in reversed(range(4 * ih + 4))]
        for i in range(OWN, NTW):
            tasks += [(i, kb, 0, 512) for kb in reversed(range(4 * i + 4))]
        NTK = len(tasks)
        side = []

        def first_of_tile(n):
            i, kb, c0, qw = tasks[n]
            return kb == 4 * i + 3

        def last_of_tile(n):
            return tasks[n][1] == 0

        for s in range(NTK + 5):
            if s < NTK:
                i, kb, c0, qw = tasks[s]
                assert (kb // 4) in inproj_done and i in inproj_done, ("in-proj not emitted before use", i, kb)
                if first_of_tile(s):
                    ni = i + 1
                    if ni < NTW and ni > first_q and ni not in queued:
                        queued.add(ni)
                        pend.append(ni)
                zb = ZB[s % 2]
                diag = kb >= 4 * i
                qs = i % 3
                P.op("pe", lambda e, zb=zb, kb=kb, qs=qs, diag=diag, c0=c0, qw=qw: e.matmul(banks[zb][:, 0:qw], lhsT=kT[:, kb * 128:(kb + 1) * 128], rhs=qTb[qs][:, c0:c0 + qw], start=True, stop=not diag),
                     reads=[("kT", kb // 4), ("qT", qs)], writes=[B(zb)])
                if diag:
                    r = kb - 4 * i
                    P.op("pe", lambda e, zb=zb, r=r, c0=c0, qw=qw: e.matmul(banks[zb][:, 0:qw], lhsT=ident, rhs=cm[:, r * 512 + c0:r * 512 + c0 + qw], start=False, stop=True),
                         reads=["cm", "cmt"], writes=[B(zb)])
            n = s - 3
            if 0 <= n < NTK:
                qw = tasks[n][3]
                P.op("act", lambda e, n=n, qw=qw: e.activation(out=E2b[n % NE2][:, 0:qw], in_=banks[CB][:, 0:qw], func=AF.Exp),
                     reads=[B(CB)], writes=[("E2", n % NE2)])
            n = s - 1
            if 0 <= n < NTK:
                zb = ZB[n % 2]
                kb, qw = tasks[n][1], tasks[n][3]
                P.op("act", lambda e, n=n, zb=zb, kb=kb, qw=qw: e.activation(out=eb[n % NE][:, 0:qw], in_=banks[zb][:, 0:qw], func=AF.Exp, bias=kmask[:, kb // 4:kb // 4 + 1]),
                     reads=[B(zb), "kmask"], writes=[("e", n % NE)])
                P.op("act", lambda e, n=n, qw=qw: e.activation(out=Lb[n % NL][:, 0:qw], in_=eb[n % NE][:, 0:qw], func=AF.Ln, bias=1.0),
                     reads=[("e", n % NE)], writes=[("L", n % NL)])
            n = s - 3
            if 0 <= n < NTK and not last_of_tile(n):
                qw = tasks[n][3]
                P.op("pe", lambda e, n=n, qw=qw: e.matmul(banks[CB][:, 0:qw], lhsT=restneg, rhs=Lb[n % NL][:, 0:qw], start=False, stop=True, skip_group_check=True),
                     reads=[("L", n % NL), "cmt"], writes=[B(CB)])
            n = s - 2
            if 0 <= n < NTK:
                qw = tasks[n][3]
                P.op("pe", lambda e, n=n, st=first_of_tile(n), qw=qw: e.matmul(banks[CB][:, 0:qw], lhsT=trineg, rhs=Lb[n % NL][:, 0:qw], start=st, stop=True, skip_group_check=True),
                     reads=[("L", n % NL), "cmt"], writes=[B(CB)])
            n = s - 3
            if 0 <= n < NTK:
                qw = tasks[n][3]
                P.op("dve", lambda e, n=n, qw=qw: e.tensor_tensor(out=wb[n % NW][:, 0:qw], in0=eb[n % NE][:, 0:qw], in1=E2b[n % NE2][:, 0:qw], op=ALU.mult),
                     reads=[("e", n % NE), ("E2", n % NE2)], writes=[("w", n % NW)])
            n = s - 4
            if 0 <= n < NTK:
                i, kb, c0, qw = tasks[n]
                ob = OB[i % 2]
                P.op("pe", lambda e, n=n, kb=kb, ob=ob, st=first_of_tile(n), sp=last_of_tile(n), qw=qw: e.matmul(banks[ob][:, 0:qw], lhsT=V[:, kb, :], rhs=wb[n % NW][:, 0:qw], start=st, stop=sp),
                     reads=[("V", kb // 4), ("w", n % NW)], writes=[B(ob)])
                if last_of_tile(n):
                    qs = i % 3
                    P.op("dve", lambda e, ob=ob, qs=qs, c0=c0, qw=qw: e.tensor_tensor(out=yb[qs][:, 0:qw], in0=banks[ob][:, 0:qw], in1=sgb[qs][:, c0:c0 + qw], op=ALU.mult),
                         reads=[B(ob), ("sg", qs)], writes=[("yb", qs)])
                    row0 = yrow_b + hl * 128
                    col0 = 0 if qw == 128 else ycol(i)
                    out_dmas.append(P.dma("sp", ydst[row0:row0 + 128, col0:col0 + qw], yb[qs][:, 0:qw], reads=[("yb", qs)], chan=("yb", qs)))
            in_halo = HALO and s < 32
            npop = 2 if in_halo else (1 if s % 3 == 0 else 0)
            for _ in range(npop):
                if not side and pend:
                    ti = pend.pop(0)
                    ensure_loaded(ti)
                    side.extend(inproj_groups(ti))
                if side:
                    side.pop(0)()
            lru_hook[0](2 if s % 12 == 0 else 1)
        while side or pend:
            if not side:
                ti = pend.pop(0)
                ensure_loaded(ti)
                side.extend(inproj_groups(ti))
            side.pop(0)()

    Wl = sb("Wl", [128, 8, 1024], BF16)
    Wa = sb("Wa", [128, 4, 128], BF16)
    Wx = sb("Wx", [128, 4, 128], BF16)
    cw = sb("cw", [128, 4, 4])
    vc = sb("vc", [128, 4, 4])
    nb = sb("nb", [128, 4, 2])
    cch = sb("cch", [128, 4, 3])
    axbuf = [sb("axbuf%d" % c, [128, 515]) for c in range(4)]
    hst = sb("hst", [128, 4])
    T = {}
    for nm, dt in [("xc", F32), ("xcb", BF16), ("er", F32), ("ei", F32), ("a", F32), ("a2", F32), ("u", F32), ("h", F32), ("eg", F32), ("yl", BF16)]:
        T[nm] = [sb("l_%s%d" % (nm, k), [128, 512], dt) for k in range(2)]
    xtl = [sb("xtl%d" % i, [128, 8, 512], BF16) for i in range(2)]
    xlcnt = [0]

    def load_xl(i):
        s = xlcnt[0] % 2
        xlcnt[0] += 1
        P.dma("pool", xtl[s], xT_v[:, :, i * 512:(i + 1) * 512], writes=[("xtl", s)], chan=("xtl", s))
        return s


    def lru_pass(grp):
        yield
        P.dma("pool", Wl, w_lru[grp].rearrange("(c p) n -> p c n", p=128), writes=["Wl"], chan="Wl")
        P.dma("pool", Wa, wga[grp].rearrange("n d e -> d n e"), writes=["Wa"], chan="Wa")
        P.dma("pool", Wx, wgx[grp].rearrange("n d e -> d n e"), writes=["Wx"], chan="Wx")
        P.dma("sp", cw, conv_w[grp].rearrange("(c p) k -> p c k", p=128), writes=["cw"], chan="cw")
        P.dma("sp", vc, vecs[grp].rearrange("(c p) k -> p c k", p=128), writes=["vc"], chan="vc")
        P.op("dve", lambda e: e.tensor_scalar(out=nb, in0=vc[:, :, 1:3], scalar1=-1.0, scalar2=None, op0=ALU.mult), reads=["vc"], writes=["nb"])
        P.op("act", lambda e: e.activation(out=cch[:, :, 2:3], in_=vc[:, :, 3:4], func=AF.Exp, scale=-1.0), reads=["vc"], writes=["cch"])
        P.op("act", lambda e: e.activation(out=cch[:, :, 2:3], in_=cch[:, :, 2:3], func=AF.Ln, bias=1.0), reads=["cch"], writes=["cch"])
        P.op("dve", lambda e: e.tensor_scalar(out=cch[:, :, 0:1], in0=cch[:, :, 2:3], scalar1=-8.0, scalar2=None, op0=ALU.mult), reads=["cch"], writes=["cch"])
        P.op("dve", lambda e: e.tensor_scalar(out=cch[:, :, 1:2], in0=cch[:, :, 2:3], scalar1=-16.0, scalar2=None, op0=ALU.mult), reads=["cch"], writes=["cch"])
        for c in range(4):
            P.op("pool", lambda e, c=c: e.memset(axbuf[c], 0.0), writes=[("axbuf", c)])
        P.op("pool", lambda e: e.memset(hst, 0.0), writes=[("hst", c) for c in range(4)])
        yield
        xslot = {0: load_xl(0)}
        first_y = OWN - 1 if HALO else OWN

        def chunk_gen(i, c, k):
            if c == 0 and i + 1 < NTW:
                xslot[i + 1] = load_xl(i + 1)
            xs = xslot[i]
            x = xtl[xs]
            need_y = i >= first_y
            is_ctx = i < OWN
            bax, bag, br, bi = 4, 7, 4, 7
            R = lambda nm: (nm, k)
            ab = axbuf[c]
            xc, xcb, er, ei, a, a2, u, h, eg, yl = [T[nm][k] for nm in ("xc", "xcb", "er", "ei", "a", "a2", "u", "h", "eg", "yl")]
            for kc in range(8):
                P.op("pe", lambda e, kc=kc: e.matmul(banks[bax], lhsT=Wl[:, kc, c * 128:(c + 1) * 128], rhs=x[:, kc, :], start=(kc == 0), stop=(kc == 7)),
                     reads=[("xtl", xs), "Wl"], writes=[B(bax)])
                if kc == 3:
                    yield
            yield
            P.op("dve", lambda e: e.tensor_copy(out=ab[:, 3:515], in_=banks[bax]), reads=[B(bax)], writes=[("axbuf", c)])
            yield
            if POOL3:
                P.op("pool", lambda e: e.tensor_scalar(out=xc, in0=ab[:, 3:515], scalar1=cw[:, c, 3:4], scalar2=vc[:, c, 0:1], op0=ALU.mult, op1=ALU.add),
                     reads=[("axbuf", c), "cw", "vc"], writes=[R("xc")])
            else:
                P.op("dve", lambda e: e.tensor_scalar(out=xc, in0=ab[:, 3:515], scalar1=cw[:, c, 3:4], scalar2=vc[:, c, 0:1], op0=ALU.mult, op1=ALU.add),
                     reads=[("axbuf", c), "cw", "vc"], writes=[R("xc")])
            yield
            for tap in (2, 1, 0):
                P.op("dve", lambda e, tap=tap: e.scalar_tensor_tensor(out=xc, in0=ab[:, tap:tap + 512], scalar=cw[:, c, tap:tap + 1], in1=xc, op0=ALU.mult, op1=ALU.add),
                     reads=[("axbuf", c), "cw", R("xc")], writes=[R("xc")])
                yield
            if POOL3:
                P.op("pool", lambda e: e.tensor_copy(out=xcb, in_=xc), reads=[R("xc")], writes=[R("xcb")])
                P.op("pool", lambda e: e.tensor_copy(out=ab[:, 0:3], in_=ab[:, 512:515]), reads=[("axbuf", c)], writes=[("axbuf", c)])
            else:
                P.op("dve", lambda e: e.tensor_copy(out=xcb, in_=xc), reads=[R("xc")], writes=[R("xcb")])
                P.op("dve", lambda e: e.tensor_copy(out=ab[:, 0:3], in_=ab[:, 512:515]), reads=[("axbuf", c)], writes=[("axbuf", c)])
            yield
            P.op("pe", lambda e: e.matmul(banks[br], lhsT=Wa[:, c, :], rhs=xcb, start=True, stop=True), reads=["Wa", R("xcb")], writes=[B(br)])
            P.op("pe", lambda e: e.matmul(banks[bi], lhsT=Wx[:, c, :], rhs=xcb, start=True, stop=True), reads=["Wx", R("xcb")], writes=[B(bi)])
            yield
            P.op("act", lambda e: e.activation(out=er, in_=banks[br], func=AF.Exp, scale=-1.0, bias=nb[:, c, 0:1]), reads=[B(br), "nb"], writes=[R("er")])
            yield
            P.op("act", lambda e: e.activation(out=ei, in_=banks[bi], func=AF.Exp, scale=-1.0, bias=nb[:, c, 1:2]), reads=[B(bi), "nb"], writes=[R("ei")])
            yield
            P.op(POOLENG, lambda e: e.tensor_scalar_add(out=er, in0=er, scalar1=1.0), reads=[R("er")], writes=[R("er")])
            yield
            P.op("dve", lambda e: e.reciprocal(out=er, in_=er), reads=[R("er")], writes=[R("er")])
            yield
            P.op("act", lambda e: e.activation(out=a, in_=er, func=AF.Exp, scale=cch[:, c, 0:1]), reads=[R("er"), "cch"], writes=[R("a")])
            yield
            P.op("dve", lambda e: e.tensor_tensor(out=a2, in0=a, in1=a, op=ALU.mult), reads=[R("a")], writes=[R("a2")])
            yield
            P.op("act", lambda e: e.activation(out=a2, in_=a2, func=AF.Ln, scale=-1.0, bias=1.0), reads=[R("a2")], writes=[R("a2")])
            yield
            P.op("act", lambda e: e.activation(out=a2, in_=a2, func=AF.Exp, scale=0.5), reads=[R("a2")], writes=[R("a2")])
            yield
            P.op(POOLENG, lambda e: e.tensor_scalar_add(out=ei, in0=ei, scalar1=1.0), reads=[R("ei")], writes=[R("ei")])
            yield
            P.op(POOLENG, lambda e: e.tensor_tensor(out=u, in0=xc, in1=a2, op=ALU.mult), reads=[R("xc"), R("a2")], writes=[R("u")])
            yield
            P.op("dve", lambda e: e.reciprocal(out=ei, in_=ei), reads=[R("ei")], writes=[R("ei")])
            yield
            if is_ctx:
                P.op("dve", lambda e: e.scalar_tensor_tensor(out=u, in0=u, scalar=ctxf[:, 0:1], in1=ei, op0=ALU.mult, op1=ALU.mult), reads=[R("u"), R("ei"), "ctxf"], writes=[R("u")])
            else:
                P.op("dve", lambda e: e.tensor_tensor(out=u, in0=u, in1=ei, op=ALU.mult), reads=[R("u"), R("ei")], writes=[R("u")])
            yield
            P.op("dve", lambda e: e.tensor_tensor_scan(out=h, data0=a, data1=u, initial=hst[:, c:c + 1], op0=ALU.mult, op1=ALU.add),
                 reads=[R("a"), R("u"), ("hst", c)], writes=[R("h")])
            yield
            P.op("dve", lambda e: e.tensor_copy(out=hst[:, c:c + 1], in_=h[:, 511:512]), reads=[R("h")], writes=[("hst", c)])
            yield
            if need_y:
                for kc in range(8):
                    P.op("pe", lambda e, kc=kc: e.matmul(banks[bag], lhsT=Wl[:, kc, 512 + c * 128:512 + (c + 1) * 128], rhs=x[:, kc, :], start=(kc == 0), stop=(kc == 7)),
                         reads=[("xtl", xs), "Wl"], writes=[B(bag)])
                    if kc == 3:
                        yield
                yield
                P.op("act", lambda e: e.activation(out=eg, in_=banks[bag], func=AF.Exp, scale=-1.0), reads=[B(bag)], writes=[R("eg")])
                yield
                P.op(POOLENG, lambda e: e.tensor_scalar_add(out=eg, in0=eg, scalar1=1.0), reads=[R("eg")], writes=[R("eg")])
                yield
                P.op("dve", lambda e: e.reciprocal(out=eg, in_=eg), reads=[R("eg")], writes=[R("eg")])
                yield
                P.op("dve", lambda e: e.tensor_tensor(out=eg, in0=banks[bag], in1=eg, op=ALU.mult), reads=[R("eg"), B(bag)], writes=[R("eg")])
                yield
                P.op("dve", lambda e: e.tensor_tensor(out=yl, in0=h, in1=eg, op=ALU.mult), reads=[R("eg"), R("h")], writes=[R("yl")])
                yield
                row0 = yrow_a + grp * 512 + c * 128
                if i >= OWN:
                    out_dmas.append(P.dma("sp", ydst[row0:row0 + 128, ycol(i):ycol(i) + 512], yl, reads=[R("yl")], chan=("yl", k)))
                else:
                    out_dmas.append(P.dma("sp", ydst[row0:row0 + 128, 0:128], yl[:, 384:512], reads=[R("yl")], chan=("yl", k)))
                yield

        it = 0
        for i in range(NTW):
            for c in range(4):
                yield from chunk_gen(i, c, it % 2)
                it += 1

    def lru_master():
        for grp in range(NG):
            yield from lru_pass(grp)

    lru_gen = lru_master()
    lru_done = [False]

    def lru_advance(nsteps):
        for _ in range(nsteps):
            if lru_done[0]:
                return
            if next(lru_gen, "DONE") == "DONE":
                lru_done[0] = True

    lru_hook[0] = lru_advance
    for hl in range(NH):
        sb_pass(hl)
    while not lru_done[0]:
        lru_advance(64)
    return out_dmas


def build_F(S):
    nc = bass.Bass("TRN2", target_bir_lowering=False)
    NB = S // 128
    HALF = S // 2
    TOKB = HALF + 128
    dr = lambda name, shape, dt=F32, kind="ExternalInput": nc.dram_tensor(name, shape, dt, kind=kind).ap()
    xT = dr("xT", [1024, S])
    w_sb = dr("w_sb", [8, 1024, 512])
    w_lru = dr("w_lru", [2, 1024, 1024])
    conv_w = dr("conv_w", [2, 512, 4])
    vecs = dr("vecs", [2, 512, 4])
    wga = dr("wga", [2, 4, 128, 128])
    wgx = dr("wgx", [2, 4, 128, 128])
    cmask = dr("cmask", [128, 2048])
    cmats = dr("cmats", [128, 384])
    kmask = dr("kmask", [128, S // 512])
    ctxf = dr("ctxf", [128, 1])
    x_in = dr("x_in", [TOKB, 1024])
    w_o0 = dr("w_o0", [2048, 1024])
    w_i1 = dr("w_i1", [1024, 2688])
    w_o1 = dr("w_o1", [1024, 1024])
    lnp = dr("lnp", [128, 4096])
    sinks = dr("sinks", [128, 16])
    abias_d = dr("abias", [128, 4096])
    abias0_d = dr("abias0", [128, 4096])
    ident_d = dr("ident", [128, 128])
    out = dr("out", [HALF, 1024], F32, "ExternalOutput")
    yint = nc.dram_tensor("yint", [2048, TOKB], BF16).ap()
    NTW = S // 512
    OWN = NTW // 2
    P = Prog(nc)
    cfg = dict(NTW=NTW, own_from=OWN, halo=True, n_heads=8, n_lru_groups=2, ycol=lambda i: 128 + (i - OWN) * 512, yrow_a=0, yrow_b=1024)
    with ExitStack() as stk:
        record_A(nc, P, stk, cfg, xT, w_sb, w_lru, conv_w, vecs, wga, wgx, cmask, cmats, kmask, ctxf, yint)
        P.barrier()
        P.emit()
    P2 = Prog(nc)
    outs = record_B(nc, P2, HALF // 128, yint, x_in, w_o0, w_i1, w_o1, lnp, sinks, abias_d, abias0_d, ident_d, out)
    fin = P2.op("sp", lambda eng: None)
    fin.deps = list(outs)
    P2.emit()
    return nc


R_OLD = 1
R_YOUNG = 1
ALPHA = float(4 ** 0.25)
EPS = 1e-5


def record_B(nc, P, TB, yT_in, x_in, w_o0, w_i1, w_o1, lnp, sinks, abias_d, abias0_d, ident_d, out, ykey=None):
    NBK = TB + 1
    sb = lambda name, shape, dt=F32: nc.alloc_sbuf_tensor(name, shape, dt).ap()
    banks = [nc.alloc_psum_tensor("bankB%d" % i, [128, 512], F32).ap() for i in range(8)]
    B = lambda i: ("bankB", i)

    Wo0 = sb("Wo0", [128, 16, 1024], BF16)
    Wi1 = sb("Wi1", [128, 8, 2688], BF16)
    Wo1 = sb("Wo1", [128, 8, 1024], BF16)
    for h in range(2):
        P.dma("pool", Wo0[:, 8 * h:8 * h + 8, :], w_o0[1024 * h:1024 * (h + 1)].rearrange("(c p) n -> p c n", p=128), writes=["Wo0"], chan=("Wo0", h))
    P.dma("pool", Wi1[:, 0:4, :], w_i1[0:512].rearrange("(c p) n -> p c n", p=128), writes=["Wi1"], chan=("Wi1", 0))
    P.dma("pool", Wi1[:, 4:8, :], w_i1[512:1024].rearrange("(c p) n -> p c n", p=128), writes=["Wi1"], chan=("Wi1", 1))
    P.dma("pool", Wo1, w_o1.rearrange("(c p) n -> p c n", p=128), writes=["Wo1"], chan="Wo1")
    lnb = sb("lnb", [128, 4, 1024])
    P.dma("sp", lnb, lnp.rearrange("p (k n) -> p k n", k=4), writes=["lnb"], chan="lnb")
    snk = sb("snk", [128, 16])
    P.dma("sp", snk, sinks, writes=["snk"], chan="snk")
    abT = sb("abT_s", [128, 16, 2, 128])
    abT0 = sb("abT0_s", [128, 16, 128])
    P.dma("sp", abT, abias_d.rearrange("p (i h q) -> p i h q", i=16, h=2), writes=["abT"], chan="abias")
    P.dma("sp", abT0, abias0_d.rearrange("p (i h q) -> p i h q", i=16, h=2)[:, :, 0, :], writes=["abT0"], chan="abias0")
    cvec = sb("cvec", [128, 16])
    esk = sb("esk", [128, 16])
    P.op("dve", lambda e: e.tensor_scalar_max(out=cvec, in0=snk, scalar1=0.0), reads=["snk"], writes=["cvec"])
    P.op("dve", lambda e: e.tensor_tensor(out=esk, in0=snk, in1=cvec, op=ALU.subtract), reads=["snk", "cvec"], writes=["esk"])
    P.op("act", lambda e: e.activation(out=esk, in_=esk, func=AF.Exp), reads=["esk"], writes=["esk"])
    for i16 in range(16):
        P.op("dve", lambda e, i16=i16: e.tensor_scalar(out=abT[:, i16, :, :], in0=abT[:, i16, :, :], scalar1=cvec[:, i16:i16 + 1], scalar2=None, op0=ALU.subtract), reads=["abT", "cvec"], writes=["abT"])
        P.op("dve", lambda e, i16=i16: e.tensor_scalar(out=abT0[:, i16, :], in0=abT0[:, i16, :], scalar1=cvec[:, i16:i16 + 1], scalar2=None, op0=ALU.subtract), reads=["abT0", "cvec"], writes=["abT0"])
    ident = sb("ident_s", [128, 128], BF16)
    P.dma("pool", ident, ident_d, writes=["identB"], chan="identB")

    yin = [sb("yin%d" % i, [128, 16, 128], BF16) for i in range(2)]
    xin = [sb("xin%d" % i, [128, 1024]) for i in range(2)]
    x1 = [sb("x1_%d" % i, [128, 1024]) for i in range(2)]
    x1b = sb("x1b", [128, 1024], BF16)
    x1T = sb("x1T", [128, 8, 128], BF16)
    qT = [sb("qTB%d" % i, [128, 8, 128], BF16) for i in range(2)]
    kT = sb("kTr", [128, 4, 3, 128], BF16)
    vv = sb("vr", [128, 3, 2, 65], BF16)
    sg = [sb("sgB%d" % i, [128, 1024]) for i in range(2)]
    scb = [sb("scb%d" % i, [128, 2, 512]) for i in range(2)]
    pT = [sb("pT%d" % i, [128, 2, 512], BF16) for i in range(2)]
    st = [sb("st%d" % i, [128, 16]) for i in range(2)]
    y1 = sb("y1", [128, 1024], BF16)
    y1T = sb("y1T", [128, 8, 128], BF16)
    ob = [sb("ob%d" % i, [128, 1024]) for i in range(2)]
    stats = [sb("stats%d" % i, [128, 2, 6]) for i in range(2)]
    mv = [sb("mv%d" % i, [128, 4]) for i in range(2)]
    P.op("pool", lambda e: e.memset(vv, 1.0), writes=[("vv", 0), ("vv", 1), ("vv", 2)])
    out_dmas = []
    yT_v = yT_in.rearrange("(c p) t -> p c t", p=128)

    def layer_norm(buf, key, gi, sid):
        S_, M_ = stats[sid], mv[sid]
        sk, mk = ("stats", sid), ("mv", sid)
        for hh in range(2):
            P.op("dve", lambda e, hh=hh: e.bn_stats(out=S_[:, hh, :], in_=buf[:, hh * 512:(hh + 1) * 512]), reads=[key], writes=[sk])
        yield
        P.op("dve", lambda e: e.bn_aggr(out=M_[:, 0:2], in_=S_.rearrange("p a b -> p (a b)")), reads=[sk], writes=[mk])
        yield
        P.op("dve", lambda e: e.tensor_scalar_add(out=M_[:, 2:3], in0=M_[:, 1:2], scalar1=EPS), reads=[mk], writes=[mk])
        yield
        P.op("act", lambda e: e.activation(out=M_[:, 2:3], in_=M_[:, 2:3], func=AF.Ln), reads=[mk], writes=[mk])
        yield
        P.op("act", lambda e: e.activation(out=M_[:, 2:3], in_=M_[:, 2:3], func=AF.Exp, scale=-0.5), reads=[mk], writes=[mk])
        yield
        P.op("dve", lambda e: e.scalar_tensor_tensor(out=M_[:, 3:4], in0=M_[:, 0:1], scalar=-1.0, in1=M_[:, 2:3], op0=ALU.mult, op1=ALU.mult), reads=[mk], writes=[mk])
        yield
        P.op("act", lambda e: e.activation(out=buf, in_=buf, func=AF.Identity, scale=M_[:, 2:3], bias=M_[:, 3:4]), reads=[key, mk], writes=[key])
        yield
        P.op("pool", lambda e: e.tensor_tensor(out=buf, in0=buf, in1=lnb[:, gi, :], op=ALU.mult), reads=[key, "lnb"], writes=[key])
        yield
        P.op("pool", lambda e: e.tensor_tensor(out=buf, in0=buf, in1=lnb[:, gi + 1, :], op=ALU.add), reads=[key, "lnb"], writes=[key])
        yield

    def load_blk(j):
        ys = j % 2
        extra = list(ykey(j)) if ykey is not None else []
        for q4 in range(4):
            P.dma("sp", yin[ys][:, 4 * q4:4 * q4 + 4, :], yT_v[:, 4 * q4:4 * q4 + 4, j * 128:(j + 1) * 128], reads=extra, writes=[("yin", ys)], chan=("yin", ys, q4))
        P.dma("sp", xin[ys], x_in[j * 128:(j + 1) * 128, :], writes=[("xin", ys)], chan=("xin", ys))

    def blk(j):
        ys = j % 2
        xs = j % 2
        X1 = x1[xs]
        x1k = ("x1", xs)
        if j == 0:
            load_blk(0)
        if j + 1 < NBK:
            load_blk(j + 1)
        for hh in range(2):
            for kc in range(16):
                P.op("pe", lambda e, hh=hh, kc=kc: e.matmul(banks[hh], lhsT=yin[ys][:, kc, :], rhs=Wo0[:, kc, hh * 512:(hh + 1) * 512], start=(kc == 0), stop=(kc == 15)),
                     reads=[("yin", ys), "Wo0"], writes=[B(hh)])
                if kc % 4 == 3:
                    yield
        for hh in range(2):
            P.op("dve", lambda e, hh=hh: e.scalar_tensor_tensor(out=X1[:, hh * 512:(hh + 1) * 512], in0=xin[ys][:, hh * 512:(hh + 1) * 512], scalar=ALPHA, in1=banks[hh], op0=ALU.mult, op1=ALU.add),
                 reads=[("xin", ys), B(hh)], writes=[x1k])
            yield
        yield from layer_norm(X1, x1k, 0, 0)
        P.op("pool", lambda e: e.tensor_copy(out=x1b, in_=X1), reads=[x1k], writes=["x1b"])
        yield
        tb = banks[2].bitcast(BF16)
        for c in range(8):
            P.op("pe", lambda e, c=c: e.transpose(tb[:, c * 128:(c + 1) * 128], x1b[:, c * 128:(c + 1) * 128], ident), reads=["x1b", "identB"], writes=[B(2)])
            if c % 4 == 3:
                yield
        P.op("act", lambda e: e.activation(out=x1T.rearrange("p c t -> p (c t)"), in_=tb, func=AF.Copy), reads=[B(2)], writes=["x1T"])
        yield
        ks = j % 3
        for c2 in range(4):
            for kc in range(8):
                P.op("pe", lambda e, c2=c2, kc=kc: e.matmul(banks[3][:, c2 * 128:(c2 + 1) * 128], lhsT=Wi1[:, kc, 1024 + c2 * 128:1024 + (c2 + 1) * 128], rhs=x1T[:, kc, :], start=(kc == 0), stop=(kc == 7)),
                     reads=["x1T", "Wi1"], writes=[B(3)])
            yield
        for kc in range(8):
            P.op("pe", lambda e, kc=kc: e.matmul(banks[2][:, 0:128], lhsT=x1T[:, kc, :], rhs=Wi1[:, kc, 1536:1664], start=(kc == 0), stop=(kc == 7)),
                 reads=["x1T", "Wi1"], writes=[B(2)])
        yield
        P.op("dve", lambda e: e.tensor_copy(out=kT[:, :, ks, :], in_=banks[3].rearrange("p (c t) -> p c t", c=4)), reads=[B(3)], writes=[("kT", ks)])
        yield
        P.op("dve", lambda e: e.tensor_copy(out=vv[:, ks, :, 0:64], in_=banks[2][:, 0:128].rearrange("p (c d) -> p c d", c=2)), reads=[B(2)], writes=[("vv", ks)])
        yield
        if j == 0:
            yield "S2"
            return
        qs = j % 2
        Q = qT[qs]
        qk = ("qTB", qs)
        SG = sg[qs]
        sgk = ("sgB", qs)
        for c in range(8):
            bq = c // 4
            for kc in range(8):
                P.op("pe", lambda e, c=c, kc=kc, bq=bq: e.matmul(banks[bq][:, (c % 4) * 128:(c % 4 + 1) * 128], lhsT=Wi1[:, kc, c * 128:(c + 1) * 128], rhs=x1T[:, kc, :], start=(kc == 0), stop=(kc == 7)),
                     reads=["x1T", "Wi1"], writes=[B(bq)])
            yield
        for hh in range(2):
            P.op("dve", lambda e, hh=hh: e.tensor_scalar(out=Q[:, 4 * hh:4 * hh + 4, :], in0=banks[hh].rearrange("p (c t) -> p c t", c=4), scalar1=0.125, scalar2=None, op0=ALU.mult),
                 reads=[B(hh)], writes=[qk])
            yield
        for hh in range(2):
            for kc in range(8):
                P.op("pe", lambda e, hh=hh, kc=kc: e.matmul(banks[hh], lhsT=x1T[:, kc, :], rhs=Wi1[:, kc, 1664 + hh * 512:1664 + (hh + 1) * 512], start=(kc == 0), stop=(kc == 7)),
                     reads=["x1T", "Wi1"], writes=[B(hh)])
                if kc % 4 == 3:
                    yield
        for hh in range(2):
            P.op("act", lambda e, hh=hh: e.activation(out=SG[:, hh * 512:(hh + 1) * 512], in_=banks[hh], func=AF.Exp, scale=-1.0), reads=[B(hh)], writes=[sgk])
            yield
        P.op("dve", lambda e: e.tensor_scalar_add(out=SG, in0=SG, scalar1=1.0), reads=[sgk], writes=[sgk])
        yield
        P.op("dve", lambda e: e.reciprocal(out=SG, in_=SG), reads=[sgk], writes=[sgk])
        yield
        for hh in range(2):
            P.op("dve", lambda e, hh=hh: e.tensor_tensor(out=SG[:, hh * 512:(hh + 1) * 512], in0=SG[:, hh * 512:(hh + 1) * 512], in1=banks[hh], op=ALU.mult), reads=[sgk, B(hh)], writes=[sgk])
            yield
        yield "S2"
        ps = (j - 1) % 3
        tb6 = banks[6].bitcast(BF16)

        def grp_gen(gi):
            c, par = gi // 2, gi % 2
            sl = gi % 2
            S_ = scb[sl]
            PT = pT[sl]
            T_ = st[sl]
            ob_ = 6 + sl
            for half, slot in ((0, ps), (1, ks)):
                P.op("pe", lambda e, half=half, slot=slot: e.matmul(banks[4 + half], lhsT=kT[:, 2 * c + par, slot, :], rhs=Q[:, 4 * c:4 * c + 4, :].rearrange("p a q -> p (a q)"), start=True, stop=True),
                     reads=[qk, ("kT", slot)], writes=[B(4 + half)])
            yield
            for half in range(2):
                if half == 0 and j == 1:
                    bsrc, bkey = abT0[:, 4 * gi:4 * gi + 4, :], "abT0"
                else:
                    bsrc, bkey = abT[:, 4 * gi:4 * gi + 4, half, :], "abT"
                P.op("dve", lambda e, half=half, bsrc=bsrc: e.tensor_tensor(out=S_[:, half, :].rearrange("p (a q) -> p a q", a=4), in0=banks[4 + half].rearrange("p (a q) -> p a q", a=4), in1=bsrc, op=ALU.add),
                     reads=[B(4 + half), bkey], writes=[("scb", sl)])
                yield
            for half in range(2):
                P.op("act", lambda e, half=half: e.activation(out=PT[:, half, :], in_=S_[:, half, :], func=AF.Exp), reads=[("scb", sl)], writes=[("pT", sl)])
                yield
            for jj in range(4):
                for half, slot in ((0, ps), (1, ks)):
                    P.op("pe", lambda e, jj=jj, half=half, slot=slot: e.matmul(banks[ob_][:, jj * 65:(jj + 1) * 65], lhsT=PT[:, half, jj * 128:(jj + 1) * 128],
                                                                            rhs=vv[:, slot, c, :], start=(half == 0), stop=(half == 1)),
                         reads=[("pT", sl), ("vv", slot)], writes=[B(ob_)])
            yield
            ov = banks[ob_][:, 0:260].rearrange("p (a d) -> p a d", a=4)
            P.op("dve", lambda e: e.tensor_tensor(out=T_[:, 0:4], in0=ov[:, :, 64], in1=esk[:, 4 * gi:4 * gi + 4], op=ALU.add), reads=[B(ob_), "esk"], writes=[("st", sl)])
            yield
            P.op("dve", lambda e: e.reciprocal(out=T_[:, 4:8], in_=T_[:, 0:4]), reads=[("st", sl)], writes=[("st", sl)])
            yield
            for jj in range(4):
                h = 8 * c + 2 * jj + par
                P.op("dve", lambda e, jj=jj, h=h: e.scalar_tensor_tensor(out=y1[:, h * 64:(h + 1) * 64], in0=ov[:, jj, 0:64], scalar=T_[:, 4 + jj:5 + jj],
                                                                     in1=SG[:, h * 64:(h + 1) * 64], op0=ALU.mult, op1=ALU.mult),
                     reads=[B(ob_), ("st", sl), sgk], writes=["y1"])
                yield

        for gi in range(4):
            yield from grp_gen(gi)
        for c in range(8):
            P.op("pe", lambda e, c=c: e.transpose(tb6[:, c * 128:(c + 1) * 128], y1[:, c * 128:(c + 1) * 128], ident), reads=["y1", "identB"], writes=[B(6)])
            if c % 4 == 3:
                yield
        P.op("act", lambda e: e.activation(out=y1T.rearrange("p c t -> p (c t)"), in_=tb6, func=AF.Copy), reads=[B(6)], writes=["y1T"])
        yield
        for hh in range(2):
            for kc in range(8):
                P.op("pe", lambda e, hh=hh, kc=kc: e.matmul(banks[4 + hh], lhsT=y1T[:, kc, :], rhs=Wo1[:, kc, hh * 512:(hh + 1) * 512], start=(kc == 0), stop=(kc == 7)),
                     reads=["y1T", "Wo1"], writes=[B(4 + hh)])
                if kc % 4 == 3:
                    yield
        os_ = j % 2
        OB = ob[os_]
        obk = ("ob", os_)
        for hh in range(2):
            P.op("dve", lambda e, hh=hh: e.scalar_tensor_tensor(out=OB[:, hh * 512:(hh + 1) * 512], in0=X1[:, hh * 512:(hh + 1) * 512], scalar=ALPHA, in1=banks[4 + hh], op0=ALU.mult, op1=ALU.add),
                 reads=[x1k, B(4 + hh)], writes=[obk])
            yield
        yield from layer_norm(OB, obk, 2, 1)
        out_dmas.append(P.dma("sp", out[(j - 1) * 128:j * 128, :], OB, reads=[obk], chan=("ob", os_)))
        yield

    gens = [blk(j) for j in range(NBK)]
    older = None
    younger = None
    paused = False
    nxt = 0
    DONE = object()
    while True:
        if younger is None and nxt < NBK:
            younger = gens[nxt]
            nxt += 1
            paused = False
        if older is None and younger is None:
            break
        if older is None and paused:
            older, younger, paused = younger, None, False
            continue
        for _ in range(R_OLD):
            if older is not None:
                if next(older, DONE) is DONE:
                    older = None
        for _ in range(R_YOUNG):
            if younger is not None and not paused:
                r = next(younger, DONE)
                if r is DONE:
                    younger = None
                elif r == "S2":
                    paused = True
    return out_dmas


import ml_dtypes
from concourse.bass_utils import run_bass_kernel_spmd

S_FULL = 8192
_CACHE = {}


def _consts_A():
    p = np.arange(128)[:, None]; f = np.arange(512)[None, :]
    cmask = np.zeros((128, 2048), np.float32)
    for r in range(4):
        cmask[:, r * 512:(r + 1) * 512] = np.where(128 * r + p < f, 0.0, -30000.0)
    j = np.arange(128)[:, None]; s = np.arange(128)[None, :]
    ident = np.eye(128, dtype=np.float32)
    trineg = np.where(j >= s, -1.0, 0.0).astype(np.float32)
    restneg = np.where(j < s, -1.0, 0.0).astype(np.float32)
    return cmask, np.concatenate([ident, trineg, restneg], axis=1)


def _consts_B():
    k = np.arange(128)[:, None]; q = np.arange(128)[None, :]
    slopes = np.array([2.0 ** (-8.0 * (i + 1) / 16) for i in range(16)], np.float32)
    ab = np.zeros((128, 16, 2, 128), np.float32)
    for idx in range(16):
        g, jj = idx // 4, idx % 4
        c, par = g // 2, g % 2
        h = 8 * c + 2 * jj + par
        for half in range(2):
            dist = (q - k + 128).astype(np.float32) if half == 0 else (q - k).astype(np.float32)
            valid = (dist >= 0) & (dist < 128)
            ab[:, idx, half, :] = np.where(valid, -slopes[h] * dist, -30000.0)
    ab0 = ab.copy(); ab0[:, :, 0, :] = -30000.0
    return ab.reshape(128, 4096), ab0.reshape(128, 4096)


def _sinks_group_order(s16):
    out = np.zeros(16, np.float32)
    for idx in range(16):
        g, jj = idx // 4, idx % 4
        c, par = g // 2, g % 2
        out[idx] = s16[8 * c + 2 * jj + par]
    return out


def _shared_inputs(inp):
    w_in = inp["e_w_in"][0]
    w_sb = np.zeros((8, 1024, 512), np.float32)
    for h in range(8):
        for k, base in enumerate([2048, 3072, 4096, 5120]):
            w_sb[h, :, k * 128:(k + 1) * 128] = w_in[:, base + h * 128: base + (h + 1) * 128]
    w_lru = np.stack([np.concatenate([w_in[:, 512 * q:512 * q + 512], w_in[:, 1024 + 512 * q:1024 + 512 * q + 512]], axis=1) for q in range(2)])
    conv_w = np.stack([np.ascontiguousarray(inp["e_conv_w"][0][:, 512 * q:512 * q + 512].T) for q in range(2)])
    vecs = np.stack([np.stack([inp["e_conv_b"][0][512 * q:512 * q + 512], inp["e_b_gate_a"][0][512 * q:512 * q + 512],
                               inp["e_b_gate_x"][0][512 * q:512 * q + 512], inp["e_lru_lambda"][0][512 * q:512 * q + 512]], axis=1) for q in range(2)])
    cmask, cmats = _consts_A()
    w = inp["o_w_in"][0]
    q = w[:, 0:1024]; k = w[:, 1024:1152]; v = w[:, 1152:1280]; gg = w[:, 1280:2304]
    z64 = np.zeros((1024, 64), np.float32)
    w_i1 = np.concatenate([q, k[:, 0:64], z64, z64, k[:, 0:64], k[:, 64:128], z64, z64, k[:, 64:128], v, gg], axis=1)
    lnp = np.concatenate([inp["e_ln_g"][0], inp["e_ln_b"][0], inp["o_ln_g"][0], inp["o_ln_b"][0]])[None, :].repeat(128, 0)
    sinks = _sinks_group_order(inp["o_sinks"][0])[None, :].repeat(128, 0)
    ab, ab0 = _consts_B()
    return {"w_sb": w_sb, "w_lru": np.ascontiguousarray(w_lru), "conv_w": np.ascontiguousarray(conv_w), "vecs": np.ascontiguousarray(vecs),
            "wga": np.ascontiguousarray(inp["e_w_gate_a"][0].reshape(2, 4, 128, 128)), "wgx": np.ascontiguousarray(inp["e_w_gate_x"][0].reshape(2, 4, 128, 128)),
            "cmask": cmask, "cmats": cmats, "w_o0": np.ascontiguousarray(inp["e_w_out"][0]), "w_i1": np.ascontiguousarray(w_i1),
            "w_o1": np.ascontiguousarray(inp["o_w_out"][0]), "lnp": np.ascontiguousarray(lnp), "sinks": np.ascontiguousarray(sinks),
            "abias": ab, "ident": np.eye(128, dtype=np.float32)}, ab0


def _core_inputs(inp, shared, ab0, b, g, S):
    HALF = S // 2
    x = inp["x"][b, :S]
    xw = np.zeros((S, 1024), np.float32)
    if g == 0:
        xw[HALF:] = x[:HALF]
    else:
        xw[:] = x
    kmask = np.zeros((128, S // 512), np.float32)
    if g == 0:
        kmask[:, :S // 1024] = -30000.0
    x_in = np.zeros((HALF + 128, 1024), np.float32)
    t0 = g * HALF - 128
    lo = max(t0, 0)
    x_in[lo - t0:] = x[lo:t0 + HALF + 128]
    d = dict(shared)
    d.update({"xT": np.ascontiguousarray(xw.T), "kmask": kmask, "ctxf": np.full((128, 1), float(g), np.float32),
              "x_in": x_in, "abias0": ab0 if g == 0 else shared["abias"]})
    return d


def kernel(**inputs):
    inp = {k: np.asarray(v, dtype=np.float32) for k, v in inputs.items()}
    S = S_FULL
    cores = [(b, g) for b in range(4) for g in range(2)]
    if "F" not in _CACHE:
        _CACHE["F"] = build_F(S)
    shared, ab0 = _shared_inputs(inp)
    res = run_bass_kernel_spmd(_CACHE["F"], [_core_inputs(inp, shared, ab0, b, g, S) for b, g in cores], core_ids=list(range(8)))
    out = np.zeros((4, S, 1024), np.float32)
    for ci, (b, g) in enumerate(cores):
        out[b, g * (S // 2):(g + 1) * (S // 2)] = res.results[ci]["out"]
    return out
```

```python
import numpy as np
from contextlib import ExitStack
import concourse.bass as bass
import concourse.mybir as mybir

F32 = mybir.dt.float32
BF16 = mybir.dt.bfloat16
AF = mybir.ActivationFunctionType
ALU = mybir.AluOpType
AX = mybir.AxisListType

ENGS = ("pe", "act", "dve", "pool", "sp")
SAME_ENGINE_SYNC = True


class Op:
    __slots__ = ("eng", "fn", "deps", "needs_inc", "inc_val", "dma_sem", "dma_val", "idx", "is_dma")

    def __init__(self, eng, fn, is_dma=False, dma_sem=None):
        self.eng = eng
        self.fn = fn
        self.deps = []
        self.needs_inc = False
        self.inc_val = None
        self.is_dma = is_dma
        self.dma_sem = dma_sem
        self.dma_val = None


class Prog:
    def __init__(self, nc):
        self.nc = nc
        self.ops = {e: [] for e in ENGS}
        self.res = {}
        self.stack = ExitStack()
        self.esem = {}
        self.dma_chan = {}
        self.nsem = 0
        self.chan_last = {}

    def new_sem(self, name):
        self.nsem += 1
        return self.stack.enter_context(self.nc.semaphore(name))

    def chan(self, key):
        if key not in self.dma_chan:
            self.dma_chan[key] = [self.new_sem("dc%d" % len(self.dma_chan)), 0]
        return self.dma_chan[key]

    def _rec(self, o, reads, writes):
        deps = []
        for r in reads:
            st = self.res.setdefault(r, [None, []])
            if st[0] is not None:
                deps.append(st[0])
        for w in writes:
            st = self.res.setdefault(w, [None, []])
            if st[0] is not None:
                deps.append(st[0])
            deps.extend(st[1])
        for r in reads:
            self.res[r][1].append(o)
        for w in writes:
            st = self.res[w]
            st[0] = o
            st[1] = []
        seen = set()
        for d in deps:
            if d is o or id(d) in seen:
                continue
            seen.add(id(d))
            if (not d.is_dma) and d.eng == o.eng and (d.eng == "pe" or not SAME_ENGINE_SYNC):
                continue
            o.deps.append(d)
            if not d.is_dma:
                d.needs_inc = True
        self.ops[o.eng].append(o)
        return o

    def op(self, eng, fn, reads=(), writes=()):
        return self._rec(Op(eng, fn), reads, writes)

    def dma(self, eng, out, in_, reads=(), writes=(), chan=None, **kw):
        c = self.chan(chan)
        c[1] += 16
        o = Op(eng, lambda e, out=out, in_=in_, kw=kw: e.dma_start(out=out, in_=in_, **kw), is_dma=True, dma_sem=c[0])
        o.dma_val = c[1]
        self.chan_last[chan] = o
        return self._rec(o, reads, writes)

    def barrier(self):
        lasts = []
        for e in ENGS:
            for o in reversed(self.ops[e]):
                if not o.is_dma:
                    o.needs_inc = True
                    lasts.append(o)
                    break
        dmas = list(self.chan_last.values())
        for e in ENGS:
            b = Op(e, lambda eng: None)
            b.deps = [o for o in lasts if o.eng != e] + dmas
            self.ops[e].append(b)

    def emit(self):
        nc = self.nc
        for e in ENGS:
            self.esem[e] = self.new_sem("es_" + e)
            cnt = 0
            for o in self.ops[e]:
                if o.needs_inc and not o.is_dma:
                    cnt += 1
                    o.inc_val = cnt
        esem = self.esem

        def run(e, engobj):
            waited = {}
            for o in self.ops[e]:
                for d in o.deps:
                    if d.is_dma:
                        sem, val = d.dma_sem, d.dma_val
                    else:
                        sem, val = esem[d.eng], d.inc_val
                    k = id(sem)
                    if waited.get(k, 0) < val:
                        engobj.wait_ge(sem, val)
                        waited[k] = val
                inst = o.fn(engobj)
                if o.is_dma:
                    inst.then_inc(o.dma_sem, 16)
                elif o.needs_inc:
                    inst.then_inc(esem[e], 1)

        with nc.Block() as block:
            @block.tensor
            def _(eng):
                run("pe", eng)

            @block.scalar
            def _(eng):
                run("act", eng)

            @block.vector
            def _(eng):
                run("dve", eng)

            @block.gpsimd
            def _(eng):
                run("pool", eng)

            @block.sync
            def _(eng):
                run("sp", eng)
        self.stack.close()

POOLENG = 'dve'
LWIN = 3
POOL3 = False


def interleave(gens, window=2):
    gens = list(gens)
    active = []
    nxt = 0
    DONE = object()
    while active or nxt < len(gens):
        while len(active) < window and nxt < len(gens):
            active.append(gens[nxt])
            nxt += 1
        for g in list(active):
            if next(g, DONE) is DONE:
                assert active[0] is g, "generators must finish in admission order (slot reuse safety)"
                active.remove(g)


def record_A(nc, P, stk, cfg, xT, w_sb, w_lru, conv_w, vecs, wga, wgx, cmask, cmats, kmask_d, ctxf_d, ydst):
    NTW, OWN, HALO = cfg["NTW"], cfg["own_from"], cfg["halo"]
    NH, NG = cfg["n_heads"], cfg["n_lru_groups"]
    ycol = cfg["ycol"]
    yrow_a, yrow_b = cfg["yrow_a"], cfg["yrow_b"]
    S = NTW * 512
    NB = S // 128

    def sb(name, shape, dt=F32):
        return stk.enter_context(nc.sbuf_tensor(name, shape, dt)).ap()
    banks = [stk.enter_context(nc.psum_tensor("bank%d" % i, [128, 512], F32)).ap() for i in range(8)]
    B = lambda i: ("bank", i)

    cm = sb("cm", [128, 2048], BF16)
    cmt = sb("cmt", [128, 384], BF16)
    P.dma("pool", cm, cmask, writes=["cm"], chan="cm")
    P.dma("pool", cmt, cmats, writes=["cmt"], chan="cmt")
    ident = cmt[:, 0:128]
    trineg = cmt[:, 128:256]
    restneg = cmt[:, 256:384]
    kmask = sb("kmask_s", [128, NTW])
    P.dma("sp", kmask, kmask_d, writes=["kmask"], chan="kmask")
    ctxf = sb("ctxf_s", [128, 1])
    P.dma("sp", ctxf, ctxf_d, writes=["ctxf"], chan="ctxf")

    out_dmas = []
    lru_hook = [None]
    xT_v = xT.rearrange("(c p) t -> p c t", p=128)
    NXS = 4
    xts = [sb("xt%d" % i, [128, 8, 512], BF16) for i in range(NXS)]
    xcnt = [0]

    def load_x(i):
        s = xcnt[0] % NXS
        xcnt[0] += 1
        P.dma("pool", xts[s], xT_v[:, :, i * 512:(i + 1) * 512], writes=[("xt", s)], chan=("xt", s))
        return s

    Wt = [sb("Wt%d" % i, [128, 8, 512], BF16) for i in range(2)]
    kT = sb("kT0", [128, S], BF16)
    V = sb("V0", [128, NB, 128], BF16)
    qTb = [sb("qT%d" % i, [128, 512], BF16) for i in range(3)]
    sgb = [sb("sg%d" % i, [128, 512], F32) for i in range(3)]
    NE, NL, NE2, NW = 4, 4, 3, 3
    eb = [sb("e%d" % i, [128, 512], F32) for i in range(NE)]
    Lb = [sb("L%d" % i, [128, 512], BF16) for i in range(NL)]
    E2b = [sb("E2%d" % i, [128, 512], F32) for i in range(NE2)]
    wb = [sb("w%d" % i, [128, 512], BF16) for i in range(NW)]
    yb = [sb("yb%d" % i, [128, 512], BF16) for i in range(3)]
    vTs = sb("vTs", [128, 512], BF16)
    SCALE = 1.0 / np.sqrt(128.0)
    ZB = [0, 1]
    CB = 2
    OB = [3, 3]
    IB = [5, 6]
    ibc = [0]

    def next_ib():
        b = IB[ibc[0] % len(IB)]
        ibc[0] += 1
        return b

    def sb_pass(hl):
        ws = hl % 2
        W = Wt[ws]
        P.dma("pool", W, w_sb[hl].rearrange("(c p) n -> p c n", p=128), writes=[("W", ws)], chan=("W", ws))
        xslot = {}
        first_q = OWN - 1 if HALO else OWN
        cons = ([OWN - 1] + list(range(OWN - 2, -1, -1)) if HALO else []) + list(range(OWN, NTW))
        cpos = {t: k for k, t in enumerate(cons)}
        lptr = [0]

        def ensure_loaded(t):
            while lptr[0] <= min(cpos[t] + 1, len(cons) - 1):
                tt = cons[lptr[0]]
                xslot[tt] = load_x(tt)
                lptr[0] += 1

        inproj_done = set()

        def inproj_groups(i):
            xs = xslot[i]
            x = xts[xs]
            qs = i % 3

            def mm_part(b_hold, col0, part, first):
                def f():
                    if first:
                        b_hold[0] = next_ib()
                    b = b_hold[0]
                    for c in range(part * 4, part * 4 + 4):
                        P.op("pe", lambda e, c=c, b=b: e.matmul(banks[b], lhsT=W[:, c, col0:col0 + 128], rhs=x[:, c, :], start=(c == 0), stop=(c == 7)),
                             reads=[("xt", xs), ("W", ws)], writes=[B(b)])
                return f

            bq, bk, bv, bg = [None], [None], [None], [None]

            def q_fin():
                mm_part(bq, 0, 1, False)()
                b = bq[0]
                P.op("dve", lambda e, b=b: e.tensor_scalar(out=qTb[qs], in0=banks[b], scalar1=float(SCALE), scalar2=None, op0=ALU.mult),
                     reads=[B(b)], writes=[("qT", qs)])

            def k_fin():
                mm_part(bk, 128, 1, False)()
                b = bk[0]
                P.op("dve", lambda e, b=b: e.tensor_copy(out=kT[:, i * 512:(i + 1) * 512], in_=banks[b]),
                     reads=[B(b)], writes=[("kT", i)])

            def v_fin():
                mm_part(bv, 256, 1, False)()
                b = bv[0]
                P.op("dve", lambda e, b=b: e.tensor_copy(out=vTs, in_=banks[b]), reads=[B(b)], writes=["vTs"])

            def g_v2():
                b2 = next_ib()
                tb2 = banks[b2].bitcast(BF16)
                for j in range(4):
                    P.op("pe", lambda e, j=j, tb2=tb2: e.transpose(tb2[:, j * 128:(j + 1) * 128], vTs[:, j * 128:(j + 1) * 128], ident), reads=["vTs", "cmt"], writes=[B(b2)])
                P.op("dve", lambda e, tb2=tb2: e.tensor_copy(out=V[:, 4 * i:4 * i + 4, :], in_=tb2[:, 0:512].rearrange("p (j d) -> p j d", j=4)),
                     reads=[B(b2)], writes=[("V", i)])

            def g_fin():
                mm_part(bg, 384, 1, False)()
                b = bg[0]
                S_ = sgb[qs]
                P.op("act", lambda e, b=b: e.activation(out=S_, in_=banks[b], func=AF.Exp, scale=-1.0), reads=[B(b)], writes=[("sg", qs)])
                P.op("dve", lambda e: e.tensor_scalar_add(out=S_, in0=S_, scalar1=1.0), reads=[("sg", qs)], writes=[("sg", qs)])
                P.op("dve", lambda e: e.reciprocal(out=S_, in_=S_), reads=[("sg", qs)], writes=[("sg", qs)])
                P.op("dve", lambda e, b=b: e.tensor_tensor(out=S_, in0=S_, in1=banks[b], op=ALU.mult), reads=[("sg", qs), B(b)], writes=[("sg", qs)])

            def done_mark(f):
                def g():
                    f()
                    inproj_done.add(i)
                return g

            v_items = [mm_part(bv, 256, 0, True), v_fin]
            k_items = [mm_part(bk, 128, 0, True), k_fin]
            if i >= first_q:
                return v_items + [mm_part(bq, 0, 0, True), q_fin, g_v2] + k_items + [mm_part(bg, 384, 0, True), done_mark(g_fin)]
            return v_items + k_items + [done_mark(g_v2)]

        ensure_loaded(cons[0])
        for g in inproj_groups(cons[0]):
            g()
        pend = list(cons[1:OWN + 1]) if HALO else []
        queued = set(pend) | {cons[0]}

        tasks = []
        if HALO:
            ih = OWN - 1
            tasks += [(ih, kb, 384, 128) for kb in reversed(range(4 * ih + 4))]
        for i in range(OWN, NTW):
            tasks += [(i, kb, 0, 512) for kb in reversed(range(4 * i + 4))]
        NTK = len(tasks)
        side = []

        def first_of_tile(n):
            i, kb, c0, qw = tasks[n]
            return kb == 4 * i + 3

        def last_of_tile(n):
            return tasks[n][1] == 0

        for s in range(NTK + 5):
            if s < NTK:
                i, kb, c0, qw = tasks[s]
                assert (kb // 4) in inproj_done and i in inproj_done, ("in-proj not emitted before use", i, kb)
                if first_of_tile(s):
                    ni = i + 1
                    if ni < NTW and ni > first_q and ni not in queued:
                        queued.add(ni)
                        pend.append(ni)
                zb = ZB[s % 2]
                diag = kb >= 4 * i
                qs = i % 3
                P.op("pe", lambda e, zb=zb, kb=kb, qs=qs, diag=diag, c0=c0, qw=qw: e.matmul(banks[zb][:, 0:qw], lhsT=kT[:, kb * 128:(kb + 1) * 128], rhs=qTb[qs][:, c0:c0 + qw], start=True, stop=not diag),
                     reads=[("kT", kb // 4), ("qT", qs)], writes=[B(zb)])
                if diag:
                    r = kb - 4 * i
                    P.op("pe", lambda e, zb=zb, r=r, c0=c0, qw=qw: e.matmul(banks[zb][:, 0:qw], lhsT=ident, rhs=cm[:, r * 512 + c0:r * 512 + c0 + qw], start=False, stop=True),
                         reads=["cm", "cmt"], writes=[B(zb)])
            n = s - 3
            if 0 <= n < NTK:
                qw = tasks[n][3]
                P.op("act", lambda e, n=n, qw=qw: e.activation(out=E2b[n % NE2][:, 0:qw], in_=banks[CB][:, 0:qw], func=AF.Exp),
                     reads=[B(CB)], writes=[("E2", n % NE2)])
            n = s - 1
            if 0 <= n < NTK:
                zb = ZB[n % 2]
                kb, qw = tasks[n][1], tasks[n][3]
                P.op("act", lambda e, n=n, zb=zb, kb=kb, qw=qw: e.activation(out=eb[n % NE][:, 0:qw], in_=banks[zb][:, 0:qw], func=AF.Exp, bias=kmask[:, kb // 4:kb // 4 + 1]),
                     reads=[B(zb), "kmask"], writes=[("e", n % NE)])
                P.op("act", lambda e, n=n, qw=qw: e.activation(out=Lb[n % NL][:, 0:qw], in_=eb[n % NE][:, 0:qw], func=AF.Ln, bias=1.0),
                     reads=[("e", n % NE)], writes=[("L", n % NL)])
            n = s - 3
            if 0 <= n < NTK and not last_of_tile(n):
                qw = tasks[n][3]
                P.op("pe", lambda e, n=n, qw=qw: e.matmul(banks[CB][:, 0:qw], lhsT=restneg, rhs=Lb[n % NL][:, 0:qw], start=False, stop=True, skip_group_check=True),
                     reads=[("L", n % NL), "cmt"], writes=[B(CB)])
            n = s - 2
            if 0 <= n < NTK:
                qw = tasks[n][3]
                P.op("pe", lambda e, n=n, st=first_of_tile(n), qw=qw: e.matmul(banks[CB][:, 0:qw], lhsT=trineg, rhs=Lb[n % NL][:, 0:qw], start=st, stop=True, skip_group_check=True),
                     reads=[("L", n % NL), "cmt"], writes=[B(CB)])
            n = s - 3
            if 0 <= n < NTK:
                qw = tasks[n][3]
                P.op("dve", lambda e, n=n, qw=qw: e.tensor_tensor(out=wb[n % NW][:, 0:qw], in0=eb[n % NE][:, 0:qw], in1=E2b[n % NE2][:, 0:qw], op=ALU.mult),
                     reads=[("e", n % NE), ("E2", n % NE2)], writes=[("w", n % NW)])
            n = s - 4
            if 0 <= n < NTK:
                i, kb, c0, qw = tasks[n]
                ob = OB[i % 2]
                P.op("pe", lambda e, n=n, kb=kb, ob=ob, st=first_of_tile(n), sp=last_of_tile(n), qw=qw: e.matmul(banks[ob][:, 0:qw], lhsT=V[:, kb, :], rhs=wb[n % NW][:, 0:qw], start=st, stop=sp),
                     reads=[("V", kb // 4), ("w", n % NW)], writes=[B(ob)])
                if last_of_tile(n):
                    qs = i % 3
                    P.op("dve", lambda e, ob=ob, qs=qs, c0=c0, qw=qw: e.tensor_tensor(out=yb[qs][:, 0:qw], in0=banks[ob][:, 0:qw], in1=sgb[qs][:, c0:c0 + qw], op=ALU.mult),
                         reads=[B(ob), ("sg", qs)], writes=[("yb", qs)])
                    row0 = yrow_b + hl * 128
                    col0 = 0 if qw == 128 else ycol(i)
                    out_dmas.append(P.dma("sp", ydst[row0:row0 + 128, col0:col0 + qw], yb[qs][:, 0:qw], reads=[("yb", qs)], chan=("yb", qs)))
            in_halo = HALO and s < 32
            npop = 2 if in_halo else (1 if s % 3 == 0 else 0)
            for _ in range(npop):
                if not side and pend:
                    ti = pend.pop(0)
                    ensure_loaded(ti)
                    side.extend(inproj_groups(ti))
                if side:
                    side.pop(0)()
            lru_hook[0](0 if s % 24 == 23 else 1)
        while side or pend:
            if not side:
                ti = pend.pop(0)
                ensure_loaded(ti)
                side.extend(inproj_groups(ti))
            side.pop(0)()

    Wl = sb("Wl", [128, 8, 1024], BF16)
    Wa = sb("Wa", [128, 4, 128], BF16)
    Wx = sb("Wx", [128, 4, 128], BF16)
    cw = sb("cw", [128, 4, 4])
    vc = sb("vc", [128, 4, 4])
    nb = sb("nb", [128, 4, 2])
    cch = sb("cch", [128, 4, 3])
    axbuf = [sb("axbuf%d" % c, [128, 515]) for c in range(4)]
    hst = sb("hst", [128, 4])
    T = {}
    for nm, dt in [("xc", F32), ("xcb", BF16), ("er", F32), ("ei", F32), ("a", F32), ("a2", F32), ("u", F32), ("h", F32), ("eg", F32), ("yl", BF16)]:
        T[nm] = [sb("l_%s%d" % (nm, k), [128, 512], dt) for k in range(2)]
    xtl = [sb("xtl%d" % i, [128, 8, 512], BF16) for i in range(2)]
    xlcnt = [0]

    def load_xl(i):
        s = xlcnt[0] % 2
        xlcnt[0] += 1
        P.dma("pool", xtl[s], xT_v[:, :, i * 512:(i + 1) * 512], writes=[("xtl", s)], chan=("xtl", s))
        return s


    def lru_pass(grp):
        yield
        P.dma("pool", Wl, w_lru[grp].rearrange("(c p) n -> p c n", p=128), writes=["Wl"], chan="Wl")
        P.dma("pool", Wa, wga[grp].rearrange("n d e -> d n e"), writes=["Wa"], chan="Wa")
        P.dma("pool", Wx, wgx[grp].rearrange("n d e -> d n e"), writes=["Wx"], chan="Wx")
        P.dma("sp", cw, conv_w[grp].rearrange("(c p) k -> p c k", p=128), writes=["cw"], chan="cw")
        P.dma("sp", vc, vecs[grp].rearrange("(c p) k -> p c k", p=128), writes=["vc"], chan="vc")
        P.op("dve", lambda e: e.tensor_scalar(out=nb, in0=vc[:, :, 1:3], scalar1=-1.0, scalar2=None, op0=ALU.mult), reads=["vc"], writes=["nb"])
        P.op("act", lambda e: e.activation(out=cch[:, :, 2:3], in_=vc[:, :, 3:4], func=AF.Exp, scale=-1.0), reads=["vc"], writes=["cch"])
        P.op("act", lambda e: e.activation(out=cch[:, :, 2:3], in_=cch[:, :, 2:3], func=AF.Ln, bias=1.0), reads=["cch"], writes=["cch"])
        P.op("dve", lambda e: e.tensor_scalar(out=cch[:, :, 0:1], in0=cch[:, :, 2:3], scalar1=-8.0, scalar2=None, op0=ALU.mult), reads=["cch"], writes=["cch"])
        P.op("dve", lambda e: e.tensor_scalar(out=cch[:, :, 1:2], in0=cch[:, :, 2:3], scalar1=-16.0, scalar2=None, op0=ALU.mult), reads=["cch"], writes=["cch"])
        for c in range(4):
            P.op("pool", lambda e, c=c: e.memset(axbuf[c], 0.0), writes=[("axbuf", c)])
        P.op("pool", lambda e: e.memset(hst, 0.0), writes=[("hst", c) for c in range(4)])
        yield
        xslot = {0: load_xl(0)}
        first_y = OWN - 1 if HALO else OWN

        def chunk_gen(i, c, k):
            if c == 0 and i + 1 < NTW:
                xslot[i + 1] = load_xl(i + 1)
            xs = xslot[i]
            x = xtl[xs]
            need_y = i >= first_y
            is_ctx = i < OWN
            bax, bag, br, bi = 4, 7, 4, 7
            R = lambda nm: (nm, k)
            ab = axbuf[c]
            xc, xcb, er, ei, a, a2, u, h, eg, yl = [T[nm][k] for nm in ("xc", "xcb", "er", "ei", "a", "a2", "u", "h", "eg", "yl")]
            for kc in range(8):
                P.op("pe", lambda e, kc=kc: e.matmul(banks[bax], lhsT=Wl[:, kc, c * 128:(c + 1) * 128], rhs=x[:, kc, :], start=(kc == 0), stop=(kc == 7)),
                     reads=[("xtl", xs), "Wl"], writes=[B(bax)])
            yield
            P.op("dve", lambda e: e.tensor_copy(out=ab[:, 3:515], in_=banks[bax]), reads=[B(bax)], writes=[("axbuf", c)])
            yield
            if POOL3:
                P.op("pool", lambda e: e.tensor_scalar(out=xc, in0=ab[:, 3:515], scalar1=cw[:, c, 3:4], scalar2=vc[:, c, 0:1], op0=ALU.mult, op1=ALU.add),
                     reads=[("axbuf", c), "cw", "vc"], writes=[R("xc")])
            else:
                P.op("dve", lambda e: e.tensor_scalar(out=xc, in0=ab[:, 3:515], scalar1=cw[:, c, 3:4], scalar2=vc[:, c, 0:1], op0=ALU.mult, op1=ALU.add),
                     reads=[("axbuf", c), "cw", "vc"], writes=[R("xc")])
            yield
            for tap in (2, 1, 0):
                P.op("dve", lambda e, tap=tap: e.scalar_tensor_tensor(out=xc, in0=ab[:, tap:tap + 512], scalar=cw[:, c, tap:tap + 1], in1=xc, op0=ALU.mult, op1=ALU.add),
                     reads=[("axbuf", c), "cw", R("xc")], writes=[R("xc")])
                yield
            if POOL3:
                P.op("pool", lambda e: e.tensor_copy(out=xcb, in_=xc), reads=[R("xc")], writes=[R("xcb")])
                P.op("pool", lambda e: e.tensor_copy(out=ab[:, 0:3], in_=ab[:, 512:515]), reads=[("axbuf", c)], writes=[("axbuf", c)])
            else:
                P.op("dve", lambda e: e.tensor_copy(out=xcb, in_=xc), reads=[R("xc")], writes=[R("xcb")])
                P.op("dve", lambda e: e.tensor_copy(out=ab[:, 0:3], in_=ab[:, 512:515]), reads=[("axbuf", c)], writes=[("axbuf", c)])
            yield
            P.op("pe", lambda e: e.matmul(banks[br], lhsT=Wa[:, c, :], rhs=xcb, start=True, stop=True), reads=["Wa", R("xcb")], writes=[B(br)])
            P.op("pe", lambda e: e.matmul(banks[bi], lhsT=Wx[:, c, :], rhs=xcb, start=True, stop=True), reads=["Wx", R("xcb")], writes=[B(bi)])
            yield
            P.op("act", lambda e: e.activation(out=er, in_=banks[br], func=AF.Exp, scale=-1.0, bias=nb[:, c, 0:1]), reads=[B(br), "nb"], writes=[R("er")])
            yield
            P.op("act", lambda e: e.activation(out=ei, in_=banks[bi], func=AF.Exp, scale=-1.0, bias=nb[:, c, 1:2]), reads=[B(bi), "nb"], writes=[R("ei")])
            yield
            P.op(POOLENG, lambda e: e.tensor_scalar_add(out=er, in0=er, scalar1=1.0), reads=[R("er")], writes=[R("er")])
            yield
            P.op("dve", lambda e: e.reciprocal(out=er, in_=er), reads=[R("er")], writes=[R("er")])
            yield
            P.op("act", lambda e: e.activation(out=a, in_=er, func=AF.Exp, scale=cch[:, c, 0:1]), reads=[R("er"), "cch"], writes=[R("a")])
            yield
            P.op("dve", lambda e: e.tensor_tensor(out=a2, in0=a, in1=a, op=ALU.mult), reads=[R("a")], writes=[R("a2")])
            yield
            P.op("act", lambda e: e.activation(out=a2, in_=a2, func=AF.Ln, scale=-1.0, bias=1.0), reads=[R("a2")], writes=[R("a2")])
            yield
            P.op("act", lambda e: e.activation(out=a2, in_=a2, func=AF.Exp, scale=0.5), reads=[R("a2")], writes=[R("a2")])
            yield
            P.op(POOLENG, lambda e: e.tensor_scalar_add(out=ei, in0=ei, scalar1=1.0), reads=[R("ei")], writes=[R("ei")])
            yield
            P.op(POOLENG, lambda e: e.tensor_tensor(out=u, in0=xc, in1=a2, op=ALU.mult), reads=[R("xc"), R("a2")], writes=[R("u")])
            yield
            P.op("dve", lambda e: e.reciprocal(out=ei, in_=ei), reads=[R("ei")], writes=[R("ei")])
            yield
            if is_ctx:
                P.op("dve", lambda e: e.scalar_tensor_tensor(out=u, in0=u, scalar=ctxf[:, 0:1], in1=ei, op0=ALU.mult, op1=ALU.mult), reads=[R("u"), R("ei"), "ctxf"], writes=[R("u")])
            else:
                P.op("dve", lambda e: e.tensor_tensor(out=u, in0=u, in1=ei, op=ALU.mult), reads=[R("u"), R("ei")], writes=[R("u")])
            yield
            P.op("dve", lambda e: e.tensor_tensor_scan(out=h, data0=a, data1=u, initial=hst[:, c:c + 1], op0=ALU.mult, op1=ALU.add),
                 reads=[R("a"), R("u"), ("hst", c)], writes=[R("h")])
            yield
            P.op("dve", lambda e: e.tensor_copy(out=hst[:, c:c + 1], in_=h[:, 511:512]), reads=[R("h")], writes=[("hst", c)])
            yield
            if need_y:
                for kc in range(8):
                    P.op("pe", lambda e, kc=kc: e.matmul(banks[bag], lhsT=Wl[:, kc, 512 + c * 128:512 + (c + 1) * 128], rhs=x[:, kc, :], start=(kc == 0), stop=(kc == 7)),
                         reads=[("xtl", xs), "Wl"], writes=[B(bag)])
                yield
                P.op("act", lambda e: e.activation(out=eg, in_=banks[bag], func=AF.Exp, scale=-1.0), reads=[B(bag)], writes=[R("eg")])
                yield
                P.op(POOLENG, lambda e: e.tensor_scalar_add(out=eg, in0=eg, scalar1=1.0), reads=[R("eg")], writes=[R("eg")])
                yield
                P.op("dve", lambda e: e.reciprocal(out=eg, in_=eg), reads=[R("eg")], writes=[R("eg")])
                yield
                P.op("dve", lambda e: e.tensor_tensor(out=eg, in0=banks[bag], in1=eg, op=ALU.mult), reads=[R("eg"), B(bag)], writes=[R("eg")])
                yield
                P.op("dve", lambda e: e.tensor_tensor(out=yl, in0=h, in1=eg, op=ALU.mult), reads=[R("eg"), R("h")], writes=[R("yl")])
                yield
                row0 = yrow_a + grp * 512 + c * 128
                if i >= OWN:
                    out_dmas.append(P.dma("sp", ydst[row0:row0 + 128, ycol(i):ycol(i) + 512], yl, reads=[R("yl")], chan=("yl", k)))
                else:
                    out_dmas.append(P.dma("sp", ydst[row0:row0 + 128, 0:128], yl[:, 384:512], reads=[R("yl")], chan=("yl", k)))
                yield

        it = 0
        for i in range(NTW):
            for c in range(4):
                yield from chunk_gen(i, c, it % 2)
                it += 1

    def lru_master():
        for grp in range(NG):
            yield from lru_pass(grp)

    lru_gen = lru_master()
    lru_done = [False]

    def lru_advance(nsteps):
        for _ in range(nsteps):
            if lru_done[0]:
                return
            if next(lru_gen, "DONE") == "DONE":
                lru_done[0] = True

    lru_hook[0] = lru_advance
    for hl in range(NH):
        sb_pass(hl)
    while not lru_done[0]:
        lru_advance(64)
    return out_dmas


def build_F(S):
    nc = bass.Bass("TRN2", target_bir_lowering=False)
    NB = S // 128
    HALF = S // 2
    TOKB = HALF + 128
    dr = lambda name, shape, dt=F32, kind="ExternalInput": nc.dram_tensor(name, shape, dt, kind=kind).ap()
    xT = dr("xT", [1024, S])
    w_sb = dr("w_sb", [8, 1024, 512])
    w_lru = dr("w_lru", [2, 1024, 1024])
    conv_w = dr("conv_w", [2, 512, 4])
    vecs = dr("vecs", [2, 512, 4])
    wga = dr("wga", [2, 4, 128, 128])
    wgx = dr("wgx", [2, 4, 128, 128])
    cmask = dr("cmask", [128, 2048])
    cmats = dr("cmats", [128, 384])
    kmask = dr("kmask", [128, S // 512])
    ctxf = dr("ctxf", [128, 1])
    x_in = dr("x_in", [TOKB, 1024])
    w_o0 = dr("w_o0", [2048, 1024])
    w_i1 = dr("w_i1", [1024, 2688])
    w_o1 = dr("w_o1", [1024, 1024])
    lnp = dr("lnp", [128, 4096])
    sinks = dr("sinks", [128, 16])
    abias_d = dr("abias", [128, 4096])
    abias0_d = dr("abias0", [128, 4096])
    ident_d = dr("ident", [128, 128])
    out = dr("out", [HALF, 1024], F32, "ExternalOutput")
    yint = nc.dram_tensor("yint", [2048, TOKB], BF16).ap()
    NTW = S // 512
    OWN = NTW // 2
    P = Prog(nc)
    cfg = dict(NTW=NTW, own_from=OWN, halo=True, n_heads=8, n_lru_groups=2, ycol=lambda i: 128 + (i - OWN) * 512, yrow_a=0, yrow_b=1024)
    with ExitStack() as stk:
        record_A(nc, P, stk, cfg, xT, w_sb, w_lru, conv_w, vecs, wga, wgx, cmask, cmats, kmask, ctxf, yint)
        P.barrier()
        P.emit()
    P2 = Prog(nc)
    outs = record_B(nc, P2, HALF // 128, yint, x_in, w_o0, w_i1, w_o1, lnp, sinks, abias_d, abias0_d, ident_d, out)
    fin = P2.op("sp", lambda eng: None)
    fin.deps = list(outs)
    P2.emit()
    return nc


R_OLD = 1
R_YOUNG = 1
ALPHA = float(4 ** 0.25)
EPS = 1e-5


def record_B(nc, P, TB, yT_in, x_in, w_o0, w_i1, w_o1, lnp, sinks, abias_d, abias0_d, ident_d, out, ykey=None):
    NBK = TB + 1
    sb = lambda name, shape, dt=F32: nc.alloc_sbuf_tensor(name, shape, dt).ap()
    banks = [nc.alloc_psum_tensor("bankB%d" % i, [128, 512], F32).ap() for i in range(8)]
    B = lambda i: ("bankB", i)

    Wo0 = sb("Wo0", [128, 16, 1024], BF16)
    Wi1 = sb("Wi1", [128, 8, 2688], BF16)
    Wo1 = sb("Wo1", [128, 8, 1024], BF16)
    for h in range(2):
        P.dma("pool", Wo0[:, 8 * h:8 * h + 8, :], w_o0[1024 * h:1024 * (h + 1)].rearrange("(c p) n -> p c n", p=128), writes=["Wo0"], chan=("Wo0", h))
    P.dma("pool", Wi1[:, 0:4, :], w_i1[0:512].rearrange("(c p) n -> p c n", p=128), writes=["Wi1"], chan=("Wi1", 0))
    P.dma("pool", Wi1[:, 4:8, :], w_i1[512:1024].rearrange("(c p) n -> p c n", p=128), writes=["Wi1"], chan=("Wi1", 1))
    P.dma("pool", Wo1, w_o1.rearrange("(c p) n -> p c n", p=128), writes=["Wo1"], chan="Wo1")
    lnb = sb("lnb", [128, 4, 1024])
    P.dma("sp", lnb, lnp.rearrange("p (k n) -> p k n", k=4), writes=["lnb"], chan="lnb")
    snk = sb("snk", [128, 16])
    P.dma("sp", snk, sinks, writes=["snk"], chan="snk")
    abT = sb("abT_s", [128, 16, 2, 128])
    abT0 = sb("abT0_s", [128, 16, 128])
    P.dma("sp", abT, abias_d.rearrange("p (i h q) -> p i h q", i=16, h=2), writes=["abT"], chan="abias")
    P.dma("sp", abT0, abias0_d.rearrange("p (i h q) -> p i h q", i=16, h=2)[:, :, 0, :], writes=["abT0"], chan="abias0")
    cvec = sb("cvec", [128, 16])
    esk = sb("esk", [128, 16])
    P.op("dve", lambda e: e.tensor_scalar_max(out=cvec, in0=snk, scalar1=0.0), reads=["snk"], writes=["cvec"])
    P.op("dve", lambda e: e.tensor_tensor(out=esk, in0=snk, in1=cvec, op=ALU.subtract), reads=["snk", "cvec"], writes=["esk"])
    P.op("act", lambda e: e.activation(out=esk, in_=esk, func=AF.Exp), reads=["esk"], writes=["esk"])
    for i16 in range(16):
        P.op("dve", lambda e, i16=i16: e.tensor_scalar(out=abT[:, i16, :, :], in0=abT[:, i16, :, :], scalar1=cvec[:, i16:i16 + 1], scalar2=None, op0=ALU.subtract), reads=["abT", "cvec"], writes=["abT"])
        P.op("dve", lambda e, i16=i16: e.tensor_scalar(out=abT0[:, i16, :], in0=abT0[:, i16, :], scalar1=cvec[:, i16:i16 + 1], scalar2=None, op0=ALU.subtract), reads=["abT0", "cvec"], writes=["abT0"])
    ident = sb("ident_s", [128, 128], BF16)
    P.dma("pool", ident, ident_d, writes=["identB"], chan="identB")

    yin = [sb("yin%d" % i, [128, 16, 128], BF16) for i in range(2)]
    xin = [sb("xin%d" % i, [128, 1024]) for i in range(2)]
    x1 = [sb("x1_%d" % i, [128, 1024]) for i in range(2)]
    x1b = sb("x1b", [128, 1024], BF16)
    x1T = sb("x1T", [128, 8, 128], BF16)
    qT = [sb("qTB%d" % i, [128, 8, 128], BF16) for i in range(2)]
    kT = sb("kTr", [128, 4, 3, 128], BF16)
    vv = sb("vr", [128, 3, 2, 65], BF16)
    sg = [sb("sgB%d" % i, [128, 1024]) for i in range(2)]
    scb = [sb("scb%d" % i, [128, 2, 512]) for i in range(2)]
    pT = [sb("pT%d" % i, [128, 2, 512], BF16) for i in range(2)]
    st = [sb("st%d" % i, [128, 16]) for i in range(2)]
    y1 = sb("y1", [128, 1024], BF16)
    y1T = sb("y1T", [128, 8, 128], BF16)
    ob = [sb("ob%d" % i, [128, 1024]) for i in range(2)]
    stats = [sb("stats%d" % i, [128, 2, 6]) for i in range(2)]
    mv = [sb("mv%d" % i, [128, 4]) for i in range(2)]
    P.op("pool", lambda e: e.memset(vv, 1.0), writes=[("vv", 0), ("vv", 1), ("vv", 2)])
    out_dmas = []
    yT_v = yT_in.rearrange("(c p) t -> p c t", p=128)

    def layer_norm(buf, key, gi, sid):
        S_, M_ = stats[sid], mv[sid]
        sk, mk = ("stats", sid), ("mv", sid)
        for hh in range(2):
            P.op("dve", lambda e, hh=hh: e.bn_stats(out=S_[:, hh, :], in_=buf[:, hh * 512:(hh + 1) * 512]), reads=[key], writes=[sk])
        yield
        P.op("dve", lambda e: e.bn_aggr(out=M_[:, 0:2], in_=S_.rearrange("p a b -> p (a b)")), reads=[sk], writes=[mk])
        yield
        P.op("dve", lambda e: e.tensor_scalar_add(out=M_[:, 2:3], in0=M_[:, 1:2], scalar1=EPS), reads=[mk], writes=[mk])
        yield
        P.op("act", lambda e: e.activation(out=M_[:, 2:3], in_=M_[:, 2:3], func=AF.Ln), reads=[mk], writes=[mk])
        yield
        P.op("act", lambda e: e.activation(out=M_[:, 2:3], in_=M_[:, 2:3], func=AF.Exp, scale=-0.5), reads=[mk], writes=[mk])
        yield
        P.op("dve", lambda e: e.scalar_tensor_tensor(out=M_[:, 3:4], in0=M_[:, 0:1], scalar=-1.0, in1=M_[:, 2:3], op0=ALU.mult, op1=ALU.mult), reads=[mk], writes=[mk])
        yield
        P.op("act", lambda e: e.activation(out=buf, in_=buf, func=AF.Identity, scale=M_[:, 2:3], bias=M_[:, 3:4]), reads=[key, mk], writes=[key])
        yield
        P.op("pool", lambda e: e.tensor_tensor(out=buf, in0=buf, in1=lnb[:, gi, :], op=ALU.mult), reads=[key, "lnb"], writes=[key])
        yield
        P.op("pool", lambda e: e.tensor_tensor(out=buf, in0=buf, in1=lnb[:, gi + 1, :], op=ALU.add), reads=[key, "lnb"], writes=[key])
        yield

    def load_blk(j):
        ys = j % 2
        extra = list(ykey(j)) if ykey is not None else []
        for q4 in range(4):
            P.dma("sp", yin[ys][:, 4 * q4:4 * q4 + 4, :], yT_v[:, 4 * q4:4 * q4 + 4, j * 128:(j + 1) * 128], reads=extra, writes=[("yin", ys)], chan=("yin", ys, q4))
        P.dma("sp", xin[ys], x_in[j * 128:(j + 1) * 128, :], writes=[("xin", ys)], chan=("xin", ys))

    def blk(j):
        ys = j % 2
        xs = j % 2
        X1 = x1[xs]
        x1k = ("x1", xs)
        if j == 0:
            load_blk(0)
        if j + 1 < NBK:
            load_blk(j + 1)
        for hh in range(2):
            for kc in range(16):
                P.op("pe", lambda e, hh=hh, kc=kc: e.matmul(banks[hh], lhsT=yin[ys][:, kc, :], rhs=Wo0[:, kc, hh * 512:(hh + 1) * 512], start=(kc == 0), stop=(kc == 15)),
                     reads=[("yin", ys), "Wo0"], writes=[B(hh)])
                if kc % 4 == 3:
                    yield
        for hh in range(2):
            P.op("dve", lambda e, hh=hh: e.scalar_tensor_tensor(out=X1[:, hh * 512:(hh + 1) * 512], in0=xin[ys][:, hh * 512:(hh + 1) * 512], scalar=ALPHA, in1=banks[hh], op0=ALU.mult, op1=ALU.add),
                 reads=[("xin", ys), B(hh)], writes=[x1k])
            yield
        yield from layer_norm(X1, x1k, 0, 0)
        P.op("pool", lambda e: e.tensor_copy(out=x1b, in_=X1), reads=[x1k], writes=["x1b"])
        yield
        tb = banks[2].bitcast(BF16)
        for c in range(8):
            P.op("pe", lambda e, c=c: e.transpose(tb[:, c * 128:(c + 1) * 128], x1b[:, c * 128:(c + 1) * 128], ident), reads=["x1b", "identB"], writes=[B(2)])
            if c % 4 == 3:
                yield
        P.op("act", lambda e: e.activation(out=x1T.rearrange("p c t -> p (c t)"), in_=tb, func=AF.Copy), reads=[B(2)], writes=["x1T"])
        yield
        ks = j % 3
        for c2 in range(4):
            for kc in range(8):
                P.op("pe", lambda e, c2=c2, kc=kc: e.matmul(banks[3][:, c2 * 128:(c2 + 1) * 128], lhsT=Wi1[:, kc, 1024 + c2 * 128:1024 + (c2 + 1) * 128], rhs=x1T[:, kc, :], start=(kc == 0), stop=(kc == 7)),
                     reads=["x1T", "Wi1"], writes=[B(3)])
            yield
        for kc in range(8):
            P.op("pe", lambda e, kc=kc: e.matmul(banks[2][:, 0:128], lhsT=x1T[:, kc, :], rhs=Wi1[:, kc, 1536:1664], start=(kc == 0), stop=(kc == 7)),
                 reads=["x1T", "Wi1"], writes=[B(2)])
        yield
        P.op("dve", lambda e: e.tensor_copy(out=kT[:, :, ks, :], in_=banks[3].rearrange("p (c t) -> p c t", c=4)), reads=[B(3)], writes=[("kT", ks)])
        yield
        P.op("dve", lambda e: e.tensor_copy(out=vv[:, ks, :, 0:64], in_=banks[2][:, 0:128].rearrange("p (c d) -> p c d", c=2)), reads=[B(2)], writes=[("vv", ks)])
        yield
        if j == 0:
            yield "S2"
            return
        qs = j % 2
        Q = qT[qs]
        qk = ("qTB", qs)
        SG = sg[qs]
        sgk = ("sgB", qs)
        for c in range(8):
            bq = c // 4
            for kc in range(8):
                P.op("pe", lambda e, c=c, kc=kc, bq=bq: e.matmul(banks[bq][:, (c % 4) * 128:(c % 4 + 1) * 128], lhsT=Wi1[:, kc, c * 128:(c + 1) * 128], rhs=x1T[:, kc, :], start=(kc == 0), stop=(kc == 7)),
                     reads=["x1T", "Wi1"], writes=[B(bq)])
            yield
        for hh in range(2):
            P.op("dve", lambda e, hh=hh: e.tensor_scalar(out=Q[:, 4 * hh:4 * hh + 4, :], in0=banks[hh].rearrange("p (c t) -> p c t", c=4), scalar1=0.125, scalar2=None, op0=ALU.mult),
                 reads=[B(hh)], writes=[qk])
            yield
        for hh in range(2):
            for kc in range(8):
                P.op("pe", lambda e, hh=hh, kc=kc: e.matmul(banks[hh], lhsT=x1T[:, kc, :], rhs=Wi1[:, kc, 1664 + hh * 512:1664 + (hh + 1) * 512], start=(kc == 0), stop=(kc == 7)),
                     reads=["x1T", "Wi1"], writes=[B(hh)])
                if kc % 4 == 3:
                    yield
        for hh in range(2):
            P.op("act", lambda e, hh=hh: e.activation(out=SG[:, hh * 512:(hh + 1) * 512], in_=banks[hh], func=AF.Exp, scale=-1.0), reads=[B(hh)], writes=[sgk])
            yield
        P.op("dve", lambda e: e.tensor_scalar_add(out=SG, in0=SG, scalar1=1.0), reads=[sgk], writes=[sgk])
        yield
        P.op("dve", lambda e: e.reciprocal(out=SG, in_=SG), reads=[sgk], writes=[sgk])
        yield
        for hh in range(2):
            P.op("dve", lambda e, hh=hh: e.tensor_tensor(out=SG[:, hh * 512:(hh + 1) * 512], in0=SG[:, hh * 512:(hh + 1) * 512], in1=banks[hh], op=ALU.mult), reads=[sgk, B(hh)], writes=[sgk])
            yield
        yield "S2"
        ps = (j - 1) % 3
        tb6 = banks[6].bitcast(BF16)

        def grp_gen(gi):
            c, par = gi // 2, gi % 2
            sl = gi % 2
            S_ = scb[sl]
            PT = pT[sl]
            T_ = st[sl]
            ob_ = 6 + sl
            for half, slot in ((0, ps), (1, ks)):
                P.op("pe", lambda e, half=half, slot=slot: e.matmul(banks[4 + half], lhsT=kT[:, 2 * c + par, slot, :], rhs=Q[:, 4 * c:4 * c + 4, :].rearrange("p a q -> p (a q)"), start=True, stop=True),
                     reads=[qk, ("kT", slot)], writes=[B(4 + half)])
            yield
            for half in range(2):
                if half == 0 and j == 1:
                    bsrc, bkey = abT0[:, 4 * gi:4 * gi + 4, :], "abT0"
                else:
                    bsrc, bkey = abT[:, 4 * gi:4 * gi + 4, half, :], "abT"
                P.op("dve", lambda e, half=half, bsrc=bsrc: e.tensor_tensor(out=S_[:, half, :].rearrange("p (a q) -> p a q", a=4), in0=banks[4 + half].rearrange("p (a q) -> p a q", a=4), in1=bsrc, op=ALU.add),
                     reads=[B(4 + half), bkey], writes=[("scb", sl)])
                yield
            for half in range(2):
                P.op("act", lambda e, half=half: e.activation(out=PT[:, half, :], in_=S_[:, half, :], func=AF.Exp), reads=[("scb", sl)], writes=[("pT", sl)])
                yield
            for jj in range(4):
                for half, slot in ((0, ps), (1, ks)):
                    P.op("pe", lambda e, jj=jj, half=half, slot=slot: e.matmul(banks[ob_][:, jj * 65:(jj + 1) * 65], lhsT=PT[:, half, jj * 128:(jj + 1) * 128],
                                                                            rhs=vv[:, slot, c, :], start=(half == 0), stop=(half == 1)),
                         reads=[("pT", sl), ("vv", slot)], writes=[B(ob_)])
            yield
            ov = banks[ob_][:, 0:260].rearrange("p (a d) -> p a d", a=4)
            P.op("dve", lambda e: e.tensor_tensor(out=T_[:, 0:4], in0=ov[:, :, 64], in1=esk[:, 4 * gi:4 * gi + 4], op=ALU.add), reads=[B(ob_), "esk"], writes=[("st", sl)])
            yield
            P.op("dve", lambda e: e.reciprocal(out=T_[:, 4:8], in_=T_[:, 0:4]), reads=[("st", sl)], writes=[("st", sl)])
            yield
            for jj in range(4):
                h = 8 * c + 2 * jj + par
                P.op("dve", lambda e, jj=jj, h=h: e.scalar_tensor_tensor(out=y1[:, h * 64:(h + 1) * 64], in0=ov[:, jj, 0:64], scalar=T_[:, 4 + jj:5 + jj],
                                                                     in1=SG[:, h * 64:(h + 1) * 64], op0=ALU.mult, op1=ALU.mult),
                     reads=[B(ob_), ("st", sl), sgk], writes=["y1"])
                yield

        for gi in range(4):
            yield from grp_gen(gi)
        for c in range(8):
            P.op("pe", lambda e, c=c: e.transpose(tb6[:, c * 128:(c + 1) * 128], y1[:, c * 128:(c + 1) * 128], ident), reads=["y1", "identB"], writes=[B(6)])
            if c % 4 == 3:
                yield
        P.op("act", lambda e: e.activation(out=y1T.rearrange("p c t -> p (c t)"), in_=tb6, func=AF.Copy), reads=[B(6)], writes=["y1T"])
        yield
        for hh in range(2):
            for kc in range(8):
                P.op("pe", lambda e, hh=hh, kc=kc: e.matmul(banks[4 + hh], lhsT=y1T[:, kc, :], rhs=Wo1[:, kc, hh * 512:(hh + 1) * 512], start=(kc == 0), stop=(kc == 7)),
                     reads=["y1T", "Wo1"], writes=[B(4 + hh)])
                if kc % 4 == 3:
                    yield
        os_ = j % 2
        OB = ob[os_]
        obk = ("ob", os_)
        for hh in range(2):
            P.op("dve", lambda e, hh=hh: e.scalar_tensor_tensor(out=OB[:, hh * 512:(hh + 1) * 512], in0=X1[:, hh * 512:(hh + 1) * 512], scalar=ALPHA, in1=banks[4 + hh], op0=ALU.mult, op1=ALU.add),
                 reads=[x1k, B(4 + hh)], writes=[obk])
            yield
        yield from layer_norm(OB, obk, 2, 1)
        out_dmas.append(P.dma("sp", out[(j - 1) * 128:j * 128, :], OB, reads=[obk], chan=("ob", os_)))
        yield

    gens = [blk(j) for j in range(NBK)]
    older = None
    younger = None
    paused = False
    nxt = 0
    DONE = object()
    while True:
        if younger is None and nxt < NBK:
            younger = gens[nxt]
            nxt += 1
            paused = False
        if older is None and younger is None:
            break
        if older is None and paused:
            older, younger, paused = younger, None, False
            continue
        for _ in range(R_OLD):
            if older is not None:
                if next(older, DONE) is DONE:
                    older = None
        for _ in range(R_YOUNG):
            if younger is not None and not paused:
                r = next(younger, DONE)
                if r is DONE:
                    younger = None
                elif r == "S2":
                    paused = True
    return out_dmas


import ml_dtypes
from concourse.bass_utils import run_bass_kernel_spmd

S_FULL = 8192
_CACHE = {}


def _consts_A():
    p = np.arange(128)[:, None]; f = np.arange(512)[None, :]
    cmask = np.zeros((128, 2048), np.float32)
    for r in range(4):
        cmask[:, r * 512:(r + 1) * 512] = np.where(128 * r + p < f, 0.0, -30000.0)
    j = np.arange(128)[:, None]; s = np.arange(128)[None, :]
    ident = np.eye(128, dtype=np.float32)
    trineg = np.where(j >= s, -1.0, 0.0).astype(np.float32)
    restneg = np.where(j < s, -1.0, 0.0).astype(np.float32)
    return cmask, np.concatenate([ident, trineg, restneg], axis=1)


def _consts_B():
    k = np.arange(128)[:, None]; q = np.arange(128)[None, :]
    slopes = np.array([2.0 ** (-8.0 * (i + 1) / 16) for i in range(16)], np.float32)
    ab = np.zeros((128, 16, 2, 128), np.float32)
    for idx in range(16):
        g, jj = idx // 4, idx % 4
        c, par = g // 2, g % 2
        h = 8 * c + 2 * jj + par
        for half in range(2):
            dist = (q - k + 128).astype(np.float32) if half == 0 else (q - k).astype(np.float32)
            valid = (dist >= 0) & (dist < 128)
            ab[:, idx, half, :] = np.where(valid, -slopes[h] * dist, -30000.0)
    ab0 = ab.copy(); ab0[:, :, 0, :] = -30000.0
    return ab.reshape(128, 4096), ab0.reshape(128, 4096)


def _sinks_group_order(s16):
    out = np.zeros(16, np.float32)
    for idx in range(16):
        g, jj = idx // 4, idx % 4
        c, par = g // 2, g % 2
        out[idx] = s16[8 * c + 2 * jj + par]
    return out


def _shared_inputs(inp):
    w_in = inp["e_w_in"][0]
    w_sb = np.zeros((8, 1024, 512), np.float32)
    for h in range(8):
        for k, base in enumerate([2048, 3072, 4096, 5120]):
            w_sb[h, :, k * 128:(k + 1) * 128] = w_in[:, base + h * 128: base + (h + 1) * 128]
    w_lru = np.stack([np.concatenate([w_in[:, 512 * q:512 * q + 512], w_in[:, 1024 + 512 * q:1024 + 512 * q + 512]], axis=1) for q in range(2)])
    conv_w = np.stack([np.ascontiguousarray(inp["e_conv_w"][0][:, 512 * q:512 * q + 512].T) for q in range(2)])
    vecs = np.stack([np.stack([inp["e_conv_b"][0][512 * q:512 * q + 512], inp["e_b_gate_a"][0][512 * q:512 * q + 512],
                               inp["e_b_gate_x"][0][512 * q:512 * q + 512], inp["e_lru_lambda"][0][512 * q:512 * q + 512]], axis=1) for q in range(2)])
    cmask, cmats = _consts_A()
    w = inp["o_w_in"][0]
    q = w[:, 0:1024]; k = w[:, 1024:1152]; v = w[:, 1152:1280]; gg = w[:, 1280:2304]
    z64 = np.zeros((1024, 64), np.float32)
    w_i1 = np.concatenate([q, k[:, 0:64], z64, z64, k[:, 0:64], k[:, 64:128], z64, z64, k[:, 64:128], v, gg], axis=1)
    lnp = np.concatenate([inp["e_ln_g"][0], inp["e_ln_b"][0], inp["o_ln_g"][0], inp["o_ln_b"][0]])[None, :].repeat(128, 0)
    sinks = _sinks_group_order(inp["o_sinks"][0])[None, :].repeat(128, 0)
    ab, ab0 = _consts_B()
    return {"w_sb": w_sb, "w_lru": np.ascontiguousarray(w_lru), "conv_w": np.ascontiguousarray(conv_w), "vecs": np.ascontiguousarray(vecs),
            "wga": np.ascontiguousarray(inp["e_w_gate_a"][0].reshape(2, 4, 128, 128)), "wgx": np.ascontiguousarray(inp["e_w_gate_x"][0].reshape(2, 4, 128, 128)),
            "cmask": cmask, "cmats": cmats, "w_o0": np.ascontiguousarray(inp["e_w_out"][0]), "w_i1": np.ascontiguousarray(w_i1),
            "w_o1": np.ascontiguousarray(inp["o_w_out"][0]), "lnp": np.ascontiguousarray(lnp), "sinks": np.ascontiguousarray(sinks),
            "abias": ab, "ident": np.eye(128, dtype=np.float32)}, ab0


def _core_inputs(inp, shared, ab0, b, g, S):
    HALF = S // 2
    x = inp["x"][b, :S]
    xw = np.zeros((S, 1024), np.float32)
    if g == 0:
        xw[HALF:] = x[:HALF]
    else:
        xw[:] = x
    kmask = np.zeros((128, S // 512), np.float32)
    if g == 0:
        kmask[:, :S // 1024] = -30000.0
    x_in = np.zeros((HALF + 128, 1024), np.float32)
    t0 = g * HALF - 128
    lo = max(t0, 0)
    x_in[lo - t0:] = x[lo:t0 + HALF + 128]
    d = dict(shared)
    d.update({"xT": np.ascontiguousarray(xw.T), "kmask": kmask, "ctxf": np.full((128, 1), float(g), np.float32),
              "x_in": x_in, "abias0": ab0 if g == 0 else shared["abias"]})
    return d


def kernel(**inputs):
    inp = {k: np.asarray(v, dtype=np.float32) for k, v in inputs.items()}
    S = S_FULL
    cores = [(b, g) for b in range(4) for g in range(2)]
    if "F" not in _CACHE:
        _CACHE["F"] = build_F(S)
    shared, ab0 = _shared_inputs(inp)
    res = run_bass_kernel_spmd(_CACHE["F"], [_core_inputs(inp, shared, ab0, b, g, S) for b, g in cores], core_ids=list(range(8)))
    out = np.zeros((4, S, 1024), np.float32)
    for ci, (b, g) in enumerate(cores):
        out[b, g * (S // 2):(g + 1) * (S // 2)] = res.results[ci]["out"]
    return out
```
